# Optimizing a Trainium2 kernel written in Bass

```python
import math
import jax, jax.numpy as jnp
from jax import lax
import numpy as np

D_MODEL = 1024
BATCH = 32
SEQ = 2048
DEPTH = 2

GRID_W = 64
CTX_LEN = 256
BRANCH_WIDTH = D_MODEL // 2
MIX_WIDTH = 2 * BRANCH_WIDTH
LRU_HEAD_DIM = 64
LRU_HEADS = BRANCH_WIDTH // LRU_HEAD_DIM
LRU_CONV = 4
LRU_C = 8.0
SGU_CHUNK = 128
SGU_GROUPS = 4
SGU_GROUP_DIM = BRANCH_WIDTH // SGU_GROUPS
SCONV_K = 3
DIFF_V_DIM = 128
DIFF_HEADS = BRANCH_WIDTH // DIFF_V_DIM
DIFF_HEAD_DIM = DIFF_V_DIM // 2
Q_BLOCK = 128
ROPE_BASE = 10000.0
AB_IN = 5 * BRANCH_WIDTH
CD_IN = 8 * BRANCH_WIDTH
N_AB = (DEPTH + 1) // 2
N_CD = DEPTH // 2
DEEPNORM_ALPHA = (2 * DEPTH) ** 0.25
DEEPNORM_BETA = (8 * DEPTH) ** -0.25
LN_EPS = 1e-5

kernel_name = "hybrid_lru_sgu_conv_diffattn_trunk"


def layer_norm(x, g, b):
    xf = x.astype(jnp.float32)
    mu = jnp.mean(xf, axis=-1, keepdims=True)
    var = jnp.mean(jnp.square(xf - mu), axis=-1, keepdims=True)
    return ((xf - mu) * lax.rsqrt(var + LN_EPS)).astype(x.dtype) * g + b


def rms_norm(x, g):
    xf = x.astype(jnp.float32)
    return (xf * lax.rsqrt(jnp.mean(jnp.square(xf), axis=-1, keepdims=True) + LN_EPS)).astype(x.dtype) * g


def modulation(cond, w, b):
    m = jax.nn.silu(cond) @ w + b
    shift, scale, gate = jnp.split(m[:, None, :], 3, axis=-1)
    return shift, scale, gate


def depthwise_conv(x, w, b=None):
    k = w.shape[0]
    y = lax.conv_general_dilated(x, w[:, None, :].astype(x.dtype), (1,), [(k // 2, k - 1 - k // 2)],
                                 dimension_numbers=('NWC', 'WIO', 'NWC'), feature_group_count=x.shape[-1])
    if b is not None:
        y = y + b
    return y


def rglru_coeffs(xc, w_a, b_a, w_x, b_x, lam):
    bsz, n, _ = xc.shape
    xh = xc.reshape(bsz, n, LRU_HEADS, LRU_HEAD_DIM)
    gate_r = jax.nn.sigmoid(jnp.einsum('blhi,hij->blhj', xh, w_a).reshape(bsz, n, BRANCH_WIDTH) + b_a)
    gate_i = jax.nn.sigmoid(jnp.einsum('blhi,hij->blhj', xh, w_x).reshape(bsz, n, BRANCH_WIDTH) + b_x)
    log_a = (-LRU_C * gate_r.astype(jnp.float32)) * jax.nn.softplus(-lam.astype(jnp.float32))
    a = jnp.exp(log_a)
    bterm = jnp.sqrt(-jnp.expm1(2.0 * log_a)) * (gate_i * xc).astype(jnp.float32)
    return a, bterm


def _scan_combine(left, right):
    a_l, h_l = left
    a_r, h_r = right
    return a_l * a_r, a_r * h_l + h_r


def linear_scan(a, b, reverse, h0=None):
    if h0 is not None:
        edge = -1 if reverse else 0
        b = b.at[:, edge].add(a[:, edge] * h0)
    return lax.associative_scan(_scan_combine, (a, b), reverse=reverse, axis=1)[1]


def spatial_gating(u, v, ln_g, ln_b, w_s, b_s):
    bsz, n, _ = v.shape
    vh = layer_norm(v, ln_g, ln_b).reshape(bsz, n // SGU_CHUNK, SGU_CHUNK, SGU_GROUPS, SGU_GROUP_DIM)
    s = jnp.einsum('gpq,bnqgc->bnpgc', w_s, vh) + b_s.T[:, :, None]
    return u * s.reshape(bsz, n, BRANCH_WIDTH)


def axial_rope(n_tokens):
    rows = n_tokens // GRID_W
    row = jnp.repeat(jnp.arange(rows, dtype=jnp.float32), GRID_W)
    col = jnp.tile(jnp.arange(GRID_W, dtype=jnp.float32), rows)
    n_freq = DIFF_HEAD_DIM // 4
    inv = ROPE_BASE ** (-jnp.arange(n_freq, dtype=jnp.float32) / n_freq)
    ang_r = row[:, None] * inv
    ang_c = col[:, None] * inv
    ang = jnp.concatenate([ang_r, ang_r, ang_c, ang_c], axis=-1)
    return jnp.cos(ang), jnp.sin(ang)


def apply_rope(x, cos, sin):
    n_freq = DIFF_HEAD_DIM // 4
    xs = x.reshape(x.shape[:-1] + (2, 2, n_freq))
    rot = jnp.stack([-xs[..., 1, :], xs[..., 0, :]], axis=-2).reshape(x.shape)
    cos = cos[:, None, None, :].astype(x.dtype)
    sin = sin[:, None, None, :].astype(x.dtype)
    return x * cos + rot * sin


def diff_attention(q, k, v, lam):
    s = jnp.einsum('bqhmd,bkhmd->bhmqk', q, k).astype(jnp.float32) * (DIFF_HEAD_DIM ** -0.5)
    p = jax.nn.softmax(s, axis=-1)
    a = p[:, :, 0] - lam * p[:, :, 1]
    return jnp.einsum('bhqk,bkhe->bqhe', a.astype(v.dtype), v)


def blocked_diff_attention(q, k, v, lam):
    bsz, n, nh, two, dh = q.shape
    qb = jnp.moveaxis(q.reshape(bsz, n // Q_BLOCK, Q_BLOCK, nh, two, dh), 1, 0)
    ob = lax.map(lambda qq: diff_attention(qq, k, v, lam), qb)
    return jnp.moveaxis(ob, 0, 1).reshape(bsz, n, nh, DIFF_V_DIM)


def ab_mixer(h, hc, w_in, w_out, conv_w, conv_b, w_a, b_a, w_x, b_x, lam,
             sgu_ln_g, sgu_ln_b, sgu_w, sgu_b, need_ctx_out):
    W = BRANCH_WIDTH
    p = h @ w_in
    pc = hc @ (w_in if need_ctx_out else w_in[:, :W])
    xl = depthwise_conv(p[..., :W], conv_w, conv_b)
    xc = depthwise_conv(pc[..., :W], conv_w, conv_b)
    h_lat = 0.0
    h_ctx_dirs = []
    for d, reverse in enumerate((False, True)):
        ac, bc = rglru_coeffs(xc, w_a[d], b_a[d], w_x[d], b_x[d], lam[d])
        hcd = linear_scan(ac, bc, reverse)
        h_end = hcd[:, 0] if reverse else hcd[:, -1]
        al, bl = rglru_coeffs(xl, w_a[d], b_a[d], w_x[d], b_x[d], lam[d])
        h_lat = h_lat + linear_scan(al, bl, reverse, h_end)
        h_ctx_dirs.append(hcd)

    def branches(pp, h_rec):
        y_a = h_rec.astype(pp.dtype) * jax.nn.silu(pp[..., W:2 * W])
        u = jax.nn.gelu(pp[..., 2 * W:3 * W])
        v = jax.nn.gelu(pp[..., 3 * W:4 * W])
        y_b = spatial_gating(u, v, sgu_ln_g, sgu_ln_b, sgu_w, sgu_b) * jax.nn.silu(pp[..., 4 * W:5 * W])
        return jnp.concatenate([y_a, y_b], axis=-1) @ w_out

    y = branches(p, h_lat)
    yc = branches(pc, h_ctx_dirs[0] + h_ctx_dirs[1]) if need_ctx_out else None
    return y, yc


def cd_mixer(h, hc, w_in, w_out, sconv_w, lam_vec, subln_g, lam_init, need_ctx_out):
    W = BRANCH_WIDTH
    bsz, n, _ = h.shape
    p = h @ w_in
    pc = hc @ (w_in if need_ctx_out else w_in[:, 5 * W:7 * W])
    kvc = pc[..., 5 * W:7 * W] if need_ctx_out else pc
    lam_f = lam_vec.astype(jnp.float32)
    lam = (jnp.exp(jnp.sum(lam_f[0] * lam_f[1])) - jnp.exp(jnp.sum(lam_f[2] * lam_f[3])) + lam_init)

    def heads_qk(t):
        return t.reshape(t.shape[0], t.shape[1], DIFF_HEADS, 2, DIFF_HEAD_DIM)

    def heads_v(t):
        return t.reshape(t.shape[0], t.shape[1], DIFF_HEADS, DIFF_V_DIM)

    cos, sin = axial_rope(n)
    q = apply_rope(heads_qk(p[..., 4 * W:5 * W]), cos, sin)
    k = apply_rope(heads_qk(p[..., 5 * W:6 * W]), cos, sin)
    v = heads_v(p[..., 6 * W:7 * W])
    kc = heads_qk(kvc[..., :W])
    vc = heads_v(kvc[..., W:])
    k_all = jnp.concatenate([kc, k], axis=1)
    v_all = jnp.concatenate([vc, v], axis=1)
    o = blocked_diff_attention(q, k_all, v_all, lam)

    def finish(pp, oo):
        y_c = pp[..., W:2 * W] * depthwise_conv(pp[..., 2 * W:3 * W] * pp[..., :W], sconv_w)
        y_c = y_c * jax.nn.silu(pp[..., 3 * W:4 * W])
        oo = rms_norm(oo, subln_g) * (1.0 - lam_init)
        y_d = oo.reshape(oo.shape[0], oo.shape[1], W) * jax.nn.silu(pp[..., 7 * W:8 * W])
        return jnp.concatenate([y_c, y_d], axis=-1) @ w_out

    y = finish(p, o)
    yc = None
    if need_ctx_out:
        oc = diff_attention(heads_qk(pc[..., 4 * W:5 * W]), kc, vc, lam)
        yc = finish(pc, oc)
    return y, yc


def setup_inputs(seed: int = 0) -> dict:
    key = jax.random.key(seed)
    ks = jax.random.split(key, 26)
    f32 = jnp.float32
    W = BRANCH_WIDTH

    def nrm(k, shape, s):
        return jax.random.normal(k, shape, f32) * s

    u = jax.random.uniform(ks[16], (N_AB, 2, W), f32, 0.9, 0.999)
    s = u ** (1.0 / LRU_C)
    return {
        "x": nrm(ks[0], (BATCH, SEQ, D_MODEL), 1.0),
        "c": nrm(ks[1], (BATCH, D_MODEL), 1.0),
        "ctx": nrm(ks[2], (BATCH, CTX_LEN, D_MODEL), 1.0),
        "c_ctx": nrm(ks[3], (D_MODEL,), 1.0),
        "w_mod": nrm(ks[4], (DEPTH, D_MODEL, 3 * D_MODEL), D_MODEL ** -0.5),
        "b_mod": nrm(ks[5], (DEPTH, 3 * D_MODEL), 0.02),
        "ln_g": 1.0 + nrm(ks[6], (DEPTH, D_MODEL), 0.02),
        "ln_b": nrm(ks[7], (DEPTH, D_MODEL), 0.02),
        "ab_w_in": nrm(ks[8], (N_AB, D_MODEL, AB_IN), D_MODEL ** -0.5),
        "ab_w_out": nrm(ks[9], (N_AB, MIX_WIDTH, D_MODEL), DEEPNORM_BETA * MIX_WIDTH ** -0.5),
        "lru_conv_w": nrm(ks[10], (N_AB, LRU_CONV, W), LRU_CONV ** -0.5),
        "lru_conv_b": nrm(ks[11], (N_AB, W), 0.02),
        "lru_w_a": nrm(ks[12], (N_AB, 2, LRU_HEADS, LRU_HEAD_DIM, LRU_HEAD_DIM), LRU_HEAD_DIM ** -0.5),
        "lru_b_a": nrm(ks[13], (N_AB, 2, W), 0.02),
        "lru_w_x": nrm(ks[14], (N_AB, 2, LRU_HEADS, LRU_HEAD_DIM, LRU_HEAD_DIM), LRU_HEAD_DIM ** -0.5),
        "lru_b_x": nrm(ks[15], (N_AB, 2, W), 0.02),
        "lru_lambda": jnp.log(s) - jnp.log1p(-s),
        "sgu_ln_g": 1.0 + nrm(ks[17], (N_AB, W), 0.02),
        "sgu_ln_b": nrm(ks[18], (N_AB, W), 0.02),
        "sgu_w": nrm(ks[19], (N_AB, SGU_GROUPS, SGU_CHUNK, SGU_CHUNK), SGU_CHUNK ** -0.5),
        "sgu_b": 1.0 + nrm(ks[20], (N_AB, SGU_GROUPS, SGU_CHUNK), 0.02),
        "cd_w_in": nrm(ks[21], (N_CD, D_MODEL, CD_IN), D_MODEL ** -0.5),
        "cd_w_out": nrm(ks[22], (N_CD, MIX_WIDTH, D_MODEL), DEEPNORM_BETA * MIX_WIDTH ** -0.5),
        "sconv_w": nrm(ks[23], (N_CD, SCONV_K, W), SCONV_K ** -0.5),
        "diff_lambda": nrm(ks[24], (N_CD, 4, DIFF_HEAD_DIM), 0.1),
        "diff_subln_g": 1.0 + nrm(ks[25], (N_CD, DIFF_V_DIM), 0.02),
    }


def reference(x, c, ctx, c_ctx, w_mod, b_mod, ln_g, ln_b, ab_w_in, ab_w_out, lru_conv_w, lru_conv_b,
              lru_w_a, lru_b_a, lru_w_x, lru_b_x, lru_lambda, sgu_ln_g, sgu_ln_b, sgu_w, sgu_b,
              cd_w_in, cd_w_out, sconv_w, diff_lambda, diff_subln_g):
    for l in range(DEPTH):
        need_ctx_out = l < DEPTH - 1
        shift, scale, gate = modulation(c, w_mod[l], b_mod[l])
        shift_c, scale_c, gate_c = modulation(c_ctx[None], w_mod[l], b_mod[l])
        h = x * (1.0 + scale) + shift
        hc = ctx * (1.0 + scale_c) + shift_c
        i = l // 2
        if l % 2 == 0:
            y, yc = ab_mixer(h, hc, ab_w_in[i], ab_w_out[i], lru_conv_w[i], lru_conv_b[i],
                             lru_w_a[i], lru_b_a[i], lru_w_x[i], lru_b_x[i], lru_lambda[i],
                             sgu_ln_g[i], sgu_ln_b[i], sgu_w[i], sgu_b[i], need_ctx_out)
        else:
            lam_init = 0.8 - 0.6 * math.exp(-0.3 * l)
            y, yc = cd_mixer(h, hc, cd_w_in[i], cd_w_out[i], sconv_w[i], diff_lambda[i],
                             diff_subln_g[i], lam_init, need_ctx_out)
        x = layer_norm(DEEPNORM_ALPHA * x + gate * y, ln_g[l], ln_b[l])
        if need_ctx_out:
            ctx = layer_norm(DEEPNORM_ALPHA * ctx + gate_c * yc, ln_g[l], ln_b[l])
    return x
```

```python
import math

import numpy as np
import concourse.bass as bass
import concourse.mybir as mybir
from concourse.bass_utils import run_bass_kernel_spmd

F32 = mybir.dt.float32
BF16 = mybir.dt.bfloat16
AF = mybir.ActivationFunctionType
ALU = mybir.AluOpType
AX = mybir.AxisListType

ENG_ATTR = {"pe": "tensor", "act": "scalar", "dve": "vector", "pool": "gpsimd", "sp": "sync"}
COMPUTE = ("pe", "act", "dve", "pool")
_ESZ = {F32: 4, BF16: 2, mybir.dt.int32: 4, mybir.dt.uint32: 4, mybir.dt.float16: 2}


class Prog:
    SEM_ROT = 30000

    def __init__(self):
        self.nc = bass.Bass("TRN2", target_bir_lowering=False)
        self.ops = []
        self.acc = {}
        self.base = {}
        self.sb_off = 16512
        self.sb_top = 229344
        self.n_psum = 0
        self.dma_rr = {"sp": 0, "pool": 0, "act": 0}
        self.out_dmas = []

    def sbuf(self, name, shape, dtype, at=None):
        esz = _ESZ[dtype]
        nbytes = int(np.prod(shape[1:])) * esz
        if at is None:
            at = (self.sb_off + 63) // 64 * 64
            self.sb_off = at + nbytes
            assert self.sb_off <= self.sb_top, (name, self.sb_off)
        t = self.nc.alloc_sbuf_tensor_at(name, list(shape), dtype, offset=at)
        self.base[t.name] = ("SB", at)
        return t

    def psum(self, name, shape, dtype=F32):
        t = self.nc.alloc_psum_tensor(name, list(shape), dtype)
        self.base[t.name] = ("PS_" + name, 0)
        return t

    def bitcast(self, t, dtype):
        v = t.bitcast(dtype)
        self.base[v.name] = self.base[t.name]
        return v

    def dram_in(self, name, shape, dtype=F32):
        return self.nc.dram_tensor(name, list(shape), dtype, kind="ExternalInput")

    def dram_out(self, name, shape, dtype=F32):
        return self.nc.dram_tensor(name, list(shape), dtype, kind="ExternalOutput")

    def dram_tmp(self, name, shape, dtype=F32):
        return self.nc.dram_tensor(name, list(shape), dtype)

    def region(self, ap):
        t = ap.tensor
        key = self.base.get(t.name)
        if key is None:
            return None
        space, b0 = key
        if space.startswith("PS_"):
            return (space, 0, 128, 0, 2048)
        esz = _ESZ[ap.dtype]
        dims = ap.ap
        pstep, pcnt = dims[0]
        off = ap.offset
        if pstep > 0:
            p0 = off // pstep
            f0 = off % pstep
        else:
            p0, f0 = 0, off
        lo = hi = f0
        for step, cnt in dims[1:]:
            if step >= 0:
                hi += step * (cnt - 1)
            else:
                lo += step * (cnt - 1)
        hi += 1
        return (space, p0, p0 + pcnt, b0 + lo * esz, b0 + hi * esz)

    def add(self, eng, emit, reads=(), writes=(), is_dma=False, is_out=False):
        idx = len(self.ops)
        tag = ("dma", idx) if is_dma else eng
        deps = set()
        rr = [self.region(a) for a in reads if a is not None and not isinstance(a, (int, float))]
        wr = [self.region(a) for a in writes]
        rr = [r for r in rr if r is not None]
        wr = [r for r in wr if r is not None]
        wr = wr + [r for r in rr if r[0].startswith("PS_") and r not in wr]
        for r in rr:
            for e in self.acc.get(r[0], ()):
                if e[6] and e[0] < r[2] and r[1] < e[1] and e[2] < r[4] and r[3] < e[3]:
                    deps.add((e[5], "raw"))
        for w in wr:
            for e in self.acc.get(w[0], ()):
                if e[0] < w[2] and w[1] < e[1] and e[2] < w[4] and w[3] < e[3]:
                    deps.add((e[5], "waw" if e[6] else "war"))
        fdeps = set()
        for d, kind in deps:
            po = self.ops[d]
            if po["is_dma"] or is_dma:
                fdeps.add(d)
            elif po["eng"] == eng:
                if eng != "pe":
                    fdeps.add(d)
            else:
                fdeps.add(d)
        op = dict(eng=eng, emit=emit, deps=fdeps, is_dma=is_dma, is_out=is_out, signal=False)
        self.ops.append(op)
        for d in fdeps:
            self.ops[d]["signal"] = True
        for w in wr:
            lst = self.acc.setdefault(w[0], [])
            lst[:] = [e for e in lst if not (w[1] <= e[0] and e[1] <= w[2] and w[3] <= e[2] and e[3] <= w[4])]
            lst.append([w[1], w[2], w[3], w[4], tag, idx, True])
        for r in rr:
            lst = self.acc.setdefault(r[0], [])
            found = False
            for e in lst:
                if (not e[6]) and e[4] == tag and e[0] == r[1] and e[1] == r[2] and e[2] == r[3] and e[3] == r[4]:
                    e[5] = idx
                    found = True
                    break
            if not found:
                lst.append([r[1], r[2], r[3], r[4], tag, idx, False])
        if is_dma:
            op["signal"] = True
        return idx

    def mm(self, out, lhsT, rhs, start=True, stop=True, **kw):
        return self.add("pe", lambda e: e.matmul(out, lhsT, rhs, start=start, stop=stop, **kw),
                        reads=[lhsT, rhs], writes=[out])

    def tr(self, out, in_, ident):
        return self.add("pe", lambda e: e.transpose(out, in_, ident), reads=[in_, ident], writes=[out])

    def act(self, out, in_, func, bias=0.0, scale=1.0, eng="act"):
        rd = [in_]
        if not isinstance(bias, (int, float)):
            rd.append(bias)
        if not isinstance(scale, (int, float)):
            rd.append(scale)
        return self.add(eng, lambda e: e.activation(out, in_, func, bias=bias, scale=scale),
                        reads=rd, writes=[out])

    def tt(self, out, a, b, op, eng="dve"):
        return self.add(eng, lambda e: e.tensor_tensor(out, a, b, op), reads=[a, b], writes=[out])

    def ts(self, out, a, s1, s2, op0, op1=None, eng="dve"):
        rd = [a] + [s for s in (s1, s2) if s is not None and not isinstance(s, (int, float))]
        if op1 is None:
            return self.add(eng, lambda e: e.tensor_scalar(out, a, s1, None, op0), reads=rd, writes=[out])
        return self.add(eng, lambda e: e.tensor_scalar(out, a, s1, s2, op0, op1), reads=rd, writes=[out])

    def stt(self, out, in0, scalar, in1, op0, op1, eng="dve"):
        rd = [in0, in1] + ([] if isinstance(scalar, (int, float)) else [scalar])
        return self.add(eng, lambda e: e.scalar_tensor_tensor(out, in0, scalar, in1, op0, op1),
                        reads=rd, writes=[out])

    def scan(self, out, d0, d1, initial, op0=ALU.mult, op1=ALU.add):
        rd = [d0, d1] + ([] if isinstance(initial, (int, float)) else [initial])
        return self.add("dve", lambda e: e.tensor_tensor_scan(out, d0, d1, initial, op0, op1),
                        reads=rd, writes=[out])

    def copy(self, out, in_, eng="dve"):
        if eng == "act":
            return self.add("act", lambda e: e.copy(out, in_), reads=[in_], writes=[out])
        return self.add(eng, lambda e: e.tensor_copy(out, in_), reads=[in_], writes=[out])

    def memset(self, ap, val, eng="dve"):
        return self.add(eng, lambda e: e.memset(ap, val), writes=[ap])

    def recip(self, out, in_):
        return self.add("dve", lambda e: e.reciprocal(out, in_), reads=[in_], writes=[out])

    def bn_stats(self, out, in_):
        return self.add("dve", lambda e: e.bn_stats(out, in_), reads=[in_], writes=[out])

    def bn_aggr(self, out, in_):
        return self.add("dve", lambda e: e.bn_aggr(out, in_), reads=[in_], writes=[out])

    def dma(self, out, in_, q="sp", is_out=False, **kw):
        if q == "pool":
            kw.setdefault("max_dma_last_dim", 2048)
        return self.add(q, lambda e: e.dma_start(out, in_, **kw), reads=[in_], writes=[out],
                        is_dma=True, is_out=is_out)

    def finalize(self):
        nc = self.nc
        ops = self.ops
        NDMA = {"sp": 10, "pool": 6, "act": 4}
        cnt = {e: 0 for e in COMPUTE}
        sems = {e: [nc.alloc_semaphore(name=f"s_{e}_0")] for e in COMPUTE}
        dma_sems = {q: [nc.alloc_semaphore(name=f"d_{q}_{i}") for i in range(n)] for q, n in NDMA.items()}
        dma_cnt = {q: [0] * n for q, n in NDMA.items()}
        dma_i = {q: 0 for q in NDMA}
        last_of = {}
        for i, op in enumerate(ops):
            last_of[op["eng"]] = i
        for e, i in last_of.items():
            ops[i]["signal"] = True
        for op in ops:
            if op["is_dma"]:
                q = op["eng"]
                k = dma_i[q] % NDMA[q]
                dma_i[q] += 1
                dma_cnt[q][k] += 1
                op["sem"] = dma_sems[q][k]
                op["semkey"] = ("dma", q, k)
                op["val"] = 16 * dma_cnt[q][k]
            elif op["signal"]:
                e = op["eng"]
                if cnt[e] >= self.SEM_ROT:
                    sems[e].append(nc.alloc_semaphore(name=f"s_{e}_{len(sems[e])}"))
                    cnt[e] = 0
                cnt[e] += 1
                op["sem"] = sems[e][-1]
                op["semkey"] = (e, len(sems[e]) - 1)
                op["val"] = cnt[e]
        by_eng = {e: [] for e in ENG_ATTR}
        for i, op in enumerate(ops):
            by_eng[op["eng"]].append(i)
        fw_ = {}
        for op in ops:
            if op["is_dma"] or op["signal"]:
                fw_[op["semkey"]] = (op["sem"], op["val"], op["semkey"])
        final_waits = list(fw_.values())

        def emit_engine(ename, eobj):
            waited = {}

            def wait(sem, val, key):
                if waited.get(key, 0) >= val:
                    return
                waited[key] = val
                eobj.wait_ge(sem, val)

            for i in by_eng[ename]:
                op = ops[i]
                for d in sorted(op["deps"]):
                    po = ops[d]
                    wait(po["sem"], po["val"], po["semkey"])
                if op["is_dma"] and op["val"] > 16:
                    wait(op["sem"], op["val"] - 16, op["semkey"])
                ins = op["emit"](eobj)
                if op["signal"]:
                    ins.then_inc(op["sem"], 16 if op["is_dma"] else 1)
            if ename == "sp":
                for sem, val, key in final_waits:
                    wait(sem, val, key)

        with nc.Block() as block:
            for ename, attr in ENG_ATTR.items():
                if not by_eng[ename] and ename != "sp":
                    continue

                def mk(ename):
                    def f(eobj):
                        emit_engine(ename, eobj)
                    return f
                getattr(block, attr)(mk(ename))
        return nc


D = 1024
S = 2048
CL = 256
T = S + CL
NT = T // 128
W = 512
ALPHA = 4.0 ** 0.25
EPS = 1e-5
LAM_INIT = 0.8 - 0.6 * math.exp(-0.3)
NTILES = [(0, 512), (512, 512), (1024, 512), (1536, 512), (2048, 256)]


def build(NB, debug=False, stage=99):
    P = Prog()
    x_d = P.dram_in("x", [NB * S, D])
    ctx_d = P.dram_in("ctx", [NB * CL, D])
    cond_d = P.dram_in("cond", [128, D])
    rows_d = P.dram_in("rows", [128, 128])
    wmod_d = [P.dram_in(f"w_mod{l}", [D, 3 * D]) for l in range(2)]
    abin_d = P.dram_in("ab_w_in", [D, 5 * W])
    about_d = P.dram_in("ab_w_out", [D, D])
    cdin_d = P.dram_in("cd_w_in", [D, 8 * W])
    cdout_d = P.dram_in("cd_w_out", [D, D])
    qksw_d = P.dram_in("w_qk_sw", [D, 2 * W])
    wbd_d = P.dram_in("lru_wbd", [128, 16, 128])
    wsT_d = P.dram_in("sgu_wT", [128, 4, 128])
    sgub_d = P.dram_in("sgu_b", [1, 4, 128])
    bcln_d = P.dram_in("bc_ln", [128, 4, D])
    bcsgu_d = P.dram_in("bc_sgu", [128, 2, W])
    bcmisc_d = P.dram_in("bc_misc", [128, 384])
    ident_d = P.dram_in("ident", [128, 128])
    sel_d = P.dram_in("sel", [8, 8, 128])
    rope_d = P.dram_in("rope", [128, 2, S])
    out_d = P.dram_out("out", [NB * S, D])
    x1s_d = P.dram_tmp("x1s", [NB * S, D])
    P.base[x1s_d.name] = ("DR_x1s", 0)
    dbg = {}
    if debug:
        dbg["x1"] = P.dram_out("dbg_x1", [S, D])
        dbg["c1"] = P.dram_out("dbg_c1", [CL, D])
        dbg["mod"] = P.dram_out("dbg_mod", [128, 2, 192])

    ident = P.sbuf("ident", [128, 128], F32)
    colsT = P.sbuf("colsT", [128, 128], F32)
    sel = P.sbuf("sel", [128, 8, 128], F32)
    ones = P.sbuf("ones", [128, 128], F32)
    modT = [P.sbuf(f"modT{l}", [128, 24, 8], F32) for l in range(2)]
    sgub = P.sbuf("sgub", [128, 4, 128], F32)
    bc_sgu = P.sbuf("bc_sgu", [128, 2, W], F32)
    bc_misc = P.sbuf("bc_misc", [128, 384], F32)
    sm = P.sbuf("sm", [128, 96], F32)

    stt_ = P.sbuf("stt_", [128, 16, 6], F32)
    mv = P.sbuf("mv", [128, 40, 2], F32)
    rs = P.sbuf("rs", [128, 40], F32)
    wsT = P.sbuf("wsT", [128, 4, 128], BF16)
    wbd = P.sbuf("wbd", [128, 16, 128], BF16)
    HT = P.sbuf("HT", [128, 8, T], BF16)
    yk0 = P.sb_off = (P.sb_off + 63) // 64 * 64
    YC = P.sbuf("YC", [128, 8, T], BF16, at=yk0)
    KT = P.sbuf("KT", [128, 4, T], BF16, at=yk0)
    VA = P.sbuf("VA", [128, NT, 4, 132], BF16, at=yk0 + 4 * T * 2 + 64)
    P.sb_off = yk0 + 4 * T * 2 + 64 + NT * 4 * 132 * 2
    w0 = P.sb_off = (P.sb_off + 63) // 64 * 64
    WQ = [P.sbuf(f"WQ{i}", [128, 8, 512], BF16, at=w0 + i * 8192) for i in range(4)]
    WS = [P.sbuf(f"WS{i}", [128, 8, 128], BF16, at=w0 + 3 * 8192 + i * 2048) for i in range(4)]
    WO = P.sbuf("WO", [128, 8, 1024], BF16, at=w0 + 16384)
    P.sb_off = w0 + 32768
    s0 = P.sb_off = (P.sb_off + 63) // 64 * 64
    TB = 4 * T
    XR = P.sbuf("XR", [128, T], F32, at=s0)
    XC = P.sbuf("XC", [128, T], F32, at=s0 + TB)
    BA = P.sbuf("BA", [128, T], F32, at=s0 + 2 * TB)
    BS = P.sbuf("BS", [128, T], F32, at=s0 + 3 * TB)
    H0 = P.sbuf("H0", [128, T], F32, at=s0 + 4 * TB)
    H1 = P.sbuf("H1", [128, T], F32, at=s0 + 5 * TB)
    XCB = P.sbuf("XCB", [128, T], BF16, at=s0 + 6 * TB)
    SEND = s0 + 6 * TB + 2 * T
    GVB = P.sbuf("GVB", [128, NT, W], BF16, at=s0)
    UGT = P.sbuf("UGT", [128, 4, T], BF16, at=s0 + NT * W * 2)
    GBC = P.sbuf("GBC", [128, D], F32, at=s0)
    LNB = P.sbuf("LNB", [128, 2, D], F32, at=s0 + 4096)
    GBX = P.sbuf("GBX", [128, D], F32, at=s0 + 12288)
    l1 = s0 + 12288
    YC1 = P.sbuf("YC1", [128, 4, S], BF16, at=l1)
    YD1 = P.sbuf("YD1", [128, 4, 512], BF16, at=l1 + 16384)
    QZ = P.sbuf("QZ", [128, 2, 4, 512], BF16, at=l1 + 20480)
    SG0 = P.sbuf("SG0", [128, 4, 512], BF16, at=l1 + 28672)
    OM = P.sbuf("OM", [128, 2, 4, 132], F32, at=l1 + 32768)
    OA = P.sbuf("OA", [128, 16, 128], F32, at=l1 + 32768 + 4224)
    ET = [P.sbuf(f"ET{i}", [128, 512], BF16, at=l1 + 32768 + 4224 + 8192 + i * 1024) for i in range(3)]
    PRD = P.sbuf("PRD", [128, S], F32, at=l1 + 16384)
    CV = P.sbuf("CV", [128, S], F32, at=l1 + 16384 + 8192)
    l1end = l1 + 32768 + 4224 + 8192 + 3072
    assert l1 + 16384 + 16384 <= l1end + 4096
    P.sb_off = max(SEND, l1end, l1 + 32768)
    SG1 = P.sbuf("SG1", [128, 4, 512], BF16)
    SGS = [SG0, SG1]
    NTMP = 5
    TMP = [P.sbuf(f"TMP{i}", [128, 512], F32) for i in range(NTMP)]
    XIN = [P.sbuf(f"XIN{i}", [128, D], F32) for i in range(2)]
    RPS = P.sbuf("RPS", [128, 2, 512], F32, at=P.base[bc_sgu.name][1])
    print("SBUF end", P.sb_off, "top", P.sb_top)
    assert P.sb_off <= P.sb_top
    ps = [P.psum(f"ps{i}", [128, 512]) for i in range(8)]
    tctr = [0]

    def tmp():
        tctr[0] += 1
        return TMP[tctr[0] % NTMP]

    xctr = [0]

    def xin():
        xctr[0] += 1
        return XIN[xctr[0] % 2]

    def wview(dh, c0, n):
        return dh.ap().rearrange("(kc p) n -> p kc n", p=128)[:, :, c0:c0 + n]

    P.dma(ident[:], ident_d.ap())
    P.dma(colsT[:], rows_d.ap())
    P.dma(sgub[0:1], sgub_d.ap())
    P.dma(bc_misc[:], bcmisc_d.ap())
    P.dma(wsT[:], wsT_d.ap(), q="pool")
    P.dma(wbd[:], wbd_d.ap(), q="pool")
    P.dma(sel[0:8], sel_d.ap())
    P.memset(ones[:], 1.0)
    P.tr(ps[0][:, 0:128], colsT[:], ident[:])
    P.copy(colsT[:], ps[0][:, 0:128])
    C_CONVW = lambda k, cc: k * 4 + cc
    C_CONVB = lambda cc: 16 + cc
    C_BA = lambda d, cc: 20 + d * 4 + cc
    C_BX = lambda d, cc: 28 + d * 4 + cc
    C_LAM = lambda d, cc: 36 + d * 4 + cc
    C_SCW = lambda k, cc: 44 + k * 4 + cc
    C_BMOD = lambda l, j: 56 + l * 24 + j
    col = lambda i: colsT[:, i:i + 1]
    for i in range(8):
        P.ts(sm[:, i:i + 1], col(20 + i), 0.5, None, ALU.mult)
        P.ts(sm[:, 8 + i:9 + i], col(28 + i), 0.5, None, ALU.mult)
    zc = sm[:, 32:40]
    P.act(zc, colsT[:, 36:44], AF.Exp, scale=-1.0)
    t1 = sm[:, 40:48]
    P.ts(t1, zc, 1.0 / 3.0, -0.5, ALU.mult, ALU.add)
    P.tt(t1, t1, zc, ALU.mult)
    P.ts(t1, t1, 1.0, None, ALU.add)
    P.tt(t1, t1, zc, ALU.mult)
    P.ts(sm[:, 16:24], t1, -8.0, None, ALU.mult)
    P.ts(sm[:, 24:32], t1, -4.0, None, ALU.mult)
    lt = TMP[0]
    P.tt(lt[:, 0:64], bc_misc[:, 0:64], bc_misc[:, 64:128], ALU.mult)
    P.tt(lt[:, 64:128], bc_misc[:, 128:192], bc_misc[:, 192:256], ALU.mult)
    P.add("dve", lambda e: e.reduce_sum(sm[:, 48:49], lt[:, 0:64], AX.X), reads=[lt[:, 0:64]], writes=[sm[:, 48:49]])
    P.add("dve", lambda e: e.reduce_sum(sm[:, 49:50], lt[:, 64:128], AX.X), reads=[lt[:, 64:128]], writes=[sm[:, 49:50]])
    P.act(sm[:, 50:52], sm[:, 48:50], AF.Exp)
    P.tt(sm[:, 52:53], sm[:, 50:51], sm[:, 51:52], ALU.subtract)
    P.ts(sm[:, 52:53], sm[:, 52:53], LAM_INIT, None, ALU.add)
    lamcol = sm[:, 52:53]
    mh = P.sbuf("mh", [128, 40], F32)
    P.memset(mh[:], -0.5)

    def rsqrt_cols(out, in_, eps):
        n = in_.shape[-1]
        P.ts(out, in_, eps, None, ALU.add)
        P.add("pool", lambda e: e.tensor_tensor(out, out, mh[:, 0:n], ALU.pow),
              reads=[out, mh[:, 0:n]], writes=[out])

    P.dma(XIN[0][:], cond_d.ap())
    sTf = P.sbuf("sTf", [128, 8, 128], BF16, at=P.base[HT.name][1])
    for kc in range(8):
        pb = ps[1 + kc // 4]
        P.tr(pb[:, (kc % 4) * 128:(kc % 4 + 1) * 128], XIN[0][:, kc * 128:(kc + 1) * 128], ident[:])
    for hh in range(2):
        th = TMP[2 + hh]
        P.act(th[:], ps[1 + hh][:], AF.Tanh, scale=0.5)
        P.stt(th[:], th[:], 1.0, ps[1 + hh][:], ALU.add, ALU.mult)
        P.ts(sTf[:, hh * 4:(hh + 1) * 4, :].rearrange("p a b -> p (a b)"), th[:], 0.5, None, ALU.mult)
    for l in range(2):
        for blk in range(6):
            wq = WQ[blk % 2]
            P.dma(wq[:], wview(wmod_d[l], blk * 512, 512), q="pool")
            for jj in range(4):
                j = blk * 4 + jj
                pm = ps[3 + j % 2]
                for kc in range(8):
                    P.mm(pm[:, 0:128], wq[:, kc, jj * 128:(jj + 1) * 128], sTf[:, kc, :],
                         start=(kc == 0), stop=(kc == 7))
                P.ts(modT[l][:, j, :], pm[:, 0:8], col(C_BMOD(l, j)), None, ALU.add)
        P.ts(modT[l][:, 8:16, :], modT[l][:, 8:16, :], 1.0, None, ALU.add)

    def gate_bcast(dst, l, r):
        gc = tmp()
        P.memset(gc[:, 0:128], 0.0)
        P.copy(gc[:, 0:8], modT[l][:, 16:24, r])
        P.tr(ps[3][:, 0:128], gc[:, 0:128], ident[:])
        gT = tmp()
        P.copy(gT[0:8, 0:128], ps[3][0:8, 0:128], eng="act")
        for kc in range(8):
            pb = ps[3 + kc // 4]
            P.mm(pb[:, (kc % 4) * 128:(kc % 4 + 1) * 128], sel[0:8, kc, :], gT[0:8, 0:128])
        P.copy(dst[:, 0:512], ps[3][:], eng="act")
        P.copy(dst[:, 512:1024], ps[4][:], eng="act")

    def load_xtile(b, t, dst):
        if t < 2:
            P.dma(dst[:], ctx_d[b * CL + t * 128: b * CL + (t + 1) * 128, :])
        else:
            P.dma(dst[:], x_d[b * S + (t - 2) * 128: b * S + (t - 1) * 128, :])

    def mod_transpose(src, l, r, tok0):
        for kc in range(8):
            pb = ps[5 + kc // 4]
            P.tr(pb[:, (kc % 4) * 128:(kc % 4 + 1) * 128], src[:, kc * 128:(kc + 1) * 128], ident[:])
        for kc in range(8):
            pb = ps[5 + kc // 4]
            P.act(HT[:, kc, tok0:tok0 + 128], pb[:, (kc % 4) * 128:(kc % 4 + 1) * 128], AF.Identity,
                  bias=modT[l][:, kc, r:r + 1], scale=modT[l][:, 8 + kc, r:r + 1])

    def ln_stats(z, tag_i):
        P.bn_stats(stt_[:, 0, :], z[:, 0:512])
        P.bn_stats(stt_[:, 1, :], z[:, 512:1024])
        P.bn_aggr(mv[:, tag_i, :], stt_[:, 0:2, :].rearrange("p a b -> p (a b)"))
        rsqrt_cols(rs[:, tag_i:tag_i + 1], mv[:, tag_i, 1:2], EPS)

    def ln_apply(z, tag_i, gb, out_ap):
        P.ts(z[:], z[:], mv[:, tag_i, 0:1], rs[:, tag_i:tag_i + 1], ALU.subtract, ALU.mult)
        P.tt(z[:], z[:], gb[:, 0, :], ALU.mult)
        P.tt(out_ap, z[:], gb[:, 1, :], ALU.add)

    def ln_tile(z, tag_i, gb, out_ap):
        ln_stats(z, tag_i)
        ln_apply(z, tag_i, gb, out_ap)

    for b in range(NB):
        for grp in ([0, 1], [2, 3, 4, 5], [6, 7, 8, 9], [10, 11, 12, 13], [14, 15, 16, 17]):
            for j, t in enumerate(grp):
                xt = xin()
                load_xtile(b, t, xt)
                for kc in range(8):
                    P.tr(ps[kc][:, j * 128:(j + 1) * 128], xt[:, kc * 128:(kc + 1) * 128], ident[:])
            r_ = 4 if grp[0] < 2 else b
            n_ = len(grp) * 128
            for kc in range(8):
                P.act(HT[:, kc, grp[0] * 128:grp[0] * 128 + n_], ps[kc][:, 0:n_], AF.Identity,
                      bias=modT[0][:, kc, r_:r_ + 1], scale=modT[0][:, 8 + kc, r_:r_ + 1])
        if stage <= 1:
            return P.finalize()
        for cc in range(4):
            wx, wg = WS[(2 * cc) % 4], WS[(2 * cc + 1) % 4]
            P.dma(wx[:], wview(abin_d, cc * 128, 128), q="pool")
            P.dma(wg[:], wview(abin_d, W + cc * 128, 128), q="pool")
            for (n0, n) in NTILES:
                pb = ps[0 + (n0 // 512) % 2]
                for kc in range(8):
                    P.mm(pb[:, 0:n], wx[:, kc, :], HT[:, kc, n0:n0 + n], start=(kc == 0), stop=(kc == 7))
                P.copy(XR[:, n0:n0 + n], pb[:, 0:n], eng="act")
            for (a0, a1) in ((0, CL), (CL, T)):
                P.ts(XC[:, a0:a1], XR[:, a0:a1], col(C_CONVW(2, cc)), col(C_CONVB(cc)), ALU.mult, ALU.add)
                P.stt(XC[:, a0 + 2:a1], XR[:, a0:a1 - 2], col(C_CONVW(0, cc)), XC[:, a0 + 2:a1], ALU.mult, ALU.add)
                P.stt(XC[:, a0 + 1:a1], XR[:, a0:a1 - 1], col(C_CONVW(1, cc)), XC[:, a0 + 1:a1], ALU.mult, ALU.add)
                P.stt(XC[:, a0:a1 - 1], XR[:, a0 + 1:a1], col(C_CONVW(3, cc)), XC[:, a0:a1 - 1], ALU.mult, ALU.add)
            P.copy(XCB[:], XC[:], eng="act")
            for d in range(2):
                Hd = H0 if d == 0 else H1
                ia = 0 * 8 + d * 4 + cc
                ix = 1 * 8 + d * 4 + cc
                k8 = d * 4 + cc
                for (n0, n) in NTILES:
                    pr = ps[2]
                    pi = ps[3]
                    P.mm(pr[:, 0:n], wbd[:, ia, :], XCB[:, n0:n0 + n])
                    P.mm(pi[:, 0:n], wbd[:, ix, :], XCB[:, n0:n0 + n])
                    thr = tmp()
                    P.act(thr[:, 0:n], pr[:, 0:n], AF.Tanh, bias=sm[:, k8:k8 + 1], scale=0.5)
                    P.act(BA[:, n0:n0 + n], thr[:, 0:n], AF.Exp, bias=sm[:, 24 + k8:25 + k8], scale=sm[:, 24 + k8:25 + k8])
                    P.act(BS[:, n0:n0 + n], thr[:, 0:n], AF.Exp, bias=sm[:, 16 + k8:17 + k8], scale=sm[:, 16 + k8:17 + k8])
                    P.act(XR[:, n0:n0 + n], pi[:, 0:n], AF.Tanh, bias=sm[:, 8 + k8:9 + k8], scale=0.5)
                P.stt(XR[:], XR[:], 1.0, XC[:], ALU.add, ALU.mult)
                P.act(BS[:], BS[:], AF.Sqrt, bias=1.0, scale=-1.0)
                P.stt(BS[:], XR[:], 0.5, BS[:], ALU.mult, ALU.mult)
                if d == 0:
                    P.scan(Hd[:], BA[:], BS[:], 0.0)
                else:
                    P.scan(Hd[:, 0:CL][:, ::-1], BA[:, 0:CL][:, ::-1], BS[:, 0:CL][:, ::-1], 0.0)
                    P.scan(Hd[:, CL:T][:, ::-1], BA[:, CL:T][:, ::-1], BS[:, CL:T][:, ::-1], Hd[:, 0:1])
            P.tt(H0[:], H0[:], H1[:], ALU.add)
            for (n0, n) in NTILES:
                pb = ps[0 + (n0 // 512) % 2]
                for kc in range(8):
                    P.mm(pb[:, 0:n], wg[:, kc, :], HT[:, kc, n0:n0 + n], start=(kc == 0), stop=(kc == 7))
                thg = tmp()
                P.act(thg[:, 0:n], pb[:, 0:n], AF.Tanh, scale=0.5)
                P.stt(thg[:, 0:n], thg[:, 0:n], 1.0, pb[:, 0:n], ALU.add, ALU.mult)
                P.stt(YC[:, cc, n0:n0 + n], thg[:, 0:n], 0.5, H0[:, n0:n0 + n], ALU.mult, ALU.mult)
        if stage <= 2:
            return P.finalize()
        wU, wV, wG = WQ[0], WQ[1], WQ[2]
        P.dma(bc_sgu[:], bcsgu_d.ap())
        P.dma(wU[:], wview(abin_d, 2 * W, W), q="pool")
        P.dma(wV[:], wview(abin_d, 3 * W, W), q="pool")
        P.dma(wG[:], wview(abin_d, 4 * W, W), q="pool")
        for (n0, n) in NTILES:
            for g in range(4):
                pu, pg = ps[0 + g % 2], ps[2 + g % 2]
                for kc in range(8):
                    P.mm(pu[:, 0:n], wU[:, kc, g * 128:(g + 1) * 128], HT[:, kc, n0:n0 + n], start=(kc == 0), stop=(kc == 7))
                for kc in range(8):
                    P.mm(pg[:, 0:n], wG[:, kc, g * 128:(g + 1) * 128], HT[:, kc, n0:n0 + n], start=(kc == 0), stop=(kc == 7))
                gu = tmp()
                P.act(gu[:, 0:n], pu[:, 0:n], AF.Gelu_apprx_tanh)
                tg = tmp()
                P.act(tg[:, 0:n], pg[:, 0:n], AF.Tanh, scale=0.5)
                P.stt(tg[:, 0:n], tg[:, 0:n], 1.0, pg[:, 0:n], ALU.add, ALU.mult)
                P.stt(UGT[:, g, n0:n0 + n], tg[:, 0:n], 0.5, gu[:, 0:n], ALU.mult, ALU.mult)
        for t in range(NT):
            pv = ps[4 + t % 2]
            for kc in range(8):
                P.mm(pv[:], HT[:, kc, t * 128:(t + 1) * 128], wV[:, kc, :], start=(kc == 0), stop=(kc == 7))
            gv = tmp()
            P.act(gv[:], pv[:], AF.Gelu_apprx_tanh)
            P.bn_stats(stt_[:, 2 + t % 2, :], gv[:])
            P.bn_aggr(mv[:, t, :], stt_[:, 2 + t % 2, :])
            P.copy(GVB[:, t, :], gv[:], eng="act")
        rsqrt_cols(rs[:, 0:NT], mv[:, 0:NT, 1], EPS)
        for t in range(NT):
            vn = tmp()
            P.ts(vn[:], GVB[:, t, :], mv[:, t, 0:1], rs[:, t:t + 1], ALU.subtract, ALU.mult)
            P.tt(vn[:], vn[:], bc_sgu[:, 0, :], ALU.mult)
            P.tt(GVB[:, t, :], vn[:], bc_sgu[:, 1, :], ALU.add)
            pS = ps[6 + t % 2]
            for g in range(4):
                P.mm(pS[:, g * 128:(g + 1) * 128], GVB[:, t, g * 128:(g + 1) * 128], wsT[:, g, :], start=True, stop=False)
                P.mm(pS[:, g * 128:(g + 1) * 128], ones[0:1, :], sgub[0:1, g, :], start=False, stop=True)
            P.tt(YC[:, 4:8, t * 128:(t + 1) * 128], pS[:].rearrange("p (g n) -> p g n", g=4),
                 UGT[:, :, t * 128:(t + 1) * 128], ALU.mult)
        if stage <= 3:
            return P.finalize()
        for hf in range(2):
            P.dma(WO[:, :, hf * 512:(hf + 1) * 512], wview(about_d, hf * 512, 512), q="pool")
        P.dma(LNB[:], bcln_d[:, 0:2, :])
        gate_bcast(GBC, 0, b)
        gate_bcast(GBX, 0, 4)
        xts = {}

        def d_mm(t):
            py0, py1 = ps[0 + 2 * (t % 2)], ps[1 + 2 * (t % 2)]
            for hf, py in ((0, py0), (1, py1)):
                for kc in range(8):
                    P.mm(py[:], YC[:, kc, t * 128:(t + 1) * 128], WO[:, kc, hf * 512:(hf + 1) * 512], start=(kc == 0), stop=(kc == 7))

        def d_A(t):
            py0, py1 = ps[0 + 2 * (t % 2)], ps[1 + 2 * (t % 2)]
            xt = xts[t] = xin()
            load_xtile(b, t, xt)
            gb = GBX if t < 2 else GBC
            gy = tmp()
            for hf, py in ((0, py0), (1, py1)):
                P.tt(gy[:], py[:], gb[:, hf * 512:(hf + 1) * 512], ALU.mult)
                P.stt(xt[:, hf * 512:(hf + 1) * 512], xt[:, hf * 512:(hf + 1) * 512], ALPHA, gy[:], ALU.mult, ALU.add)
            ln_stats(xt, 20 + t % 8)

        def d_B(t):
            xt = xts[t]
            ln_apply(xt, 20 + t % 8, LNB, xt[:])
            if t >= 2:
                r0 = b * S + (t - 2) * 128
                P.dma(x1s_d[r0:r0 + 128, :], xt[:])
                if debug and b == 0:
                    P.dma(dbg["x1"][(t - 2) * 128:(t - 1) * 128, :], xt[:], is_out=True)
            elif debug and b == 0:
                P.dma(dbg["c1"][t * 128:(t + 1) * 128, :], xt[:], is_out=True)
            mod_transpose(xt, 1, 4 if t < 2 else b, t * 128)

        d_mm(0)
        d_mm(1)
        d_A(0)
        for t in range(NT):
            if t + 2 < NT:
                d_mm(t + 2)
            if t + 1 < NT:
                d_A(t + 1)
            d_B(t)
        if stage <= 4:
            return P.finalize()
        P.dma(LNB[:], bcln_d[:, 2:4, :])
        if stage <= 4.1:
            return P.finalize()
        gate_bcast(GBC, 1, b)
        if stage <= 4.2:
            return P.finalize()
        wK, wKs, wVv = WQ[0], WQ[1], WQ[2]
        P.dma(wK[:], wview(cdin_d, 5 * W, W), q="pool")
        P.dma(wKs[:], wview(qksw_d, W, W), q="pool")
        P.dma(wVv[:], wview(cdin_d, 6 * W, W), q="pool")
        if stage <= 4.25:
            return P.finalize()
        P.memset(VA[:].rearrange("p t h e -> p (t h) e")[:, :, 128:132], 0.0)
        P.memset(VA[:].rearrange("p t h e -> p (t h) e")[:, :, 128:129], 1.0)
        if stage <= 4.3:
            return P.finalize()
        for (n0, n) in NTILES:
            la, lb = max(n0, CL), n0 + n
            ln_ = lb - la
            c0 = la - n0
            P.dma(RPS[:, :, 0:ln_], rope_d[:, :, la - CL:lb - CL])
            for h in range(4):
                pk, pks = ps[0 + h % 2], ps[2 + h % 2]
                for kc in range(8):
                    P.mm(pk[:, 0:n], wK[:, kc, h * 128:(h + 1) * 128], HT[:, kc, n0:n0 + n], start=(kc == 0), stop=(kc == 7))
                for kc in range(8):
                    P.mm(pks[:, 0:n], wKs[:, kc, h * 128:(h + 1) * 128], HT[:, kc, n0:n0 + n], start=(kc == 0), stop=(kc == 7))
                if n0 == 0:
                    P.copy(KT[:, h, 0:CL], pk[:, 0:CL], eng="act")
                ta, tb = tmp(), tmp()
                P.tt(ta[:, 0:ln_], pk[:, c0:c0 + ln_], RPS[:, 0, 0:ln_], ALU.mult)
                P.tt(tb[:, 0:ln_], pks[:, c0:c0 + ln_], RPS[:, 1, 0:ln_], ALU.mult)
                P.tt(KT[:, h, la:lb], ta[:, 0:ln_], tb[:, 0:ln_], ALU.add)
        if stage <= 4.6:
            return P.finalize()
        for t in range(NT):
            pv = ps[4 + t % 2]
            for kc in range(8):
                P.mm(pv[:], HT[:, kc, t * 128:(t + 1) * 128], wVv[:, kc, :], start=(kc == 0), stop=(kc == 7))
            P.copy(VA[:, t, :, 0:128], pv[:].rearrange("p (h e) -> p h e", h=4), eng="act")
        if stage <= 5:
            return P.finalize()
        for cc in range(4):
            for i in range(4):
                P.dma(WS[i][:], wview(cdin_d, i * W + cc * 128, 128), q="pool")
            for nb in range(4):
                tk = CL + nb * 512
                ph, pc_ = ps[0], ps[1]
                for kc in range(8):
                    P.mm(ph[:], WS[0][:, kc, :], HT[:, kc, tk:tk + 512], start=(kc == 0), stop=(kc == 7))
                for kc in range(8):
                    P.mm(pc_[:], WS[2][:, kc, :], HT[:, kc, tk:tk + 512], start=(kc == 0), stop=(kc == 7))
                hs = tmp()
                P.copy(hs[:], ph[:], eng="act")
                P.tt(PRD[:, nb * 512:(nb + 1) * 512], pc_[:], hs[:], ALU.mult)
            P.ts(CV[:], PRD[:], col(C_SCW(1, cc)), None, ALU.mult)
            P.stt(CV[:, 1:S], PRD[:, 0:S - 1], col(C_SCW(0, cc)), CV[:, 1:S], ALU.mult, ALU.add)
            P.stt(CV[:, 0:S - 1], PRD[:, 1:S], col(C_SCW(2, cc)), CV[:, 0:S - 1], ALU.mult, ALU.add)
            for nb in range(4):
                tk = CL + nb * 512
                pB, pg = ps[2], ps[3]
                for kc in range(8):
                    P.mm(pB[:], WS[1][:, kc, :], HT[:, kc, tk:tk + 512], start=(kc == 0), stop=(kc == 7))
                for kc in range(8):
                    P.mm(pg[:], WS[3][:, kc, :], HT[:, kc, tk:tk + 512], start=(kc == 0), stop=(kc == 7))
                tg, t2 = tmp(), tmp()
                P.act(tg[:], pg[:], AF.Tanh, scale=0.5)
                P.stt(tg[:], tg[:], 1.0, pg[:], ALU.add, ALU.mult)
                P.tt(t2[:], pB[:], CV[:, nb * 512:(nb + 1) * 512], ALU.mult)
                P.stt(YC1[:, cc, nb * 512:(nb + 1) * 512], tg[:], 0.5, t2[:], ALU.mult, ALU.mult)
        if stage <= 6:
            return P.finalize()
        wQ, wQs, wGa = WQ[0], WQ[1], WQ[2]
        P.dma(wQ[:], wview(cdin_d, 4 * W, W), q="pool")
        P.dma(wQs[:], wview(qksw_d, 0, W), q="pool")
        OSC = 0.5 * (1.0 - LAM_INIT)

        def emit_out_tile(qb, qt, b=b):
            t = qb * 4 + qt
            py0, py1 = ps[2], ps[3]
            for hf, py in ((0, py0), (1, py1)):
                for kc in range(8):
                    lhs = YC1[:, kc, t * 128:(t + 1) * 128] if kc < 4 else YD1[:, kc - 4, qt * 128:(qt + 1) * 128]
                    P.mm(py[:], lhs, WO[:, kc, hf * 512:(hf + 1) * 512], start=(kc == 0), stop=(kc == 7))
            xt = xin()
            r0 = b * S + t * 128
            P.dma(xt[:], x1s_d[r0:r0 + 128, :])
            gy = tmp()
            for hf, py in ((0, py0), (1, py1)):
                P.tt(gy[:], py[:], GBC[:, hf * 512:(hf + 1) * 512], ALU.mult)
                P.stt(xt[:, hf * 512:(hf + 1) * 512], xt[:, hf * 512:(hf + 1) * 512], ALPHA, gy[:], ALU.mult, ALU.add)
            ln_tile(xt, 30 + t % 8, LNB, xt[:])
            P.dma(out_d[r0:r0 + 128, :], xt[:], is_out=True)

        pending = []
        def qphase(qb):
            SG = SGS[qb % 2]
            tk = CL + qb * 512
            if qb == 0:
                P.memset(QZ[:].rearrange("p m h n -> p (m h n)"), 0.0)
            P.dma(wGa[:], wview(cdin_d, 7 * W, W), q="pool")
            P.dma(RPS[:], rope_d[:, :, qb * 512:(qb + 1) * 512])
            for h in range(4):
                pq, pqs, pg = ps[(3 * h) % 4], ps[(3 * h + 1) % 4], ps[(3 * h + 2) % 4]
                for kc in range(8):
                    P.mm(pq[:], wQ[:, kc, h * 128:(h + 1) * 128], HT[:, kc, tk:tk + 512], start=(kc == 0), stop=(kc == 7))
                for kc in range(8):
                    P.mm(pqs[:], wQs[:, kc, h * 128:(h + 1) * 128], HT[:, kc, tk:tk + 512], start=(kc == 0), stop=(kc == 7))
                for kc in range(8):
                    P.mm(pg[:], wGa[:, kc, h * 128:(h + 1) * 128], HT[:, kc, tk:tk + 512], start=(kc == 0), stop=(kc == 7))
                ta, tb, tg = tmp(), tmp(), tmp()
                P.tt(ta[:], pq[:], RPS[:, 0, :], ALU.mult)
                P.tt(tb[:], pqs[:], RPS[:, 1, :], ALU.mult)
                P.tt(QZ[0:64, 0, h, :], ta[0:64, :], tb[0:64, :], ALU.add)
                P.tt(QZ[64:128, 1, h, :], ta[64:128, :], tb[64:128, :], ALU.add)
                P.act(tg[:], pg[:], AF.Tanh, scale=0.5)
                P.stt(SG[:, h, :], tg[:], 1.0, pg[:], ALU.add, ALU.mult)

        qphase(0)
        for qb in range(4):
            SG = SGS[qb % 2]
            for hf in range(2):
                P.dma(WO[:, :, hf * 512:(hf + 1) * 512], wview(cdout_d, hf * 512, 512), q="pool")
            seq = [(h, m, kt) for h in range(4) for m in range(2) for kt in range(NT)]

            def score(i):
                h, m, kt = seq[i]
                pS_ = ps[i % 2]
                P.mm(pS_[:], KT[:, h, kt * 128:(kt + 1) * 128], QZ[:, m, h, :])
                P.act(ET[i % 3][:], pS_[:], AF.Exp, scale=0.125)

            def pv(i):
                h, m, kt = seq[i]
                e_ = ET[i % 3]
                for qt in range(4):
                    P.mm(ps[4 + qt][:, 0:130], e_[:, qt * 128:(qt + 1) * 128], VA[:, kt, h, 0:130],
                         start=(kt == 0), stop=(kt == NT - 1))
                if kt == NT - 1:
                    for qt in range(4):
                        P.copy(OM[:, m, qt, 0:129], ps[4 + qt][:, 0:129], eng="dve")
                    if pending and (h * 2 + m) >= 1:
                        emit_out_tile(*pending.pop(0))
                    if m == 1:
                        for qt in range(4):
                            sl = qt * 4 + h
                            c_ = 56 + 2 * (sl % 8)
                            r0_, r1_ = sm[:, c_:c_ + 1], sm[:, c_ + 1:c_ + 2]
                            P.recip(r0_, OM[:, 0, qt, 128:129])
                            P.recip(r1_, OM[:, 1, qt, 128:129])
                            P.ts(r1_, r1_, lamcol, None, ALU.mult)
                            t_ = tmp()
                            P.ts(t_[:, 0:128], OM[:, 1, qt, 0:128], r1_, None, ALU.mult)
                            P.stt(OA[:, sl, :], OM[:, 0, qt, 0:128], r0_, t_[:, 0:128], ALU.mult, ALU.subtract)
                            P.bn_stats(stt_[:, 4 + sl % 8, :], OA[:, sl, :])
                            P.bn_aggr(mv[:, sl, :], stt_[:, 4 + sl % 8, :])
                            P.stt(rs[:, sl:sl + 1], mv[:, sl, 0:1], mv[:, sl, 0:1], mv[:, sl, 1:2], ALU.mult, ALU.add)

            score(0)
            for i in range(len(seq)):
                if i + 1 < len(seq):
                    score(i + 1)
                    if i + 2 == len(seq) and qb + 1 < 4:
                        qphase(qb + 1)
                pv(i)
            rsqrt_cols(rs[:, 0:16], rs[:, 0:16], EPS)
            for h in range(4):
                pt = ps[2 + h % 2]
                for qt in range(4):
                    sl = qt * 4 + h
                    on = tmp()
                    P.ts(on[:, 0:128], OA[:, sl, :], rs[:, sl:sl + 1], OSC, ALU.mult, ALU.mult)
                    P.tt(on[:, 0:128], on[:, 0:128], bc_misc[:, 256:384], ALU.mult)
                    P.tr(pt[:, qt * 128:(qt + 1) * 128], on[:, 0:128], ident[:])
                P.tt(YD1[:, h, :], pt[:], SG[:, h, :], ALU.mult)
            pending.extend((qb, qt) for qt in range(4))
        while pending:
            emit_out_tile(*pending.pop(0))
    return P.finalize()


def prep_shared(inp):
    f = np.float32
    sh = {}
    rows = np.zeros((128, 128), f)
    rows[0:16] = inp["lru_conv_w"][0].reshape(4, 4, 128).reshape(16, 128)
    rows[16:20] = inp["lru_conv_b"][0].reshape(4, 128)
    rows[20:28] = inp["lru_b_a"][0].reshape(8, 128)
    rows[28:36] = inp["lru_b_x"][0].reshape(8, 128)
    rows[36:44] = inp["lru_lambda"][0].reshape(8, 128)
    rows[44:56] = inp["sconv_w"][0].reshape(12, 128)
    rows[56:104] = inp["b_mod"].reshape(48, 128)
    sh["rows"] = rows
    sh["w_mod0"] = np.ascontiguousarray(inp["w_mod"][0])
    sh["w_mod1"] = np.ascontiguousarray(inp["w_mod"][1])
    sh["ab_w_in"] = np.ascontiguousarray(inp["ab_w_in"][0])
    sh["ab_w_out"] = np.ascontiguousarray(inp["ab_w_out"][0])
    cd = inp["cd_w_in"][0]
    sh["cd_w_in"] = np.ascontiguousarray(cd)
    sh["cd_w_out"] = np.ascontiguousarray(inp["cd_w_out"][0])
    perm = np.arange(512) ^ 16
    sh["w_qk_sw"] = np.ascontiguousarray(np.concatenate([cd[:, 2048:2560][:, perm], cd[:, 2560:3072][:, perm]], axis=1))
    wbd = np.zeros((128, 16, 128), f)
    for gi, key in enumerate(("lru_w_a", "lru_w_x")):
        w = inp[key][0]
        for d in range(2):
            for cc in range(4):
                idx = gi * 8 + d * 4 + cc
                wbd[0:64, idx, 0:64] = w[d, 2 * cc]
                wbd[64:128, idx, 64:128] = w[d, 2 * cc + 1]
    sh["lru_wbd"] = wbd
    sh["sgu_wT"] = np.ascontiguousarray(np.transpose(inp["sgu_w"][0], (2, 0, 1)))
    sh["sgu_b"] = np.ascontiguousarray(inp["sgu_b"][0][None])
    bc_ln = np.stack([inp["ln_g"][0], inp["ln_b"][0], inp["ln_g"][1], inp["ln_b"][1]], 0)
    sh["bc_ln"] = np.ascontiguousarray(np.broadcast_to(bc_ln[None], (128, 4, 1024)))
    bc_sgu = np.stack([inp["sgu_ln_g"][0], inp["sgu_ln_b"][0]], 0)
    sh["bc_sgu"] = np.ascontiguousarray(np.broadcast_to(bc_sgu[None], (128, 2, 512)))
    misc = np.concatenate([inp["diff_lambda"][0].reshape(256), inp["diff_subln_g"][0].reshape(128)])
    sh["bc_misc"] = np.ascontiguousarray(np.broadcast_to(misc[None], (128, 384)))
    sh["ident"] = np.eye(128, dtype=f)
    sel = np.zeros((8, 8, 128), f)
    for k in range(8):
        sel[k, k, :] = 1.0
    sh["sel"] = sel
    t = np.arange(2048)
    row = (t // 64).astype(f)
    colp = (t % 64).astype(f)
    inv = (np.float32(10000.0) ** (-np.arange(16, dtype=f) / np.float32(16))).astype(f)
    ang = np.concatenate([row[:, None] * inv, row[:, None] * inv, colp[:, None] * inv, colp[:, None] * inv], axis=-1).astype(f)
    cos = np.cos(ang).astype(f)
    sin = np.sin(ang).astype(f)
    sgn = np.where((np.arange(64) % 32) < 16, -1.0, 1.0).astype(f)
    sin_s = sin * sgn[None, :]
    rope = np.zeros((128, 2, 2048), f)
    rope[:, 0, :] = np.concatenate([cos.T, cos.T], 0)
    rope[:, 1, :] = np.concatenate([sin_s.T, sin_s.T], 0)
    sh["rope"] = rope
    return sh

def prep_core(inp, sh, b0, NB):
    f = np.float32
    m = dict(sh)
    m["x"] = np.ascontiguousarray(inp["x"][b0:b0 + NB].reshape(NB * 2048, 1024))
    m["ctx"] = np.ascontiguousarray(inp["ctx"][b0:b0 + NB].reshape(NB * 256, 1024))
    cond = np.zeros((128, 1024), f)
    cond[0:NB] = inp["c"][b0:b0 + NB]
    cond[4] = inp["c_ctx"]
    m["cond"] = cond
    return m


NB_PER_CORE = 4
N_CORES = 8


def kernel(**inputs):
    inp = {k: np.asarray(v) for k, v in inputs.items()}
    sh = prep_shared(inp)
    nc = build(NB_PER_CORE)
    in_maps = [prep_core(inp, sh, c * NB_PER_CORE, NB_PER_CORE) for c in range(N_CORES)]
    res = run_bass_kernel_spmd(nc, in_maps, core_ids=list(range(N_CORES)))
    outs = [np.asarray(r["out"]).reshape(NB_PER_CORE, S, D) for r in res.results]
    return np.concatenate(outs, axis=0).astype(np.float32)
```

```python
import math

import numpy as np
import concourse.bass as bass
import concourse.mybir as mybir
from concourse.bass_utils import run_bass_kernel_spmd

F32 = mybir.dt.float32
BF16 = mybir.dt.bfloat16
AF = mybir.ActivationFunctionType
ALU = mybir.AluOpType
AX = mybir.AxisListType

ENG_ATTR = {"pe": "tensor", "act": "scalar", "dve": "vector", "pool": "gpsimd", "sp": "sync"}
COMPUTE = ("pe", "act", "dve", "pool")
_ESZ = {F32: 4, BF16: 2, mybir.dt.int32: 4, mybir.dt.uint32: 4, mybir.dt.float16: 2}


class Prog:
    SEM_ROT = 30000

    def __init__(self):
        self.nc = bass.Bass("TRN2", target_bir_lowering=False)
        self.ops = []
        self.acc = {}
        self.base = {}
        self.sb_off = 16512
        self.sb_top = 229344
        self.n_psum = 0
        self.dma_rr = {"sp": 0, "pool": 0, "act": 0}
        self.out_dmas = []

    def sbuf(self, name, shape, dtype, at=None):
        esz = _ESZ[dtype]
        nbytes = int(np.prod(shape[1:])) * esz
        if at is None:
            at = (self.sb_off + 63) // 64 * 64
            self.sb_off = at + nbytes
            assert self.sb_off <= self.sb_top, (name, self.sb_off)
        t = self.nc.alloc_sbuf_tensor_at(name, list(shape), dtype, offset=at)
        self.base[t.name] = ("SB", at)
        return t

    def psum(self, name, shape, dtype=F32):
        t = self.nc.alloc_psum_tensor(name, list(shape), dtype)
        self.base[t.name] = ("PS_" + name, 0)
        return t

    def bitcast(self, t, dtype):
        v = t.bitcast(dtype)
        self.base[v.name] = self.base[t.name]
        return v

    def dram_in(self, name, shape, dtype=F32):
        return self.nc.dram_tensor(name, list(shape), dtype, kind="ExternalInput")

    def dram_out(self, name, shape, dtype=F32):
        return self.nc.dram_tensor(name, list(shape), dtype, kind="ExternalOutput")

    def dram_tmp(self, name, shape, dtype=F32):
        return self.nc.dram_tensor(name, list(shape), dtype)

    def region(self, ap):
        t = ap.tensor
        key = self.base.get(t.name)
        if key is None:
            return None
        space, b0 = key
        if space.startswith("PS_"):
            return (space, 0, 128, 0, 2048)
        esz = _ESZ[ap.dtype]
        dims = ap.ap
        pstep, pcnt = dims[0]
        off = ap.offset
        if pstep > 0:
            p0 = off // pstep
            f0 = off % pstep
        else:
            p0, f0 = 0, off
        lo = hi = f0
        for step, cnt in dims[1:]:
            if step >= 0:
                hi += step * (cnt - 1)
            else:
                lo += step * (cnt - 1)
        hi += 1
        return (space, p0, p0 + pcnt, b0 + lo * esz, b0 + hi * esz)

    def add(self, eng, emit, reads=(), writes=(), is_dma=False, is_out=False):
        idx = len(self.ops)
        tag = ("dma", idx) if is_dma else eng
        deps = set()
        rr = [self.region(a) for a in reads if a is not None and not isinstance(a, (int, float))]
        wr = [self.region(a) for a in writes]
        rr = [r for r in rr if r is not None]
        wr = [r for r in wr if r is not None]
        wr = wr + [r for r in rr if r[0].startswith("PS_") and r not in wr]
        for r in rr:
            for e in self.acc.get(r[0], ()):
                if e[6] and e[0] < r[2] and r[1] < e[1] and e[2] < r[4] and r[3] < e[3]:
                    deps.add((e[5], "raw"))
        for w in wr:
            for e in self.acc.get(w[0], ()):
                if e[0] < w[2] and w[1] < e[1] and e[2] < w[4] and w[3] < e[3]:
                    deps.add((e[5], "waw" if e[6] else "war"))
        fdeps = set()
        for d, kind in deps:
            po = self.ops[d]
            if po["is_dma"] or is_dma:
                fdeps.add(d)
            elif po["eng"] == eng:
                if eng != "pe":
                    fdeps.add(d)
            else:
                fdeps.add(d)
        op = dict(eng=eng, emit=emit, deps=fdeps, is_dma=is_dma, is_out=is_out, signal=False)
        self.ops.append(op)
        for d in fdeps:
            self.ops[d]["signal"] = True
        for w in wr:
            lst = self.acc.setdefault(w[0], [])
            lst[:] = [e for e in lst if not (w[1] <= e[0] and e[1] <= w[2] and w[3] <= e[2] and e[3] <= w[4])]
            lst.append([w[1], w[2], w[3], w[4], tag, idx, True])
        for r in rr:
            lst = self.acc.setdefault(r[0], [])
            found = False
            for e in lst:
                if (not e[6]) and e[4] == tag and e[0] == r[1] and e[1] == r[2] and e[2] == r[3] and e[3] == r[4]:
                    e[5] = idx
                    found = True
                    break
            if not found:
                lst.append([r[1], r[2], r[3], r[4], tag, idx, False])
        if is_dma:
            op["signal"] = True
        return idx

    def mm(self, out, lhsT, rhs, start=True, stop=True, **kw):
        return self.add("pe", lambda e: e.matmul(out, lhsT, rhs, start=start, stop=stop, **kw),
                        reads=[lhsT, rhs], writes=[out])

    def tr(self, out, in_, ident):
        return self.add("pe", lambda e: e.transpose(out, in_, ident), reads=[in_, ident], writes=[out])

    def act(self, out, in_, func, bias=0.0, scale=1.0, eng="act"):
        rd = [in_]
        if not isinstance(bias, (int, float)):
            rd.append(bias)
        if not isinstance(scale, (int, float)):
            rd.append(scale)
        return self.add(eng, lambda e: e.activation(out, in_, func, bias=bias, scale=scale),
                        reads=rd, writes=[out])

    def tt(self, out, a, b, op, eng="dve"):
        return self.add(eng, lambda e: e.tensor_tensor(out, a, b, op), reads=[a, b], writes=[out])

    def ts(self, out, a, s1, s2, op0, op1=None, eng="dve"):
        rd = [a] + [s for s in (s1, s2) if s is not None and not isinstance(s, (int, float))]
        if op1 is None:
            return self.add(eng, lambda e: e.tensor_scalar(out, a, s1, None, op0), reads=rd, writes=[out])
        return self.add(eng, lambda e: e.tensor_scalar(out, a, s1, s2, op0, op1), reads=rd, writes=[out])

    def stt(self, out, in0, scalar, in1, op0, op1, eng="dve"):
        rd = [in0, in1] + ([] if isinstance(scalar, (int, float)) else [scalar])
        return self.add(eng, lambda e: e.scalar_tensor_tensor(out, in0, scalar, in1, op0, op1),
                        reads=rd, writes=[out])

    def scan(self, out, d0, d1, initial, op0=ALU.mult, op1=ALU.add):
        rd = [d0, d1] + ([] if isinstance(initial, (int, float)) else [initial])
        return self.add("dve", lambda e: e.tensor_tensor_scan(out, d0, d1, initial, op0, op1),
                        reads=rd, writes=[out])

    def copy(self, out, in_, eng="dve"):
        if eng == "act":
            return self.add("act", lambda e: e.copy(out, in_), reads=[in_], writes=[out])
        return self.add(eng, lambda e: e.tensor_copy(out, in_), reads=[in_], writes=[out])

    def memset(self, ap, val, eng="dve"):
        return self.add(eng, lambda e: e.memset(ap, val), writes=[ap])

    def recip(self, out, in_):
        return self.add("dve", lambda e: e.reciprocal(out, in_), reads=[in_], writes=[out])

    def bn_stats(self, out, in_):
        return self.add("dve", lambda e: e.bn_stats(out, in_), reads=[in_], writes=[out])

    def bn_aggr(self, out, in_):
        return self.add("dve", lambda e: e.bn_aggr(out, in_), reads=[in_], writes=[out])

    def dma(self, out, in_, q="sp", is_out=False, **kw):
        if q == "pool":
            kw.setdefault("max_dma_last_dim", 2048)
        return self.add(q, lambda e: e.dma_start(out, in_, **kw), reads=[in_], writes=[out],
                        is_dma=True, is_out=is_out)

    def finalize(self):
        nc = self.nc
        ops = self.ops
        NDMA = {"sp": 10, "pool": 6, "act": 4}
        cnt = {e: 0 for e in COMPUTE}
        sems = {e: [nc.alloc_semaphore(name=f"s_{e}_0")] for e in COMPUTE}
        dma_sems = {q: [nc.alloc_semaphore(name=f"d_{q}_{i}") for i in range(n)] for q, n in NDMA.items()}
        dma_cnt = {q: [0] * n for q, n in NDMA.items()}
        dma_i = {q: 0 for q in NDMA}
        last_of = {}
        for i, op in enumerate(ops):
            last_of[op["eng"]] = i
        for e, i in last_of.items():
            ops[i]["signal"] = True
        for op in ops:
            if op["is_dma"]:
                q = op["eng"]
                k = dma_i[q] % NDMA[q]
                dma_i[q] += 1
                dma_cnt[q][k] += 1
                op["sem"] = dma_sems[q][k]
                op["semkey"] = ("dma", q, k)
                op["val"] = 16 * dma_cnt[q][k]
            elif op["signal"]:
                e = op["eng"]
                if cnt[e] >= self.SEM_ROT:
                    sems[e].append(nc.alloc_semaphore(name=f"s_{e}_{len(sems[e])}"))
                    cnt[e] = 0
                cnt[e] += 1
                op["sem"] = sems[e][-1]
                op["semkey"] = (e, len(sems[e]) - 1)
                op["val"] = cnt[e]
        by_eng = {e: [] for e in ENG_ATTR}
        for i, op in enumerate(ops):
            by_eng[op["eng"]].append(i)
        fw_ = {}
        for op in ops:
            if op["is_dma"] or op["signal"]:
                fw_[op["semkey"]] = (op["sem"], op["val"], op["semkey"])
        final_waits = list(fw_.values())

        def emit_engine(ename, eobj):
            waited = {}

            def wait(sem, val, key):
                if waited.get(key, 0) >= val:
                    return
                waited[key] = val
                eobj.wait_ge(sem, val)

            for i in by_eng[ename]:
                op = ops[i]
                for d in sorted(op["deps"]):
                    po = ops[d]
                    wait(po["sem"], po["val"], po["semkey"])
                if op["is_dma"] and op["val"] > 16:
                    wait(op["sem"], op["val"] - 16, op["semkey"])
                ins = op["emit"](eobj)
                if op["signal"]:
                    ins.then_inc(op["sem"], 16 if op["is_dma"] else 1)
            if ename == "sp":
                for sem, val, key in final_waits:
                    wait(sem, val, key)

        with nc.Block() as block:
            for ename, attr in ENG_ATTR.items():
                if not by_eng[ename] and ename != "sp":
                    continue

                def mk(ename):
                    def f(eobj):
                        emit_engine(ename, eobj)
                    return f
                getattr(block, attr)(mk(ename))
        return nc


D = 1024
S = 2048
CL = 256
T = S + CL
NT = T // 128
W = 512
ALPHA = 4.0 ** 0.25
EPS = 1e-5
LAM_INIT = 0.8 - 0.6 * math.exp(-0.3)
NTILES = [(0, 512), (512, 512), (1024, 512), (1536, 512), (2048, 256)]


def build(NB, debug=False, stage=99):
    P = Prog()
    x_d = P.dram_in("x", [NB * S, D])
    ctx_d = P.dram_in("ctx", [NB * CL, D])
    cond_d = P.dram_in("cond", [128, D])
    rows_d = P.dram_in("rows", [128, 128])
    wmod_d = [P.dram_in(f"w_mod{l}", [D, 3 * D]) for l in range(2)]
    abin_d = P.dram_in("ab_w_in", [D, 5 * W])
    about_d = P.dram_in("ab_w_out", [D, D])
    cdin_d = P.dram_in("cd_w_in", [D, 8 * W])
    cdout_d = P.dram_in("cd_w_out", [D, D])
    qksw_d = P.dram_in("w_qk_sw", [D, 2 * W])
    wbd_d = P.dram_in("lru_wbd", [128, 16, 128])
    wsT_d = P.dram_in("sgu_wT", [128, 4, 128])
    sgub_d = P.dram_in("sgu_b", [1, 4, 128])
    bcln_d = P.dram_in("bc_ln", [128, 4, D])
    bcsgu_d = P.dram_in("bc_sgu", [128, 2, W])
    bcmisc_d = P.dram_in("bc_misc", [128, 384])
    ident_d = P.dram_in("ident", [128, 128])
    rope_d = P.dram_in("rope", [128, 2, S])
    out_d = P.dram_out("out", [NB * S, D])
    x1s_d = P.dram_tmp("x1s", [NB * S, D])
    P.base[x1s_d.name] = ("DR_x1s", 0)
    dbg = {}
    if debug:
        dbg["x1"] = P.dram_out("dbg_x1", [S, D])
        dbg["c1"] = P.dram_out("dbg_c1", [CL, D])
        dbg["mod"] = P.dram_out("dbg_mod", [128, 2, 192])

    ident = P.sbuf("ident", [128, 128], F32)
    colsT = P.sbuf("colsT", [128, 128], F32)
    zeros = P.sbuf("zeros", [128, 128], F32)
    ones = P.sbuf("ones", [128, 128], F32)
    modT = [P.sbuf(f"modT{l}", [128, 24, 8], F32) for l in range(2)]
    sgub = P.sbuf("sgub", [128, 4, 128], F32)
    bc_sgu = P.sbuf("bc_sgu", [128, 2, W], F32)
    bc_misc = P.sbuf("bc_misc", [128, 384], F32)
    sm = P.sbuf("sm", [128, 96], F32)

    stt_ = P.sbuf("stt_", [128, 16, 6], F32)
    mv = P.sbuf("mv", [128, 40, 2], F32)
    rs = P.sbuf("rs", [128, 40], F32)
    wsT = P.sbuf("wsT", [128, 4, 128], BF16)
    wbd = P.sbuf("wbd", [128, 16, 128], BF16)
    HT = P.sbuf("HT", [128, 8, T], BF16)
    yk0 = P.sb_off = (P.sb_off + 63) // 64 * 64
    YC = P.sbuf("YC", [128, 8, T], BF16, at=yk0)
    KT = P.sbuf("KT", [128, 4, T], BF16, at=yk0)
    VA = P.sbuf("VA", [128, NT, 4, 132], BF16, at=yk0 + 4 * T * 2 + 64)
    P.sb_off = yk0 + 4 * T * 2 + 64 + NT * 4 * 132 * 2
    w0 = P.sb_off = (P.sb_off + 63) // 64 * 64
    WQ = [P.sbuf(f"WQ{i}", [128, 8, 512], BF16, at=w0 + i * 8192) for i in range(4)]
    WS = [P.sbuf(f"WS{i}", [128, 8, 128], BF16, at=w0 + 3 * 8192 + i * 2048) for i in range(4)]
    WO = P.sbuf("WO", [128, 8, 1024], BF16, at=w0 + 16384)
    P.sb_off = w0 + 32768
    s0 = P.sb_off = (P.sb_off + 63) // 64 * 64
    TB = 4 * T
    XR = P.sbuf("XR", [128, T], F32, at=s0)
    XC = P.sbuf("XC", [128, T], F32, at=s0 + TB)
    BA = P.sbuf("BA", [128, T], F32, at=s0 + 2 * TB)
    BS = P.sbuf("BS", [128, T], F32, at=s0 + 3 * TB)
    H0 = P.sbuf("H0", [128, T], F32, at=s0 + 4 * TB)
    H1 = P.sbuf("H1", [128, T], F32, at=s0 + 5 * TB)
    XCB = P.sbuf("XCB", [128, T], BF16, at=s0 + 6 * TB)
    SEND = s0 + 6 * TB + 2 * T
    GVB = P.sbuf("GVB", [128, NT, W], BF16, at=s0)
    UGT = P.sbuf("UGT", [128, 4, T], BF16, at=s0 + NT * W * 2)
    GBC = P.sbuf("GBC", [128, D], F32, at=s0)
    LNB = P.sbuf("LNB", [128, 2, D], F32, at=s0 + 4096)
    GBX = P.sbuf("GBX", [128, D], F32, at=s0 + 12288)
    l1 = s0 + 12288
    YC1 = P.sbuf("YC1", [128, 4, S], BF16, at=l1)
    YD1 = P.sbuf("YD1", [128, 4, 512], BF16, at=l1 + 16384)
    QZ = P.sbuf("QZ", [128, 2, 4, 512], BF16, at=l1 + 20480)
    SG0 = P.sbuf("SG0", [128, 4, 512], BF16, at=l1 + 28672)
    OM = P.sbuf("OM", [128, 2, 4, 132], F32, at=l1 + 32768)
    OA = P.sbuf("OA", [128, 16, 128], F32, at=l1 + 32768 + 4224)
    ET = [P.sbuf(f"ET{i}", [128, 512], BF16, at=l1 + 32768 + 4224 + 8192 + i * 1024) for i in range(3)]
    PRD = P.sbuf("PRD", [128, S], F32, at=l1 + 16384)
    CV = P.sbuf("CV", [128, S], F32, at=l1 + 16384 + 8192)
    l1end = l1 + 32768 + 4224 + 8192 + 3072
    assert l1 + 16384 + 16384 <= l1end + 4096
    P.sb_off = max(SEND, l1end, l1 + 32768)
    SG1 = P.sbuf("SG1", [128, 4, 512], BF16)
    SGS = [SG0, SG1]
    NTMP = 5
    TMP = [P.sbuf(f"TMP{i}", [128, 512], F32) for i in range(NTMP)]
    XIN = [P.sbuf(f"XIN{i}", [128, D], F32) for i in range(3)]
    RPS = P.sbuf("RPS", [128, 2, 512], F32, at=P.base[bc_sgu.name][1])
    print("SBUF end", P.sb_off, "top", P.sb_top)
    assert P.sb_off <= P.sb_top
    ps = [P.psum(f"ps{i}", [128, 512]) for i in range(8)]
    tctr = [0]

    def tmp():
        tctr[0] += 1
        return TMP[tctr[0] % NTMP]

    xctr = [0]

    def xin():
        xctr[0] += 1
        return XIN[xctr[0] % 3]

    def wview(dh, c0, n):
        return dh.ap().rearrange("(kc p) n -> p kc n", p=128)[:, :, c0:c0 + n]

    P.dma(ident[:], ident_d.ap())
    P.dma(colsT[:], rows_d.ap())
    P.dma(sgub[0:1], sgub_d.ap())
    P.dma(bc_misc[:], bcmisc_d.ap())
    P.dma(wsT[:], wsT_d.ap(), q="pool")
    P.dma(wbd[:], wbd_d.ap(), q="pool")
    P.memset(zeros[:], 0.0)
    P.memset(ones[:], 1.0)
    P.tr(ps[0][:, 0:128], colsT[:], ident[:])
    P.copy(colsT[:], ps[0][:, 0:128])
    C_CONVW = lambda k, cc: k * 4 + cc
    C_CONVB = lambda cc: 16 + cc
    C_BA = lambda d, cc: 20 + d * 4 + cc
    C_BX = lambda d, cc: 28 + d * 4 + cc
    C_LAM = lambda d, cc: 36 + d * 4 + cc
    C_SCW = lambda k, cc: 44 + k * 4 + cc
    C_BMOD = lambda l, j: 56 + l * 24 + j
    col = lambda i: colsT[:, i:i + 1]
    for i in range(8):
        P.ts(sm[:, i:i + 1], col(20 + i), 0.5, None, ALU.mult)
        P.ts(sm[:, 8 + i:9 + i], col(28 + i), 0.5, None, ALU.mult)
    zc = sm[:, 32:40]
    P.act(zc, colsT[:, 36:44], AF.Exp, scale=-1.0)
    t1 = sm[:, 40:48]
    P.ts(t1, zc, 1.0 / 3.0, -0.5, ALU.mult, ALU.add)
    P.tt(t1, t1, zc, ALU.mult)
    P.ts(t1, t1, 1.0, None, ALU.add)
    P.tt(t1, t1, zc, ALU.mult)
    P.ts(sm[:, 16:24], t1, -8.0, None, ALU.mult)
    P.ts(sm[:, 24:32], t1, -4.0, None, ALU.mult)
    lt = TMP[0]
    P.tt(lt[:, 0:64], bc_misc[:, 0:64], bc_misc[:, 64:128], ALU.mult)
    P.tt(lt[:, 64:128], bc_misc[:, 128:192], bc_misc[:, 192:256], ALU.mult)
    P.add("dve", lambda e: e.reduce_sum(sm[:, 48:49], lt[:, 0:64], AX.X), reads=[lt[:, 0:64]], writes=[sm[:, 48:49]])
    P.add("dve", lambda e: e.reduce_sum(sm[:, 49:50], lt[:, 64:128], AX.X), reads=[lt[:, 64:128]], writes=[sm[:, 49:50]])
    P.act(sm[:, 50:52], sm[:, 48:50], AF.Exp)
    P.tt(sm[:, 52:53], sm[:, 50:51], sm[:, 51:52], ALU.subtract)
    P.ts(sm[:, 52:53], sm[:, 52:53], LAM_INIT, None, ALU.add)
    lamcol = sm[:, 52:53]
    mh = P.sbuf("mh", [128, 40], F32)
    P.memset(mh[:], -0.5)

    def rsqrt_cols(out, in_, eps):
        n = in_.shape[-1]
        P.ts(out, in_, eps, None, ALU.add)
        P.add("pool", lambda e: e.tensor_tensor(out, out, mh[:, 0:n], ALU.pow),
              reads=[out, mh[:, 0:n]], writes=[out])

    P.dma(XIN[0][:], cond_d.ap())
    sTf = P.sbuf("sTf", [128, 8, 128], BF16, at=P.base[HT.name][1])
    for kc in range(8):
        pb = ps[1 + kc // 4]
        P.tr(pb[:, (kc % 4) * 128:(kc % 4 + 1) * 128], XIN[0][:, kc * 128:(kc + 1) * 128], ident[:])
    for hh in range(2):
        th = TMP[2 + hh]
        P.act(th[:], ps[1 + hh][:], AF.Tanh, scale=0.5)
        P.stt(th[:], th[:], 1.0, ps[1 + hh][:], ALU.add, ALU.mult)
        P.ts(sTf[:, hh * 4:(hh + 1) * 4, :].rearrange("p a b -> p (a b)"), th[:], 0.5, None, ALU.mult)
    for l in range(2):
        for blk in range(6):
            wq = WQ[blk % 2]
            P.dma(wq[:], wview(wmod_d[l], blk * 512, 512), q="pool")
            for jj in range(4):
                j = blk * 4 + jj
                pm = ps[3 + j % 2]
                for kc in range(8):
                    P.mm(pm[:, 0:128], wq[:, kc, jj * 128:(jj + 1) * 128], sTf[:, kc, :],
                         start=(kc == 0), stop=(kc == 7))
                P.ts(modT[l][:, j, :], pm[:, 0:8], col(C_BMOD(l, j)), None, ALU.add)
        P.ts(modT[l][:, 8:16, :], modT[l][:, 8:16, :], 1.0, None, ALU.add)

    def gate_bcast(dst, l, r):
        for kc in range(8):
            rep = tmp()
            P.act(rep[:, 0:128], zeros[:], AF.Identity, bias=modT[l][:, 16 + kc, r:r + 1], scale=0.0)
            pb = ps[3 + kc // 4]
            P.mm(pb[:, (kc % 4) * 128:(kc % 4 + 1) * 128], rep[:, 0:128], ident[:])
        P.copy(dst[:, 0:512], ps[3][:], eng="act")
        P.copy(dst[:, 512:1024], ps[4][:], eng="act")

    def load_xtile(b, t, dst):
        if t < 2:
            P.dma(dst[:], ctx_d[b * CL + t * 128: b * CL + (t + 1) * 128, :])
        else:
            P.dma(dst[:], x_d[b * S + (t - 2) * 128: b * S + (t - 1) * 128, :])

    def mod_transpose(src, l, r, tok0):
        for kc in range(8):
            pb = ps[5 + kc // 4]
            P.tr(pb[:, (kc % 4) * 128:(kc % 4 + 1) * 128], src[:, kc * 128:(kc + 1) * 128], ident[:])
        for kc in range(8):
            pb = ps[5 + kc // 4]
            P.act(HT[:, kc, tok0:tok0 + 128], pb[:, (kc % 4) * 128:(kc % 4 + 1) * 128], AF.Identity,
                  bias=modT[l][:, kc, r:r + 1], scale=modT[l][:, 8 + kc, r:r + 1])

    def ln_stats(z, tag_i):
        P.bn_stats(stt_[:, 0, :], z[:, 0:512])
        P.bn_stats(stt_[:, 1, :], z[:, 512:1024])
        P.bn_aggr(mv[:, tag_i, :], stt_[:, 0:2, :].rearrange("p a b -> p (a b)"))
        rsqrt_cols(rs[:, tag_i:tag_i + 1], mv[:, tag_i, 1:2], EPS)

    def ln_apply(z, tag_i, gb, out_ap):
        P.ts(z[:], z[:], mv[:, tag_i, 0:1], rs[:, tag_i:tag_i + 1], ALU.subtract, ALU.mult)
        P.tt(z[:], z[:], gb[:, 0, :], ALU.mult)
        P.tt(out_ap, z[:], gb[:, 1, :], ALU.add)

    def ln_tile(z, tag_i, gb, out_ap):
        ln_stats(z, tag_i)
        ln_apply(z, tag_i, gb, out_ap)

    for b in range(NB):
        for grp in ([0, 1], [2, 3, 4, 5], [6, 7, 8, 9], [10, 11, 12, 13], [14, 15, 16, 17]):
            for j, t in enumerate(grp):
                xt = xin()
                load_xtile(b, t, xt)
                for kc in range(8):
                    P.tr(ps[kc][:, j * 128:(j + 1) * 128], xt[:, kc * 128:(kc + 1) * 128], ident[:])
            r_ = 4 if grp[0] < 2 else b
            n_ = len(grp) * 128
            for kc in range(8):
                P.act(HT[:, kc, grp[0] * 128:grp[0] * 128 + n_], ps[kc][:, 0:n_], AF.Identity,
                      bias=modT[0][:, kc, r_:r_ + 1], scale=modT[0][:, 8 + kc, r_:r_ + 1])
        if stage <= 1:
            return P.finalize()
        for cc in range(4):
            wx, wg = WS[(2 * cc) % 4], WS[(2 * cc + 1) % 4]
            P.dma(wx[:], wview(abin_d, cc * 128, 128), q="pool")
            P.dma(wg[:], wview(abin_d, W + cc * 128, 128), q="pool")
            for (n0, n) in NTILES:
                pb = ps[0 + (n0 // 512) % 2]
                for kc in range(8):
                    P.mm(pb[:, 0:n], wx[:, kc, :], HT[:, kc, n0:n0 + n], start=(kc == 0), stop=(kc == 7))
                P.copy(XR[:, n0:n0 + n], pb[:, 0:n], eng="act")
            for (a0, a1) in ((0, CL), (CL, T)):
                P.ts(XC[:, a0:a1], XR[:, a0:a1], col(C_CONVW(2, cc)), col(C_CONVB(cc)), ALU.mult, ALU.add)
                P.stt(XC[:, a0 + 2:a1], XR[:, a0:a1 - 2], col(C_CONVW(0, cc)), XC[:, a0 + 2:a1], ALU.mult, ALU.add)
                P.stt(XC[:, a0 + 1:a1], XR[:, a0:a1 - 1], col(C_CONVW(1, cc)), XC[:, a0 + 1:a1], ALU.mult, ALU.add)
                P.stt(XC[:, a0:a1 - 1], XR[:, a0 + 1:a1], col(C_CONVW(3, cc)), XC[:, a0:a1 - 1], ALU.mult, ALU.add)
            P.copy(XCB[:], XC[:], eng="act")
            for d in range(2):
                Hd = H0 if d == 0 else H1
                ia = 0 * 8 + d * 4 + cc
                ix = 1 * 8 + d * 4 + cc
                k8 = d * 4 + cc
                for (n0, n) in NTILES:
                    pr = ps[2]
                    pi = ps[3]
                    P.mm(pr[:, 0:n], wbd[:, ia, :], XCB[:, n0:n0 + n])
                    P.mm(pi[:, 0:n], wbd[:, ix, :], XCB[:, n0:n0 + n])
                    thr = tmp()
                    P.act(thr[:, 0:n], pr[:, 0:n], AF.Tanh, bias=sm[:, k8:k8 + 1], scale=0.5)
                    P.act(BA[:, n0:n0 + n], thr[:, 0:n], AF.Exp, bias=sm[:, 24 + k8:25 + k8], scale=sm[:, 24 + k8:25 + k8])
                    P.act(BS[:, n0:n0 + n], thr[:, 0:n], AF.Exp, bias=sm[:, 16 + k8:17 + k8], scale=sm[:, 16 + k8:17 + k8])
                    P.act(XR[:, n0:n0 + n], pi[:, 0:n], AF.Tanh, bias=sm[:, 8 + k8:9 + k8], scale=0.5)
                P.stt(XR[:], XR[:], 1.0, XC[:], ALU.add, ALU.mult)
                P.act(BS[:], BS[:], AF.Sqrt, bias=1.0, scale=-1.0)
                P.stt(BS[:], XR[:], 0.5, BS[:], ALU.mult, ALU.mult)
                if d == 0:
                    P.scan(Hd[:], BA[:], BS[:], 0.0)
                else:
                    P.scan(Hd[:, 0:CL][:, ::-1], BA[:, 0:CL][:, ::-1], BS[:, 0:CL][:, ::-1], 0.0)
                    P.scan(Hd[:, CL:T][:, ::-1], BA[:, CL:T][:, ::-1], BS[:, CL:T][:, ::-1], Hd[:, 0:1])
            P.tt(H0[:], H0[:], H1[:], ALU.add)
            for (n0, n) in NTILES:
                pb = ps[0 + (n0 // 512) % 2]
                for kc in range(8):
                    P.mm(pb[:, 0:n], wg[:, kc, :], HT[:, kc, n0:n0 + n], start=(kc == 0), stop=(kc == 7))
                thg = tmp()
                P.act(thg[:, 0:n], pb[:, 0:n], AF.Tanh, scale=0.5)
                P.stt(thg[:, 0:n], thg[:, 0:n], 1.0, pb[:, 0:n], ALU.add, ALU.mult)
                P.stt(YC[:, cc, n0:n0 + n], thg[:, 0:n], 0.5, H0[:, n0:n0 + n], ALU.mult, ALU.mult)
        if stage <= 2:
            return P.finalize()
        wU, wV, wG = WQ[0], WQ[1], WQ[2]
        P.dma(bc_sgu[:], bcsgu_d.ap())
        P.dma(wU[:], wview(abin_d, 2 * W, W), q="pool")
        P.dma(wV[:], wview(abin_d, 3 * W, W), q="pool")
        P.dma(wG[:], wview(abin_d, 4 * W, W), q="pool")
        for (n0, n) in NTILES:
            for g in range(4):
                pu, pg = ps[0 + g % 2], ps[2 + g % 2]
                for kc in range(8):
                    P.mm(pu[:, 0:n], wU[:, kc, g * 128:(g + 1) * 128], HT[:, kc, n0:n0 + n], start=(kc == 0), stop=(kc == 7))
                for kc in range(8):
                    P.mm(pg[:, 0:n], wG[:, kc, g * 128:(g + 1) * 128], HT[:, kc, n0:n0 + n], start=(kc == 0), stop=(kc == 7))
                gu = tmp()
                P.act(gu[:, 0:n], pu[:, 0:n], AF.Gelu_apprx_tanh)
                tg = tmp()
                P.act(tg[:, 0:n], pg[:, 0:n], AF.Tanh, scale=0.5)
                P.stt(tg[:, 0:n], tg[:, 0:n], 1.0, pg[:, 0:n], ALU.add, ALU.mult)
                P.stt(UGT[:, g, n0:n0 + n], tg[:, 0:n], 0.5, gu[:, 0:n], ALU.mult, ALU.mult)
        for t in range(NT):
            pv = ps[4 + t % 2]
            for kc in range(8):
                P.mm(pv[:], HT[:, kc, t * 128:(t + 1) * 128], wV[:, kc, :], start=(kc == 0), stop=(kc == 7))
            gv = tmp()
            P.act(gv[:], pv[:], AF.Gelu_apprx_tanh)
            P.bn_stats(stt_[:, 2 + t % 2, :], gv[:])
            P.bn_aggr(mv[:, t, :], stt_[:, 2 + t % 2, :])
            P.copy(GVB[:, t, :], gv[:], eng="act")
        rsqrt_cols(rs[:, 0:NT], mv[:, 0:NT, 1], EPS)
        for t in range(NT):
            vn = tmp()
            P.ts(vn[:], GVB[:, t, :], mv[:, t, 0:1], rs[:, t:t + 1], ALU.subtract, ALU.mult)
            P.tt(vn[:], vn[:], bc_sgu[:, 0, :], ALU.mult)
            P.tt(GVB[:, t, :], vn[:], bc_sgu[:, 1, :], ALU.add)
            pS = ps[6 + t % 2]
            for g in range(4):
                P.mm(pS[:, g * 128:(g + 1) * 128], GVB[:, t, g * 128:(g + 1) * 128], wsT[:, g, :], start=True, stop=False)
                P.mm(pS[:, g * 128:(g + 1) * 128], ones[0:1, :], sgub[0:1, g, :], start=False, stop=True)
            P.tt(YC[:, 4:8, t * 128:(t + 1) * 128], pS[:].rearrange("p (g n) -> p g n", g=4),
                 UGT[:, :, t * 128:(t + 1) * 128], ALU.mult)
        if stage <= 3:
            return P.finalize()
        for hf in range(2):
            P.dma(WO[:, :, hf * 512:(hf + 1) * 512], wview(about_d, hf * 512, 512), q="pool")
        P.dma(LNB[:], bcln_d[:, 0:2, :])
        gate_bcast(GBC, 0, b)
        gate_bcast(GBX, 0, 4)
        xts = {}

        def d_mm(t):
            py0, py1 = ps[0 + 2 * (t % 2)], ps[1 + 2 * (t % 2)]
            for hf, py in ((0, py0), (1, py1)):
                for kc in range(8):
                    P.mm(py[:], YC[:, kc, t * 128:(t + 1) * 128], WO[:, kc, hf * 512:(hf + 1) * 512], start=(kc == 0), stop=(kc == 7))

        def d_A(t):
            py0, py1 = ps[0 + 2 * (t % 2)], ps[1 + 2 * (t % 2)]
            xt = xts[t] = xin()
            load_xtile(b, t, xt)
            gb = GBX if t < 2 else GBC
            gy = tmp()
            for hf, py in ((0, py0), (1, py1)):
                P.tt(gy[:], py[:], gb[:, hf * 512:(hf + 1) * 512], ALU.mult)
                P.stt(xt[:, hf * 512:(hf + 1) * 512], xt[:, hf * 512:(hf + 1) * 512], ALPHA, gy[:], ALU.mult, ALU.add)
            ln_stats(xt, 20 + t % 8)

        def d_B(t):
            xt = xts[t]
            ln_apply(xt, 20 + t % 8, LNB, xt[:])
            if t >= 2:
                r0 = b * S + (t - 2) * 128
                P.dma(x1s_d[r0:r0 + 128, :], xt[:])
                if debug and b == 0:
                    P.dma(dbg["x1"][(t - 2) * 128:(t - 1) * 128, :], xt[:], is_out=True)
            elif debug and b == 0:
                P.dma(dbg["c1"][t * 128:(t + 1) * 128, :], xt[:], is_out=True)
            mod_transpose(xt, 1, 4 if t < 2 else b, t * 128)

        d_mm(0)
        d_mm(1)
        d_A(0)
        for t in range(NT):
            if t + 2 < NT:
                d_mm(t + 2)
            if t + 1 < NT:
                d_A(t + 1)
            d_B(t)
        if stage <= 4:
            return P.finalize()
        P.dma(LNB[:], bcln_d[:, 2:4, :])
        if stage <= 4.1:
            return P.finalize()
        gate_bcast(GBC, 1, b)
        if stage <= 4.2:
            return P.finalize()
        wK, wKs, wVv = WQ[0], WQ[1], WQ[2]
        P.dma(wK[:], wview(cdin_d, 5 * W, W), q="pool")
        P.dma(wKs[:], wview(qksw_d, W, W), q="pool")
        P.dma(wVv[:], wview(cdin_d, 6 * W, W), q="pool")
        if stage <= 4.25:
            return P.finalize()
        P.memset(VA[:].rearrange("p t h e -> p (t h) e")[:, :, 128:132], 0.0)
        P.memset(VA[:].rearrange("p t h e -> p (t h) e")[:, :, 128:129], 1.0)
        if stage <= 4.3:
            return P.finalize()
        for (n0, n) in NTILES:
            la, lb = max(n0, CL), n0 + n
            ln_ = lb - la
            c0 = la - n0
            P.dma(RPS[:, :, 0:ln_], rope_d[:, :, la - CL:lb - CL])
            for h in range(4):
                pk, pks = ps[0 + h % 2], ps[2 + h % 2]
                for kc in range(8):
                    P.mm(pk[:, 0:n], wK[:, kc, h * 128:(h + 1) * 128], HT[:, kc, n0:n0 + n], start=(kc == 0), stop=(kc == 7))
                for kc in range(8):
                    P.mm(pks[:, 0:n], wKs[:, kc, h * 128:(h + 1) * 128], HT[:, kc, n0:n0 + n], start=(kc == 0), stop=(kc == 7))
                if n0 == 0:
                    P.copy(KT[:, h, 0:CL], pk[:, 0:CL], eng="act")
                ta, tb = tmp(), tmp()
                P.tt(ta[:, 0:ln_], pk[:, c0:c0 + ln_], RPS[:, 0, 0:ln_], ALU.mult)
                P.tt(tb[:, 0:ln_], pks[:, c0:c0 + ln_], RPS[:, 1, 0:ln_], ALU.mult)
                P.tt(KT[:, h, la:lb], ta[:, 0:ln_], tb[:, 0:ln_], ALU.add)
        if stage <= 4.6:
            return P.finalize()
        for t in range(NT):
            pv = ps[4 + t % 2]
            for kc in range(8):
                P.mm(pv[:], HT[:, kc, t * 128:(t + 1) * 128], wVv[:, kc, :], start=(kc == 0), stop=(kc == 7))
            P.copy(VA[:, t, :, 0:128], pv[:].rearrange("p (h e) -> p h e", h=4), eng="act")
        if stage <= 5:
            return P.finalize()
        for cc in range(4):
            for i in range(4):
                P.dma(WS[i][:], wview(cdin_d, i * W + cc * 128, 128), q="pool")
            for nb in range(4):
                tk = CL + nb * 512
                ph, pc_ = ps[0], ps[1]
                for kc in range(8):
                    P.mm(ph[:], WS[0][:, kc, :], HT[:, kc, tk:tk + 512], start=(kc == 0), stop=(kc == 7))
                for kc in range(8):
                    P.mm(pc_[:], WS[2][:, kc, :], HT[:, kc, tk:tk + 512], start=(kc == 0), stop=(kc == 7))
                hs = tmp()
                P.copy(hs[:], ph[:], eng="act")
                P.tt(PRD[:, nb * 512:(nb + 1) * 512], pc_[:], hs[:], ALU.mult)
            P.ts(CV[:], PRD[:], col(C_SCW(1, cc)), None, ALU.mult)
            P.stt(CV[:, 1:S], PRD[:, 0:S - 1], col(C_SCW(0, cc)), CV[:, 1:S], ALU.mult, ALU.add)
            P.stt(CV[:, 0:S - 1], PRD[:, 1:S], col(C_SCW(2, cc)), CV[:, 0:S - 1], ALU.mult, ALU.add)
            for nb in range(4):
                tk = CL + nb * 512
                pB, pg = ps[2], ps[3]
                for kc in range(8):
                    P.mm(pB[:], WS[1][:, kc, :], HT[:, kc, tk:tk + 512], start=(kc == 0), stop=(kc == 7))
                for kc in range(8):
                    P.mm(pg[:], WS[3][:, kc, :], HT[:, kc, tk:tk + 512], start=(kc == 0), stop=(kc == 7))
                tg, t2 = tmp(), tmp()
                P.act(tg[:], pg[:], AF.Tanh, scale=0.5)
                P.stt(tg[:], tg[:], 1.0, pg[:], ALU.add, ALU.mult)
                P.tt(t2[:], pB[:], CV[:, nb * 512:(nb + 1) * 512], ALU.mult)
                P.stt(YC1[:, cc, nb * 512:(nb + 1) * 512], tg[:], 0.5, t2[:], ALU.mult, ALU.mult)
        if stage <= 6:
            return P.finalize()
        wQ, wQs, wGa = WQ[0], WQ[1], WQ[2]
        P.dma(wQ[:], wview(cdin_d, 4 * W, W), q="pool")
        P.dma(wQs[:], wview(qksw_d, 0, W), q="pool")
        OSC = 0.5 * (1.0 - LAM_INIT)

        def emit_out_tile(qb, qt, b=b):
            t = qb * 4 + qt
            py0, py1 = ps[2], ps[3]
            for hf, py in ((0, py0), (1, py1)):
                for kc in range(8):
                    lhs = YC1[:, kc, t * 128:(t + 1) * 128] if kc < 4 else YD1[:, kc - 4, qt * 128:(qt + 1) * 128]
                    P.mm(py[:], lhs, WO[:, kc, hf * 512:(hf + 1) * 512], start=(kc == 0), stop=(kc == 7))
            xt = xin()
            r0 = b * S + t * 128
            P.dma(xt[:], x1s_d[r0:r0 + 128, :])
            gy = tmp()
            for hf, py in ((0, py0), (1, py1)):
                P.tt(gy[:], py[:], GBC[:, hf * 512:(hf + 1) * 512], ALU.mult)
                P.stt(xt[:, hf * 512:(hf + 1) * 512], xt[:, hf * 512:(hf + 1) * 512], ALPHA, gy[:], ALU.mult, ALU.add)
            ln_tile(xt, 30 + t % 8, LNB, xt[:])
            P.dma(out_d[r0:r0 + 128, :], xt[:], is_out=True)

        pending = []
        def qphase(qb):
            SG = SGS[qb % 2]
            tk = CL + qb * 512
            if qb == 0:
                P.memset(QZ[:].rearrange("p m h n -> p (m h n)"), 0.0)
            P.dma(wGa[:], wview(cdin_d, 7 * W, W), q="pool")
            P.dma(RPS[:], rope_d[:, :, qb * 512:(qb + 1) * 512])
            for h in range(4):
                pq, pqs, pg = ps[(3 * h) % 4], ps[(3 * h + 1) % 4], ps[(3 * h + 2) % 4]
                for kc in range(8):
                    P.mm(pq[:], wQ[:, kc, h * 128:(h + 1) * 128], HT[:, kc, tk:tk + 512], start=(kc == 0), stop=(kc == 7))
                for kc in range(8):
                    P.mm(pqs[:], wQs[:, kc, h * 128:(h + 1) * 128], HT[:, kc, tk:tk + 512], start=(kc == 0), stop=(kc == 7))
                for kc in range(8):
                    P.mm(pg[:], wGa[:, kc, h * 128:(h + 1) * 128], HT[:, kc, tk:tk + 512], start=(kc == 0), stop=(kc == 7))
                ta, tb, tg = tmp(), tmp(), tmp()
                P.tt(ta[:], pq[:], RPS[:, 0, :], ALU.mult)
                P.tt(tb[:], pqs[:], RPS[:, 1, :], ALU.mult)
                P.tt(QZ[0:64, 0, h, :], ta[0:64, :], tb[0:64, :], ALU.add)
                P.tt(QZ[64:128, 1, h, :], ta[64:128, :], tb[64:128, :], ALU.add)
                P.act(tg[:], pg[:], AF.Tanh, scale=0.5)
                P.stt(SG[:, h, :], tg[:], 1.0, pg[:], ALU.add, ALU.mult)

        qphase(0)
        for qb in range(4):
            SG = SGS[qb % 2]
            for hf in range(2):
                P.dma(WO[:, :, hf * 512:(hf + 1) * 512], wview(cdout_d, hf * 512, 512), q="pool")
            seq = [(h, m, kt) for h in range(4) for m in range(2) for kt in range(NT)]

            def score(i):
                h, m, kt = seq[i]
                pS_ = ps[i % 2]
                P.mm(pS_[:], KT[:, h, kt * 128:(kt + 1) * 128], QZ[:, m, h, :])
                P.act(ET[i % 3][:], pS_[:], AF.Exp, scale=0.125)

            def pv(i):
                h, m, kt = seq[i]
                e_ = ET[i % 3]
                for qt in range(4):
                    P.mm(ps[4 + qt][:, 0:130], e_[:, qt * 128:(qt + 1) * 128], VA[:, kt, h, 0:130],
                         start=(kt == 0), stop=(kt == NT - 1))
                if kt == NT - 1:
                    for qt in range(4):
                        P.copy(OM[:, m, qt, 0:129], ps[4 + qt][:, 0:129], eng="dve")
                    if pending and (h * 2 + m) >= 1:
                        emit_out_tile(*pending.pop(0))
                    if m == 1:
                        for qt in range(4):
                            sl = qt * 4 + h
                            c_ = 56 + 2 * (sl % 8)
                            r0_, r1_ = sm[:, c_:c_ + 1], sm[:, c_ + 1:c_ + 2]
                            P.recip(r0_, OM[:, 0, qt, 128:129])
                            P.recip(r1_, OM[:, 1, qt, 128:129])
                            P.ts(r1_, r1_, lamcol, None, ALU.mult)
                            t_ = tmp()
                            P.ts(t_[:, 0:128], OM[:, 1, qt, 0:128], r1_, None, ALU.mult)
                            P.stt(OA[:, sl, :], OM[:, 0, qt, 0:128], r0_, t_[:, 0:128], ALU.mult, ALU.subtract)
                            P.bn_stats(stt_[:, 4 + sl % 8, :], OA[:, sl, :])
                            P.bn_aggr(mv[:, sl, :], stt_[:, 4 + sl % 8, :])
                            P.stt(rs[:, sl:sl + 1], mv[:, sl, 0:1], mv[:, sl, 0:1], mv[:, sl, 1:2], ALU.mult, ALU.add)

            score(0)
            for i in range(len(seq)):
                if i + 1 < len(seq):
                    score(i + 1)
                    if i + 2 == len(seq) and qb + 1 < 4:
                        qphase(qb + 1)
                pv(i)
            rsqrt_cols(rs[:, 0:16], rs[:, 0:16], EPS)
            for h in range(4):
                pt = ps[2 + h % 2]
                for qt in range(4):
                    sl = qt * 4 + h
                    on = tmp()
                    P.ts(on[:, 0:128], OA[:, sl, :], rs[:, sl:sl + 1], OSC, ALU.mult, ALU.mult)
                    P.tt(on[:, 0:128], on[:, 0:128], bc_misc[:, 256:384], ALU.mult)
                    P.tr(pt[:, qt * 128:(qt + 1) * 128], on[:, 0:128], ident[:])
                P.tt(YD1[:, h, :], pt[:], SG[:, h, :], ALU.mult)
            pending.extend((qb, qt) for qt in range(4))
        while pending:
            emit_out_tile(*pending.pop(0))
    return P.finalize()


def prep_shared(inp):
    f = np.float32
    sh = {}
    rows = np.zeros((128, 128), f)
    rows[0:16] = inp["lru_conv_w"][0].reshape(4, 4, 128).reshape(16, 128)
    rows[16:20] = inp["lru_conv_b"][0].reshape(4, 128)
    rows[20:28] = inp["lru_b_a"][0].reshape(8, 128)
    rows[28:36] = inp["lru_b_x"][0].reshape(8, 128)
    rows[36:44] = inp["lru_lambda"][0].reshape(8, 128)
    rows[44:56] = inp["sconv_w"][0].reshape(12, 128)
    rows[56:104] = inp["b_mod"].reshape(48, 128)
    sh["rows"] = rows
    sh["w_mod0"] = np.ascontiguousarray(inp["w_mod"][0])
    sh["w_mod1"] = np.ascontiguousarray(inp["w_mod"][1])
    sh["ab_w_in"] = np.ascontiguousarray(inp["ab_w_in"][0])
    sh["ab_w_out"] = np.ascontiguousarray(inp["ab_w_out"][0])
    cd = inp["cd_w_in"][0]
    sh["cd_w_in"] = np.ascontiguousarray(cd)
    sh["cd_w_out"] = np.ascontiguousarray(inp["cd_w_out"][0])
    perm = np.arange(512) ^ 16
    sh["w_qk_sw"] = np.ascontiguousarray(np.concatenate([cd[:, 2048:2560][:, perm], cd[:, 2560:3072][:, perm]], axis=1))
    wbd = np.zeros((128, 16, 128), f)
    for gi, key in enumerate(("lru_w_a", "lru_w_x")):
        w = inp[key][0]
        for d in range(2):
            for cc in range(4):
                idx = gi * 8 + d * 4 + cc
                wbd[0:64, idx, 0:64] = w[d, 2 * cc]
                wbd[64:128, idx, 64:128] = w[d, 2 * cc + 1]
    sh["lru_wbd"] = wbd
    sh["sgu_wT"] = np.ascontiguousarray(np.transpose(inp["sgu_w"][0], (2, 0, 1)))
    sh["sgu_b"] = np.ascontiguousarray(inp["sgu_b"][0][None])
    bc_ln = np.stack([inp["ln_g"][0], inp["ln_b"][0], inp["ln_g"][1], inp["ln_b"][1]], 0)
    sh["bc_ln"] = np.ascontiguousarray(np.broadcast_to(bc_ln[None], (128, 4, 1024)))
    bc_sgu = np.stack([inp["sgu_ln_g"][0], inp["sgu_ln_b"][0]], 0)
    sh["bc_sgu"] = np.ascontiguousarray(np.broadcast_to(bc_sgu[None], (128, 2, 512)))
    misc = np.concatenate([inp["diff_lambda"][0].reshape(256), inp["diff_subln_g"][0].reshape(128)])
    sh["bc_misc"] = np.ascontiguousarray(np.broadcast_to(misc[None], (128, 384)))
    sh["ident"] = np.eye(128, dtype=f)
    t = np.arange(2048)
    row = (t // 64).astype(f)
    colp = (t % 64).astype(f)
    inv = (np.float32(10000.0) ** (-np.arange(16, dtype=f) / np.float32(16))).astype(f)
    ang = np.concatenate([row[:, None] * inv, row[:, None] * inv, colp[:, None] * inv, colp[:, None] * inv], axis=-1).astype(f)
    cos = np.cos(ang).astype(f)
    sin = np.sin(ang).astype(f)
    sgn = np.where((np.arange(64) % 32) < 16, -1.0, 1.0).astype(f)
    sin_s = sin * sgn[None, :]
    rope = np.zeros((128, 2, 2048), f)
    rope[:, 0, :] = np.concatenate([cos.T, cos.T], 0)
    rope[:, 1, :] = np.concatenate([sin_s.T, sin_s.T], 0)
    sh["rope"] = rope
    return sh

def prep_core(inp, sh, b0, NB):
    f = np.float32
    m = dict(sh)
    m["x"] = np.ascontiguousarray(inp["x"][b0:b0 + NB].reshape(NB * 2048, 1024))
    m["ctx"] = np.ascontiguousarray(inp["ctx"][b0:b0 + NB].reshape(NB * 256, 1024))
    cond = np.zeros((128, 1024), f)
    cond[0:NB] = inp["c"][b0:b0 + NB]
    cond[4] = inp["c_ctx"]
    m["cond"] = cond
    return m


NB_PER_CORE = 4
N_CORES = 8


def kernel(**inputs):
    inp = {k: np.asarray(v) for k, v in inputs.items()}
    sh = prep_shared(inp)
    nc = build(NB_PER_CORE)
    in_maps = [prep_core(inp, sh, c * NB_PER_CORE, NB_PER_CORE) for c in range(N_CORES)]
    res = run_bass_kernel_spmd(nc, in_maps, core_ids=list(range(N_CORES)))
    outs = [np.asarray(r["out"]).reshape(NB_PER_CORE, S, D) for r in res.results]
    return np.concatenate(outs, axis=0).astype(np.float32)
```

```python
import math

import numpy as np
import concourse.bass as bass
import concourse.mybir as mybir
from concourse.bass_utils import run_bass_kernel_spmd

F32 = mybir.dt.float32
BF16 = mybir.dt.bfloat16
AF = mybir.ActivationFunctionType
ALU = mybir.AluOpType
AX = mybir.AxisListType

ENG_ATTR = {"pe": "tensor", "act": "scalar", "dve": "vector", "pool": "gpsimd", "sp": "sync"}
COMPUTE = ("pe", "act", "dve", "pool")
_ESZ = {F32: 4, BF16: 2, mybir.dt.int32: 4, mybir.dt.uint32: 4, mybir.dt.float16: 2}


class Prog:
    SEM_ROT = 30000

    def __init__(self):
        self.nc = bass.Bass("TRN2", target_bir_lowering=False)
        self.ops = []
        self.acc = {}
        self.base = {}
        self.sb_off = 16512
        self.sb_top = 229344
        self.n_psum = 0
        self.dma_rr = {"sp": 0, "pool": 0, "act": 0}
        self.out_dmas = []

    def sbuf(self, name, shape, dtype, at=None):
        esz = _ESZ[dtype]
        nbytes = int(np.prod(shape[1:])) * esz
        if at is None:
            at = (self.sb_off + 63) // 64 * 64
            self.sb_off = at + nbytes
            assert self.sb_off <= self.sb_top, (name, self.sb_off)
        t = self.nc.alloc_sbuf_tensor_at(name, list(shape), dtype, offset=at)
        self.base[t.name] = ("SB", at)
        return t

    def psum(self, name, shape, dtype=F32):
        t = self.nc.alloc_psum_tensor(name, list(shape), dtype)
        self.base[t.name] = ("PS_" + name, 0)
        return t

    def bitcast(self, t, dtype):
        v = t.bitcast(dtype)
        self.base[v.name] = self.base[t.name]
        return v

    def dram_in(self, name, shape, dtype=F32):
        return self.nc.dram_tensor(name, list(shape), dtype, kind="ExternalInput")

    def dram_out(self, name, shape, dtype=F32):
        return self.nc.dram_tensor(name, list(shape), dtype, kind="ExternalOutput")

    def dram_tmp(self, name, shape, dtype=F32):
        return self.nc.dram_tensor(name, list(shape), dtype)

    def region(self, ap):
        t = ap.tensor
        key = self.base.get(t.name)
        if key is None:
            return None
        space, b0 = key
        if space.startswith("PS_"):
            return (space, 0, 128, 0, 2048)
        esz = _ESZ[ap.dtype]
        dims = ap.ap
        pstep, pcnt = dims[0]
        off = ap.offset
        if pstep > 0:
            p0 = off // pstep
            f0 = off % pstep
        else:
            p0, f0 = 0, off
        lo = hi = f0
        for step, cnt in dims[1:]:
            if step >= 0:
                hi += step * (cnt - 1)
            else:
                lo += step * (cnt - 1)
        hi += 1
        return (space, p0, p0 + pcnt, b0 + lo * esz, b0 + hi * esz)

    def add(self, eng, emit, reads=(), writes=(), is_dma=False, is_out=False):
        idx = len(self.ops)
        tag = ("dma", idx) if is_dma else eng
        deps = set()
        rr = [self.region(a) for a in reads if a is not None and not isinstance(a, (int, float))]
        wr = [self.region(a) for a in writes]
        rr = [r for r in rr if r is not None]
        wr = [r for r in wr if r is not None]
        wr = wr + [r for r in rr if r[0].startswith("PS_") and r not in wr]
        for r in rr:
            for e in self.acc.get(r[0], ()):
                if e[6] and e[0] < r[2] and r[1] < e[1] and e[2] < r[4] and r[3] < e[3]:
                    deps.add((e[5], "raw"))
        for w in wr:
            for e in self.acc.get(w[0], ()):
                if e[0] < w[2] and w[1] < e[1] and e[2] < w[4] and w[3] < e[3]:
                    deps.add((e[5], "waw" if e[6] else "war"))
        fdeps = set()
        for d, kind in deps:
            po = self.ops[d]
            if po["is_dma"] or is_dma:
                fdeps.add(d)
            elif po["eng"] == eng:
                if eng != "pe":
                    fdeps.add(d)
            else:
                fdeps.add(d)
        op = dict(eng=eng, emit=emit, deps=fdeps, is_dma=is_dma, is_out=is_out, signal=False)
        self.ops.append(op)
        for d in fdeps:
            self.ops[d]["signal"] = True
        for w in wr:
            lst = self.acc.setdefault(w[0], [])
            lst[:] = [e for e in lst if not (w[1] <= e[0] and e[1] <= w[2] and w[3] <= e[2] and e[3] <= w[4])]
            lst.append([w[1], w[2], w[3], w[4], tag, idx, True])
        for r in rr:
            lst = self.acc.setdefault(r[0], [])
            found = False
            for e in lst:
                if (not e[6]) and e[4] == tag and e[0] == r[1] and e[1] == r[2] and e[2] == r[3] and e[3] == r[4]:
                    e[5] = idx
                    found = True
                    break
            if not found:
                lst.append([r[1], r[2], r[3], r[4], tag, idx, False])
        if is_dma:
            op["signal"] = True
        return idx

    def mm(self, out, lhsT, rhs, start=True, stop=True, **kw):
        return self.add("pe", lambda e: e.matmul(out, lhsT, rhs, start=start, stop=stop, **kw),
                        reads=[lhsT, rhs], writes=[out])

    def tr(self, out, in_, ident):
        return self.add("pe", lambda e: e.transpose(out, in_, ident), reads=[in_, ident], writes=[out])

    def act(self, out, in_, func, bias=0.0, scale=1.0, eng="act"):
        rd = [in_]
        if not isinstance(bias, (int, float)):
            rd.append(bias)
        if not isinstance(scale, (int, float)):
            rd.append(scale)
        return self.add(eng, lambda e: e.activation(out, in_, func, bias=bias, scale=scale),
                        reads=rd, writes=[out])

    def tt(self, out, a, b, op, eng="dve"):
        return self.add(eng, lambda e: e.tensor_tensor(out, a, b, op), reads=[a, b], writes=[out])

    def ts(self, out, a, s1, s2, op0, op1=None, eng="dve"):
        rd = [a] + [s for s in (s1, s2) if s is not None and not isinstance(s, (int, float))]
        if op1 is None:
            return self.add(eng, lambda e: e.tensor_scalar(out, a, s1, None, op0), reads=rd, writes=[out])
        return self.add(eng, lambda e: e.tensor_scalar(out, a, s1, s2, op0, op1), reads=rd, writes=[out])

    def stt(self, out, in0, scalar, in1, op0, op1, eng="dve"):
        rd = [in0, in1] + ([] if isinstance(scalar, (int, float)) else [scalar])
        return self.add(eng, lambda e: e.scalar_tensor_tensor(out, in0, scalar, in1, op0, op1),
                        reads=rd, writes=[out])

    def scan(self, out, d0, d1, initial, op0=ALU.mult, op1=ALU.add):
        rd = [d0, d1] + ([] if isinstance(initial, (int, float)) else [initial])
        return self.add("dve", lambda e: e.tensor_tensor_scan(out, d0, d1, initial, op0, op1),
                        reads=rd, writes=[out])

    def copy(self, out, in_, eng="dve"):
        if eng == "act":
            return self.add("act", lambda e: e.copy(out, in_), reads=[in_], writes=[out])
        return self.add(eng, lambda e: e.tensor_copy(out, in_), reads=[in_], writes=[out])

    def memset(self, ap, val, eng="dve"):
        return self.add(eng, lambda e: e.memset(ap, val), writes=[ap])

    def recip(self, out, in_):
        return self.add("dve", lambda e: e.reciprocal(out, in_), reads=[in_], writes=[out])

    def bn_stats(self, out, in_):
        return self.add("dve", lambda e: e.bn_stats(out, in_), reads=[in_], writes=[out])

    def bn_aggr(self, out, in_):
        return self.add("dve", lambda e: e.bn_aggr(out, in_), reads=[in_], writes=[out])

    def dma(self, out, in_, q="sp", is_out=False, **kw):
        if q == "pool":
            kw.setdefault("max_dma_last_dim", 2048)
        return self.add(q, lambda e: e.dma_start(out, in_, **kw), reads=[in_], writes=[out],
                        is_dma=True, is_out=is_out)

    def finalize(self):
        nc = self.nc
        ops = self.ops
        NDMA = {"sp": 10, "pool": 6, "act": 4}
        cnt = {e: 0 for e in COMPUTE}
        sems = {e: [nc.alloc_semaphore(name=f"s_{e}_0")] for e in COMPUTE}
        dma_sems = {q: [nc.alloc_semaphore(name=f"d_{q}_{i}") for i in range(n)] for q, n in NDMA.items()}
        dma_cnt = {q: [0] * n for q, n in NDMA.items()}
        dma_i = {q: 0 for q in NDMA}
        last_of = {}
        for i, op in enumerate(ops):
            last_of[op["eng"]] = i
        for e, i in last_of.items():
            ops[i]["signal"] = True
        for op in ops:
            if op["is_dma"]:
                q = op["eng"]
                k = dma_i[q] % NDMA[q]
                dma_i[q] += 1
                dma_cnt[q][k] += 1
                op["sem"] = dma_sems[q][k]
                op["semkey"] = ("dma", q, k)
                op["val"] = 16 * dma_cnt[q][k]
            elif op["signal"]:
                e = op["eng"]
                if cnt[e] >= self.SEM_ROT:
                    sems[e].append(nc.alloc_semaphore(name=f"s_{e}_{len(sems[e])}"))
                    cnt[e] = 0
                cnt[e] += 1
                op["sem"] = sems[e][-1]
                op["semkey"] = (e, len(sems[e]) - 1)
                op["val"] = cnt[e]
        by_eng = {e: [] for e in ENG_ATTR}
        for i, op in enumerate(ops):
            by_eng[op["eng"]].append(i)
        fw_ = {}
        for op in ops:
            if op["is_dma"] or op["signal"]:
                fw_[op["semkey"]] = (op["sem"], op["val"], op["semkey"])
        final_waits = list(fw_.values())

        def emit_engine(ename, eobj):
            waited = {}

            def wait(sem, val, key):
                if waited.get(key, 0) >= val:
                    return
                waited[key] = val
                eobj.wait_ge(sem, val)

            for i in by_eng[ename]:
                op = ops[i]
                for d in sorted(op["deps"]):
                    po = ops[d]
                    wait(po["sem"], po["val"], po["semkey"])
                if op["is_dma"] and op["val"] > 16:
                    wait(op["sem"], op["val"] - 16, op["semkey"])
                ins = op["emit"](eobj)
                if op["signal"]:
                    ins.then_inc(op["sem"], 16 if op["is_dma"] else 1)
            if ename == "sp":
                for sem, val, key in final_waits:
                    wait(sem, val, key)

        with nc.Block() as block:
            for ename, attr in ENG_ATTR.items():
                if not by_eng[ename] and ename != "sp":
                    continue

                def mk(ename):
                    def f(eobj):
                        emit_engine(ename, eobj)
                    return f
                getattr(block, attr)(mk(ename))
        return nc


D = 1024
S = 2048
CL = 256
T = S + CL
NT = T // 128
W = 512
ALPHA = 4.0 ** 0.25
EPS = 1e-5
LAM_INIT = 0.8 - 0.6 * math.exp(-0.3)
NTILES = [(0, 512), (512, 512), (1024, 512), (1536, 512), (2048, 256)]


def build(NB, debug=False, stage=99):
    P = Prog()
    x_d = P.dram_in("x", [NB * S, D])
    ctx_d = P.dram_in("ctx", [NB * CL, D])
    cond_d = P.dram_in("cond", [128, D])
    rows_d = P.dram_in("rows", [128, 128])
    wmod_d = [P.dram_in(f"w_mod{l}", [D, 3 * D]) for l in range(2)]
    abin_d = P.dram_in("ab_w_in", [D, 5 * W])
    about_d = P.dram_in("ab_w_out", [D, D])
    cdin_d = P.dram_in("cd_w_in", [D, 8 * W])
    cdout_d = P.dram_in("cd_w_out", [D, D])
    qksw_d = P.dram_in("w_qk_sw", [D, 2 * W])
    wbd_d = P.dram_in("lru_wbd", [128, 16, 128])
    wsT_d = P.dram_in("sgu_wT", [128, 4, 128])
    sgub_d = P.dram_in("sgu_b", [1, 4, 128])
    bcln_d = P.dram_in("bc_ln", [128, 4, D])
    bcsgu_d = P.dram_in("bc_sgu", [128, 2, W])
    bcmisc_d = P.dram_in("bc_misc", [128, 384])
    ident_d = P.dram_in("ident", [128, 128])
    rope_d = P.dram_in("rope", [128, 2, S])
    out_d = P.dram_out("out", [NB * S, D])
    x1s_d = P.dram_tmp("x1s", [NB * S, D])
    P.base[x1s_d.name] = ("DR_x1s", 0)
    dbg = {}
    if debug:
        dbg["x1"] = P.dram_out("dbg_x1", [S, D])
        dbg["c1"] = P.dram_out("dbg_c1", [CL, D])
        dbg["mod"] = P.dram_out("dbg_mod", [128, 2, 192])

    ident = P.sbuf("ident", [128, 128], F32)
    colsT = P.sbuf("colsT", [128, 128], F32)
    zeros = P.sbuf("zeros", [128, 128], F32)
    ones = P.sbuf("ones", [128, 128], F32)
    modT = [P.sbuf(f"modT{l}", [128, 24, 8], F32) for l in range(2)]
    sgub = P.sbuf("sgub", [128, 4, 128], F32)
    bc_sgu = P.sbuf("bc_sgu", [128, 2, W], F32)
    bc_misc = P.sbuf("bc_misc", [128, 384], F32)
    sm = P.sbuf("sm", [128, 96], F32)

    stt_ = P.sbuf("stt_", [128, 16, 6], F32)
    mv = P.sbuf("mv", [128, 40, 2], F32)
    rs = P.sbuf("rs", [128, 40], F32)
    wsT = P.sbuf("wsT", [128, 4, 128], BF16)
    wbd = P.sbuf("wbd", [128, 16, 128], BF16)
    HT = P.sbuf("HT", [128, 8, T], BF16)
    yk0 = P.sb_off = (P.sb_off + 63) // 64 * 64
    YC = P.sbuf("YC", [128, 8, T], BF16, at=yk0)
    KT = P.sbuf("KT", [128, 4, T], BF16, at=yk0)
    VA = P.sbuf("VA", [128, NT, 4, 132], BF16, at=yk0 + 4 * T * 2 + 64)
    P.sb_off = yk0 + 4 * T * 2 + 64 + NT * 4 * 132 * 2
    w0 = P.sb_off = (P.sb_off + 63) // 64 * 64
    WQ = [P.sbuf(f"WQ{i}", [128, 8, 512], BF16, at=w0 + i * 8192) for i in range(4)]
    WS = [P.sbuf(f"WS{i}", [128, 8, 128], BF16, at=w0 + 3 * 8192 + i * 2048) for i in range(4)]
    WO = P.sbuf("WO", [128, 8, 1024], BF16, at=w0 + 16384)
    P.sb_off = w0 + 32768
    s0 = P.sb_off = (P.sb_off + 63) // 64 * 64
    TB = 4 * T
    XR = P.sbuf("XR", [128, T], F32, at=s0)
    XC = P.sbuf("XC", [128, T], F32, at=s0 + TB)
    BA = P.sbuf("BA", [128, T], F32, at=s0 + 2 * TB)
    BS = P.sbuf("BS", [128, T], F32, at=s0 + 3 * TB)
    H0 = P.sbuf("H0", [128, T], F32, at=s0 + 4 * TB)
    H1 = P.sbuf("H1", [128, T], F32, at=s0 + 5 * TB)
    XCB = P.sbuf("XCB", [128, T], BF16, at=s0 + 6 * TB)
    SEND = s0 + 6 * TB + 2 * T
    GVB = P.sbuf("GVB", [128, NT, W], BF16, at=s0)
    UGT = P.sbuf("UGT", [128, 4, T], BF16, at=s0 + NT * W * 2)
    GBC = P.sbuf("GBC", [128, D], F32, at=s0)
    LNB = P.sbuf("LNB", [128, 2, D], F32, at=s0 + 4096)
    GBX = P.sbuf("GBX", [128, D], F32, at=s0 + 12288)
    l1 = s0 + 12288
    YC1 = P.sbuf("YC1", [128, 4, S], BF16, at=l1)
    YD1 = P.sbuf("YD1", [128, 4, 512], BF16, at=l1 + 16384)
    QZ = P.sbuf("QZ", [128, 2, 4, 512], BF16, at=l1 + 20480)
    SG0 = P.sbuf("SG0", [128, 4, 512], BF16, at=l1 + 28672)
    OM = P.sbuf("OM", [128, 2, 4, 132], F32, at=l1 + 32768)
    OA = P.sbuf("OA", [128, 16, 128], F32, at=l1 + 32768 + 4224)
    ET = [P.sbuf(f"ET{i}", [128, 512], BF16, at=l1 + 32768 + 4224 + 8192 + i * 1024) for i in range(3)]
    PRD = P.sbuf("PRD", [128, S], F32, at=l1 + 16384)
    CV = P.sbuf("CV", [128, S], F32, at=l1 + 16384 + 8192)
    l1end = l1 + 32768 + 4224 + 8192 + 3072
    assert l1 + 16384 + 16384 <= l1end + 4096
    P.sb_off = max(SEND, l1end, l1 + 32768)
    SG1 = P.sbuf("SG1", [128, 4, 512], BF16)
    SGS = [SG0, SG1]
    NTMP = 5
    TMP = [P.sbuf(f"TMP{i}", [128, 512], F32) for i in range(NTMP)]
    XIN = [P.sbuf(f"XIN{i}", [128, D], F32) for i in range(3)]
    RPS = P.sbuf("RPS", [128, 2, 512], F32, at=P.base[bc_sgu.name][1])
    print("SBUF end", P.sb_off, "top", P.sb_top)
    assert P.sb_off <= P.sb_top
    ps = [P.psum(f"ps{i}", [128, 512]) for i in range(8)]
    tctr = [0]

    def tmp():
        tctr[0] += 1
        return TMP[tctr[0] % NTMP]

    xctr = [0]

    def xin():
        xctr[0] += 1
        return XIN[xctr[0] % 3]

    def wview(dh, c0, n):
        return dh.ap().rearrange("(kc p) n -> p kc n", p=128)[:, :, c0:c0 + n]

    P.dma(ident[:], ident_d.ap())
    P.dma(colsT[:], rows_d.ap())
    P.dma(sgub[0:1], sgub_d.ap())
    P.dma(bc_misc[:], bcmisc_d.ap())
    P.dma(wsT[:], wsT_d.ap(), q="pool")
    P.dma(wbd[:], wbd_d.ap(), q="pool")
    P.memset(zeros[:], 0.0)
    P.memset(ones[:], 1.0)
    P.tr(ps[0][:, 0:128], colsT[:], ident[:])
    P.copy(colsT[:], ps[0][:, 0:128])
    C_CONVW = lambda k, cc: k * 4 + cc
    C_CONVB = lambda cc: 16 + cc
    C_BA = lambda d, cc: 20 + d * 4 + cc
    C_BX = lambda d, cc: 28 + d * 4 + cc
    C_LAM = lambda d, cc: 36 + d * 4 + cc
    C_SCW = lambda k, cc: 44 + k * 4 + cc
    C_BMOD = lambda l, j: 56 + l * 24 + j
    col = lambda i: colsT[:, i:i + 1]
    for i in range(8):
        P.ts(sm[:, i:i + 1], col(20 + i), 0.5, None, ALU.mult)
        P.ts(sm[:, 8 + i:9 + i], col(28 + i), 0.5, None, ALU.mult)
    zc = sm[:, 32:40]
    P.act(zc, colsT[:, 36:44], AF.Exp, scale=-1.0)
    t1 = sm[:, 40:48]
    P.ts(t1, zc, 1.0 / 3.0, -0.5, ALU.mult, ALU.add)
    P.tt(t1, t1, zc, ALU.mult)
    P.ts(t1, t1, 1.0, None, ALU.add)
    P.tt(t1, t1, zc, ALU.mult)
    P.ts(sm[:, 16:24], t1, -8.0, None, ALU.mult)
    P.ts(sm[:, 24:32], t1, -4.0, None, ALU.mult)
    lt = TMP[0]
    P.tt(lt[:, 0:64], bc_misc[:, 0:64], bc_misc[:, 64:128], ALU.mult)
    P.tt(lt[:, 64:128], bc_misc[:, 128:192], bc_misc[:, 192:256], ALU.mult)
    P.add("dve", lambda e: e.reduce_sum(sm[:, 48:49], lt[:, 0:64], AX.X), reads=[lt[:, 0:64]], writes=[sm[:, 48:49]])
    P.add("dve", lambda e: e.reduce_sum(sm[:, 49:50], lt[:, 64:128], AX.X), reads=[lt[:, 64:128]], writes=[sm[:, 49:50]])
    P.act(sm[:, 50:52], sm[:, 48:50], AF.Exp)
    P.tt(sm[:, 52:53], sm[:, 50:51], sm[:, 51:52], ALU.subtract)
    P.ts(sm[:, 52:53], sm[:, 52:53], LAM_INIT, None, ALU.add)
    lamcol = sm[:, 52:53]
    mh = P.sbuf("mh", [128, 40], F32)
    P.memset(mh[:], -0.5)

    def rsqrt_cols(out, in_, eps):
        n = in_.shape[-1]
        P.ts(out, in_, eps, None, ALU.add)
        P.add("pool", lambda e: e.tensor_tensor(out, out, mh[:, 0:n], ALU.pow),
              reads=[out, mh[:, 0:n]], writes=[out])

    P.dma(XIN[0][:], cond_d.ap())
    sTf = P.sbuf("sTf", [128, 8, 128], BF16, at=P.base[HT.name][1])
    for kc in range(8):
        pb = ps[1 + kc // 4]
        P.tr(pb[:, (kc % 4) * 128:(kc % 4 + 1) * 128], XIN[0][:, kc * 128:(kc + 1) * 128], ident[:])
    for hh in range(2):
        th = TMP[2 + hh]
        P.act(th[:], ps[1 + hh][:], AF.Tanh, scale=0.5)
        P.stt(th[:], th[:], 1.0, ps[1 + hh][:], ALU.add, ALU.mult)
        P.ts(sTf[:, hh * 4:(hh + 1) * 4, :].rearrange("p a b -> p (a b)"), th[:], 0.5, None, ALU.mult)
    for l in range(2):
        for blk in range(6):
            wq = WQ[blk % 2]
            P.dma(wq[:], wview(wmod_d[l], blk * 512, 512), q="pool")
            for jj in range(4):
                j = blk * 4 + jj
                pm = ps[3 + j % 2]
                for kc in range(8):
                    P.mm(pm[:, 0:128], wq[:, kc, jj * 128:(jj + 1) * 128], sTf[:, kc, :],
                         start=(kc == 0), stop=(kc == 7))
                P.ts(modT[l][:, j, :], pm[:, 0:8], col(C_BMOD(l, j)), None, ALU.add)
        P.ts(modT[l][:, 8:16, :], modT[l][:, 8:16, :], 1.0, None, ALU.add)

    def gate_bcast(dst, l, r):
        for kc in range(8):
            rep = tmp()
            P.act(rep[:, 0:128], zeros[:], AF.Identity, bias=modT[l][:, 16 + kc, r:r + 1], scale=0.0)
            pb = ps[3 + kc // 4]
            P.mm(pb[:, (kc % 4) * 128:(kc % 4 + 1) * 128], rep[:, 0:128], ident[:])
        P.copy(dst[:, 0:512], ps[3][:], eng="act")
        P.copy(dst[:, 512:1024], ps[4][:], eng="act")

    def load_xtile(b, t, dst):
        if t < 2:
            P.dma(dst[:], ctx_d[b * CL + t * 128: b * CL + (t + 1) * 128, :])
        else:
            P.dma(dst[:], x_d[b * S + (t - 2) * 128: b * S + (t - 1) * 128, :])

    def mod_transpose(src, l, r, tok0):
        for kc in range(8):
            pb = ps[5 + kc // 4]
            P.tr(pb[:, (kc % 4) * 128:(kc % 4 + 1) * 128], src[:, kc * 128:(kc + 1) * 128], ident[:])
        for kc in range(8):
            pb = ps[5 + kc // 4]
            P.act(HT[:, kc, tok0:tok0 + 128], pb[:, (kc % 4) * 128:(kc % 4 + 1) * 128], AF.Identity,
                  bias=modT[l][:, kc, r:r + 1], scale=modT[l][:, 8 + kc, r:r + 1])

    def ln_stats(z, tag_i):
        P.bn_stats(stt_[:, 0, :], z[:, 0:512])
        P.bn_stats(stt_[:, 1, :], z[:, 512:1024])
        P.bn_aggr(mv[:, tag_i, :], stt_[:, 0:2, :].rearrange("p a b -> p (a b)"))
        rsqrt_cols(rs[:, tag_i:tag_i + 1], mv[:, tag_i, 1:2], EPS)

    def ln_apply(z, tag_i, gb, out_ap):
        P.ts(z[:], z[:], mv[:, tag_i, 0:1], rs[:, tag_i:tag_i + 1], ALU.subtract, ALU.mult)
        P.tt(z[:], z[:], gb[:, 0, :], ALU.mult)
        P.tt(out_ap, z[:], gb[:, 1, :], ALU.add)

    def ln_tile(z, tag_i, gb, out_ap):
        ln_stats(z, tag_i)
        ln_apply(z, tag_i, gb, out_ap)

    for b in range(NB):
        for grp in ([0, 1], [2, 3, 4, 5], [6, 7, 8, 9], [10, 11, 12, 13], [14, 15, 16, 17]):
            for j, t in enumerate(grp):
                xt = xin()
                load_xtile(b, t, xt)
                for kc in range(8):
                    P.tr(ps[kc][:, j * 128:(j + 1) * 128], xt[:, kc * 128:(kc + 1) * 128], ident[:])
            r_ = 4 if grp[0] < 2 else b
            n_ = len(grp) * 128
            for kc in range(8):
                P.act(HT[:, kc, grp[0] * 128:grp[0] * 128 + n_], ps[kc][:, 0:n_], AF.Identity,
                      bias=modT[0][:, kc, r_:r_ + 1], scale=modT[0][:, 8 + kc, r_:r_ + 1])
        if stage <= 1:
            return P.finalize()
        for cc in range(4):
            wx, wg = WS[(2 * cc) % 4], WS[(2 * cc + 1) % 4]
            P.dma(wx[:], wview(abin_d, cc * 128, 128), q="pool")
            P.dma(wg[:], wview(abin_d, W + cc * 128, 128), q="pool")
            for (n0, n) in NTILES:
                pb = ps[0 + (n0 // 512) % 2]
                for kc in range(8):
                    P.mm(pb[:, 0:n], wx[:, kc, :], HT[:, kc, n0:n0 + n], start=(kc == 0), stop=(kc == 7))
                P.copy(XR[:, n0:n0 + n], pb[:, 0:n], eng="act")
            for (a0, a1) in ((0, CL), (CL, T)):
                P.ts(XC[:, a0:a1], XR[:, a0:a1], col(C_CONVW(2, cc)), col(C_CONVB(cc)), ALU.mult, ALU.add)
                P.stt(XC[:, a0 + 2:a1], XR[:, a0:a1 - 2], col(C_CONVW(0, cc)), XC[:, a0 + 2:a1], ALU.mult, ALU.add)
                P.stt(XC[:, a0 + 1:a1], XR[:, a0:a1 - 1], col(C_CONVW(1, cc)), XC[:, a0 + 1:a1], ALU.mult, ALU.add)
                P.stt(XC[:, a0:a1 - 1], XR[:, a0 + 1:a1], col(C_CONVW(3, cc)), XC[:, a0:a1 - 1], ALU.mult, ALU.add)
            P.copy(XCB[:], XC[:], eng="act")
            for d in range(2):
                Hd = H0 if d == 0 else H1
                ia = 0 * 8 + d * 4 + cc
                ix = 1 * 8 + d * 4 + cc
                k8 = d * 4 + cc
                for (n0, n) in NTILES:
                    pr = ps[2]
                    pi = ps[3]
                    P.mm(pr[:, 0:n], wbd[:, ia, :], XCB[:, n0:n0 + n])
                    P.mm(pi[:, 0:n], wbd[:, ix, :], XCB[:, n0:n0 + n])
                    thr = tmp()
                    P.act(thr[:, 0:n], pr[:, 0:n], AF.Tanh, bias=sm[:, k8:k8 + 1], scale=0.5)
                    P.act(BA[:, n0:n0 + n], thr[:, 0:n], AF.Exp, bias=sm[:, 24 + k8:25 + k8], scale=sm[:, 24 + k8:25 + k8])
                    P.act(BS[:, n0:n0 + n], thr[:, 0:n], AF.Exp, bias=sm[:, 16 + k8:17 + k8], scale=sm[:, 16 + k8:17 + k8])
                    P.act(XR[:, n0:n0 + n], pi[:, 0:n], AF.Tanh, bias=sm[:, 8 + k8:9 + k8], scale=0.5)
                P.stt(XR[:], XR[:], 1.0, XC[:], ALU.add, ALU.mult)
                P.act(BS[:], BS[:], AF.Sqrt, bias=1.0, scale=-1.0)
                P.stt(BS[:], XR[:], 0.5, BS[:], ALU.mult, ALU.mult)
                if d == 0:
                    P.scan(Hd[:], BA[:], BS[:], 0.0)
                else:
                    P.scan(Hd[:, 0:CL][:, ::-1], BA[:, 0:CL][:, ::-1], BS[:, 0:CL][:, ::-1], 0.0)
                    P.scan(Hd[:, CL:T][:, ::-1], BA[:, CL:T][:, ::-1], BS[:, CL:T][:, ::-1], Hd[:, 0:1])
            P.tt(H0[:], H0[:], H1[:], ALU.add)
            for (n0, n) in NTILES:
                pb = ps[0 + (n0 // 512) % 2]
                for kc in range(8):
                    P.mm(pb[:, 0:n], wg[:, kc, :], HT[:, kc, n0:n0 + n], start=(kc == 0), stop=(kc == 7))
                thg = tmp()
                P.act(thg[:, 0:n], pb[:, 0:n], AF.Tanh, scale=0.5)
                P.stt(thg[:, 0:n], thg[:, 0:n], 1.0, pb[:, 0:n], ALU.add, ALU.mult)
                P.stt(YC[:, cc, n0:n0 + n], thg[:, 0:n], 0.5, H0[:, n0:n0 + n], ALU.mult, ALU.mult)
        if stage <= 2:
            return P.finalize()
        wU, wV, wG = WQ[0], WQ[1], WQ[2]
        P.dma(bc_sgu[:], bcsgu_d.ap())
        P.dma(wU[:], wview(abin_d, 2 * W, W), q="pool")
        P.dma(wV[:], wview(abin_d, 3 * W, W), q="pool")
        P.dma(wG[:], wview(abin_d, 4 * W, W), q="pool")
        for (n0, n) in NTILES:
            for g in range(4):
                pu, pg = ps[0 + g % 2], ps[2 + g % 2]
                for kc in range(8):
                    P.mm(pu[:, 0:n], wU[:, kc, g * 128:(g + 1) * 128], HT[:, kc, n0:n0 + n], start=(kc == 0), stop=(kc == 7))
                for kc in range(8):
                    P.mm(pg[:, 0:n], wG[:, kc, g * 128:(g + 1) * 128], HT[:, kc, n0:n0 + n], start=(kc == 0), stop=(kc == 7))
                gu = tmp()
                P.act(gu[:, 0:n], pu[:, 0:n], AF.Gelu_apprx_tanh)
                tg = tmp()
                P.act(tg[:, 0:n], pg[:, 0:n], AF.Tanh, scale=0.5)
                P.stt(tg[:, 0:n], tg[:, 0:n], 1.0, pg[:, 0:n], ALU.add, ALU.mult)
                P.stt(UGT[:, g, n0:n0 + n], tg[:, 0:n], 0.5, gu[:, 0:n], ALU.mult, ALU.mult)
        for t in range(NT):
            pv = ps[4 + t % 2]
            for kc in range(8):
                P.mm(pv[:], HT[:, kc, t * 128:(t + 1) * 128], wV[:, kc, :], start=(kc == 0), stop=(kc == 7))
            gv = tmp()
            P.act(gv[:], pv[:], AF.Gelu_apprx_tanh)
            P.bn_stats(stt_[:, 2 + t % 2, :], gv[:])
            P.bn_aggr(mv[:, t, :], stt_[:, 2 + t % 2, :])
            P.copy(GVB[:, t, :], gv[:], eng="act")
        rsqrt_cols(rs[:, 0:NT], mv[:, 0:NT, 1], EPS)
        for t in range(NT):
            vn = tmp()
            P.ts(vn[:], GVB[:, t, :], mv[:, t, 0:1], rs[:, t:t + 1], ALU.subtract, ALU.mult)
            P.tt(vn[:], vn[:], bc_sgu[:, 0, :], ALU.mult)
            P.tt(GVB[:, t, :], vn[:], bc_sgu[:, 1, :], ALU.add)
            pS = ps[6 + t % 2]
            for g in range(4):
                P.mm(pS[:, g * 128:(g + 1) * 128], GVB[:, t, g * 128:(g + 1) * 128], wsT[:, g, :], start=True, stop=False)
                P.mm(pS[:, g * 128:(g + 1) * 128], ones[0:1, :], sgub[0:1, g, :], start=False, stop=True)
            P.tt(YC[:, 4:8, t * 128:(t + 1) * 128], pS[:].rearrange("p (g n) -> p g n", g=4),
                 UGT[:, :, t * 128:(t + 1) * 128], ALU.mult)
        if stage <= 3:
            return P.finalize()
        for hf in range(2):
            P.dma(WO[:, :, hf * 512:(hf + 1) * 512], wview(about_d, hf * 512, 512), q="pool")
        P.dma(LNB[:], bcln_d[:, 0:2, :])
        gate_bcast(GBC, 0, b)
        gate_bcast(GBX, 0, 4)
        xts = {}

        def d_mm(t):
            py0, py1 = ps[0 + 2 * (t % 2)], ps[1 + 2 * (t % 2)]
            for hf, py in ((0, py0), (1, py1)):
                for kc in range(8):
                    P.mm(py[:], YC[:, kc, t * 128:(t + 1) * 128], WO[:, kc, hf * 512:(hf + 1) * 512], start=(kc == 0), stop=(kc == 7))

        def d_A(t):
            py0, py1 = ps[0 + 2 * (t % 2)], ps[1 + 2 * (t % 2)]
            xt = xts[t] = xin()
            load_xtile(b, t, xt)
            gb = GBX if t < 2 else GBC
            gy = tmp()
            for hf, py in ((0, py0), (1, py1)):
                P.tt(gy[:], py[:], gb[:, hf * 512:(hf + 1) * 512], ALU.mult)
                P.stt(xt[:, hf * 512:(hf + 1) * 512], xt[:, hf * 512:(hf + 1) * 512], ALPHA, gy[:], ALU.mult, ALU.add)
            ln_stats(xt, 20 + t % 8)

        def d_B(t):
            xt = xts[t]
            ln_apply(xt, 20 + t % 8, LNB, xt[:])
            if t >= 2:
                r0 = b * S + (t - 2) * 128
                P.dma(x1s_d[r0:r0 + 128, :], xt[:])
                if debug and b == 0:
                    P.dma(dbg["x1"][(t - 2) * 128:(t - 1) * 128, :], xt[:], is_out=True)
            elif debug and b == 0:
                P.dma(dbg["c1"][t * 128:(t + 1) * 128, :], xt[:], is_out=True)
            mod_transpose(xt, 1, 4 if t < 2 else b, t * 128)

        d_mm(0)
        d_mm(1)
        d_A(0)
        for t in range(NT):
            if t + 2 < NT:
                d_mm(t + 2)
            if t + 1 < NT:
                d_A(t + 1)
            d_B(t)
        if stage <= 4:
            return P.finalize()
        P.dma(LNB[:], bcln_d[:, 2:4, :])
        if stage <= 4.1:
            return P.finalize()
        gate_bcast(GBC, 1, b)
        if stage <= 4.2:
            return P.finalize()
        wK, wKs, wVv = WQ[0], WQ[1], WQ[2]
        P.dma(wK[:], wview(cdin_d, 5 * W, W), q="pool")
        P.dma(wKs[:], wview(qksw_d, W, W), q="pool")
        P.dma(wVv[:], wview(cdin_d, 6 * W, W), q="pool")
        if stage <= 4.25:
            return P.finalize()
        P.memset(VA[:].rearrange("p t h e -> p (t h) e")[:, :, 128:132], 0.0)
        P.memset(VA[:].rearrange("p t h e -> p (t h) e")[:, :, 128:129], 1.0)
        if stage <= 4.3:
            return P.finalize()
        for (n0, n) in NTILES:
            la, lb = max(n0, CL), n0 + n
            ln_ = lb - la
            c0 = la - n0
            P.dma(RPS[:, :, 0:ln_], rope_d[:, :, la - CL:lb - CL])
            for h in range(4):
                pk, pks = ps[0 + h % 2], ps[2 + h % 2]
                for kc in range(8):
                    P.mm(pk[:, 0:n], wK[:, kc, h * 128:(h + 1) * 128], HT[:, kc, n0:n0 + n], start=(kc == 0), stop=(kc == 7))
                for kc in range(8):
                    P.mm(pks[:, 0:n], wKs[:, kc, h * 128:(h + 1) * 128], HT[:, kc, n0:n0 + n], start=(kc == 0), stop=(kc == 7))
                if n0 == 0:
                    P.copy(KT[:, h, 0:CL], pk[:, 0:CL], eng="act")
                ta, tb = tmp(), tmp()
                P.tt(ta[:, 0:ln_], pk[:, c0:c0 + ln_], RPS[:, 0, 0:ln_], ALU.mult)
                P.tt(tb[:, 0:ln_], pks[:, c0:c0 + ln_], RPS[:, 1, 0:ln_], ALU.mult)
                P.tt(KT[:, h, la:lb], ta[:, 0:ln_], tb[:, 0:ln_], ALU.add)
        if stage <= 4.6:
            return P.finalize()
        for t in range(NT):
            pv = ps[4 + t % 2]
            for kc in range(8):
                P.mm(pv[:], HT[:, kc, t * 128:(t + 1) * 128], wVv[:, kc, :], start=(kc == 0), stop=(kc == 7))
            P.copy(VA[:, t, :, 0:128], pv[:].rearrange("p (h e) -> p h e", h=4), eng="act")
        if stage <= 5:
            return P.finalize()
        for cc in range(4):
            for i in range(4):
                P.dma(WS[i][:], wview(cdin_d, i * W + cc * 128, 128), q="pool")
            for nb in range(4):
                tk = CL + nb * 512
                ph, pc_ = ps[0], ps[1]
                for kc in range(8):
                    P.mm(ph[:], WS[0][:, kc, :], HT[:, kc, tk:tk + 512], start=(kc == 0), stop=(kc == 7))
                for kc in range(8):
                    P.mm(pc_[:], WS[2][:, kc, :], HT[:, kc, tk:tk + 512], start=(kc == 0), stop=(kc == 7))
                hs = tmp()
                P.copy(hs[:], ph[:], eng="act")
                P.tt(PRD[:, nb * 512:(nb + 1) * 512], pc_[:], hs[:], ALU.mult)
            P.ts(CV[:], PRD[:], col(C_SCW(1, cc)), None, ALU.mult)
            P.stt(CV[:, 1:S], PRD[:, 0:S - 1], col(C_SCW(0, cc)), CV[:, 1:S], ALU.mult, ALU.add)
            P.stt(CV[:, 0:S - 1], PRD[:, 1:S], col(C_SCW(2, cc)), CV[:, 0:S - 1], ALU.mult, ALU.add)
            for nb in range(4):
                tk = CL + nb * 512
                pB, pg = ps[2], ps[3]
                for kc in range(8):
                    P.mm(pB[:], WS[1][:, kc, :], HT[:, kc, tk:tk + 512], start=(kc == 0), stop=(kc == 7))
                for kc in range(8):
                    P.mm(pg[:], WS[3][:, kc, :], HT[:, kc, tk:tk + 512], start=(kc == 0), stop=(kc == 7))
                tg, t2 = tmp(), tmp()
                P.act(tg[:], pg[:], AF.Tanh, scale=0.5)
                P.stt(tg[:], tg[:], 1.0, pg[:], ALU.add, ALU.mult)
                P.tt(t2[:], pB[:], CV[:, nb * 512:(nb + 1) * 512], ALU.mult)
                P.stt(YC1[:, cc, nb * 512:(nb + 1) * 512], tg[:], 0.5, t2[:], ALU.mult, ALU.mult)
        if stage <= 6:
            return P.finalize()
        wQ, wQs, wGa = WQ[0], WQ[1], WQ[2]
        P.dma(wQ[:], wview(cdin_d, 4 * W, W), q="pool")
        P.dma(wQs[:], wview(qksw_d, 0, W), q="pool")
        OSC = 0.5 * (1.0 - LAM_INIT)

        def emit_out_tile(qb, qt, b=b):
            t = qb * 4 + qt
            py0, py1 = ps[2], ps[3]
            for hf, py in ((0, py0), (1, py1)):
                for kc in range(8):
                    lhs = YC1[:, kc, t * 128:(t + 1) * 128] if kc < 4 else YD1[:, kc - 4, qt * 128:(qt + 1) * 128]
                    P.mm(py[:], lhs, WO[:, kc, hf * 512:(hf + 1) * 512], start=(kc == 0), stop=(kc == 7))
            xt = xin()
            r0 = b * S + t * 128
            P.dma(xt[:], x1s_d[r0:r0 + 128, :])
            gy = tmp()
            for hf, py in ((0, py0), (1, py1)):
                P.tt(gy[:], py[:], GBC[:, hf * 512:(hf + 1) * 512], ALU.mult)
                P.stt(xt[:, hf * 512:(hf + 1) * 512], xt[:, hf * 512:(hf + 1) * 512], ALPHA, gy[:], ALU.mult, ALU.add)
            ln_tile(xt, 30 + t % 8, LNB, xt[:])
            P.dma(out_d[r0:r0 + 128, :], xt[:], is_out=True)

        pending = []
        def qphase(qb):
            SG = SGS[qb % 2]
            tk = CL + qb * 512
            if qb == 0:
                P.memset(QZ[:].rearrange("p m h n -> p (m h n)"), 0.0)
            P.dma(wGa[:], wview(cdin_d, 7 * W, W), q="pool")
            P.dma(RPS[:], rope_d[:, :, qb * 512:(qb + 1) * 512])
            for h in range(4):
                pq, pqs, pg = ps[(3 * h) % 4], ps[(3 * h + 1) % 4], ps[(3 * h + 2) % 4]
                for kc in range(8):
                    P.mm(pq[:], wQ[:, kc, h * 128:(h + 1) * 128], HT[:, kc, tk:tk + 512], start=(kc == 0), stop=(kc == 7))
                for kc in range(8):
                    P.mm(pqs[:], wQs[:, kc, h * 128:(h + 1) * 128], HT[:, kc, tk:tk + 512], start=(kc == 0), stop=(kc == 7))
                for kc in range(8):
                    P.mm(pg[:], wGa[:, kc, h * 128:(h + 1) * 128], HT[:, kc, tk:tk + 512], start=(kc == 0), stop=(kc == 7))
                ta, tb, tg = tmp(), tmp(), tmp()
                P.tt(ta[:], pq[:], RPS[:, 0, :], ALU.mult)
                P.tt(tb[:], pqs[:], RPS[:, 1, :], ALU.mult)
                P.tt(QZ[0:64, 0, h, :], ta[0:64, :], tb[0:64, :], ALU.add)
                P.tt(QZ[64:128, 1, h, :], ta[64:128, :], tb[64:128, :], ALU.add)
                P.act(tg[:], pg[:], AF.Tanh, scale=0.5)
                P.stt(SG[:, h, :], tg[:], 1.0, pg[:], ALU.add, ALU.mult)

        qphase(0)
        for qb in range(4):
            SG = SGS[qb % 2]
            for hf in range(2):
                P.dma(WO[:, :, hf * 512:(hf + 1) * 512], wview(cdout_d, hf * 512, 512), q="pool")
            seq = [(h, m, kt) for h in range(4) for m in range(2) for kt in range(NT)]

            def score(i):
                h, m, kt = seq[i]
                pS_ = ps[i % 2]
                P.mm(pS_[:], KT[:, h, kt * 128:(kt + 1) * 128], QZ[:, m, h, :])
                P.act(ET[i % 3][:], pS_[:], AF.Exp, scale=0.125)

            def pv(i):
                h, m, kt = seq[i]
                e_ = ET[i % 3]
                for qt in range(4):
                    P.mm(ps[4 + qt][:, 0:130], e_[:, qt * 128:(qt + 1) * 128], VA[:, kt, h, 0:130],
                         start=(kt == 0), stop=(kt == NT - 1))
                if kt == NT - 1:
                    for qt in range(4):
                        P.copy(OM[:, m, qt, 0:129], ps[4 + qt][:, 0:129], eng="dve")
                    if pending and (h * 2 + m) >= 1:
                        emit_out_tile(*pending.pop(0))
                    if m == 1:
                        for qt in range(4):
                            sl = qt * 4 + h
                            c_ = 56 + 2 * (sl % 8)
                            r0_, r1_ = sm[:, c_:c_ + 1], sm[:, c_ + 1:c_ + 2]
                            P.recip(r0_, OM[:, 0, qt, 128:129])
                            P.recip(r1_, OM[:, 1, qt, 128:129])
                            P.ts(r1_, r1_, lamcol, None, ALU.mult)
                            t_ = tmp()
                            P.ts(t_[:, 0:128], OM[:, 1, qt, 0:128], r1_, None, ALU.mult)
                            P.stt(OA[:, sl, :], OM[:, 0, qt, 0:128], r0_, t_[:, 0:128], ALU.mult, ALU.subtract)
                            P.bn_stats(stt_[:, 4 + sl % 8, :], OA[:, sl, :])
                            P.bn_aggr(mv[:, sl, :], stt_[:, 4 + sl % 8, :])
                            P.stt(rs[:, sl:sl + 1], mv[:, sl, 0:1], mv[:, sl, 0:1], mv[:, sl, 1:2], ALU.mult, ALU.add)

            score(0)
            score(1)
            for i in range(len(seq)):
                if i + 2 < len(seq):
                    score(i + 2)
                    if i + 3 == len(seq) and qb + 1 < 4:
                        qphase(qb + 1)
                pv(i)
            rsqrt_cols(rs[:, 0:16], rs[:, 0:16], EPS)
            for h in range(4):
                pt = ps[2 + h % 2]
                for qt in range(4):
                    sl = qt * 4 + h
                    on = tmp()
                    P.ts(on[:, 0:128], OA[:, sl, :], rs[:, sl:sl + 1], OSC, ALU.mult, ALU.mult)
                    P.tt(on[:, 0:128], on[:, 0:128], bc_misc[:, 256:384], ALU.mult)
                    P.tr(pt[:, qt * 128:(qt + 1) * 128], on[:, 0:128], ident[:])
                P.tt(YD1[:, h, :], pt[:], SG[:, h, :], ALU.mult)
            pending.extend((qb, qt) for qt in range(4))
        while pending:
            emit_out_tile(*pending.pop(0))
    return P.finalize()


def prep_shared(inp):
    f = np.float32
    sh = {}
    rows = np.zeros((128, 128), f)
    rows[0:16] = inp["lru_conv_w"][0].reshape(4, 4, 128).reshape(16, 128)
    rows[16:20] = inp["lru_conv_b"][0].reshape(4, 128)
    rows[20:28] = inp["lru_b_a"][0].reshape(8, 128)
    rows[28:36] = inp["lru_b_x"][0].reshape(8, 128)
    rows[36:44] = inp["lru_lambda"][0].reshape(8, 128)
    rows[44:56] = inp["sconv_w"][0].reshape(12, 128)
    rows[56:104] = inp["b_mod"].reshape(48, 128)
    sh["rows"] = rows
    sh["w_mod0"] = np.ascontiguousarray(inp["w_mod"][0])
    sh["w_mod1"] = np.ascontiguousarray(inp["w_mod"][1])
    sh["ab_w_in"] = np.ascontiguousarray(inp["ab_w_in"][0])
    sh["ab_w_out"] = np.ascontiguousarray(inp["ab_w_out"][0])
    cd = inp["cd_w_in"][0]
    sh["cd_w_in"] = np.ascontiguousarray(cd)
    sh["cd_w_out"] = np.ascontiguousarray(inp["cd_w_out"][0])
    perm = np.arange(512) ^ 16
    sh["w_qk_sw"] = np.ascontiguousarray(np.concatenate([cd[:, 2048:2560][:, perm], cd[:, 2560:3072][:, perm]], axis=1))
    wbd = np.zeros((128, 16, 128), f)
    for gi, key in enumerate(("lru_w_a", "lru_w_x")):
        w = inp[key][0]
        for d in range(2):
            for cc in range(4):
                idx = gi * 8 + d * 4 + cc
                wbd[0:64, idx, 0:64] = w[d, 2 * cc]
                wbd[64:128, idx, 64:128] = w[d, 2 * cc + 1]
    sh["lru_wbd"] = wbd
    sh["sgu_wT"] = np.ascontiguousarray(np.transpose(inp["sgu_w"][0], (2, 0, 1)))
    sh["sgu_b"] = np.ascontiguousarray(inp["sgu_b"][0][None])
    bc_ln = np.stack([inp["ln_g"][0], inp["ln_b"][0], inp["ln_g"][1], inp["ln_b"][1]], 0)
    sh["bc_ln"] = np.ascontiguousarray(np.broadcast_to(bc_ln[None], (128, 4, 1024)))
    bc_sgu = np.stack([inp["sgu_ln_g"][0], inp["sgu_ln_b"][0]], 0)
    sh["bc_sgu"] = np.ascontiguousarray(np.broadcast_to(bc_sgu[None], (128, 2, 512)))
    misc = np.concatenate([inp["diff_lambda"][0].reshape(256), inp["diff_subln_g"][0].reshape(128)])
    sh["bc_misc"] = np.ascontiguousarray(np.broadcast_to(misc[None], (128, 384)))
    sh["ident"] = np.eye(128, dtype=f)
    t = np.arange(2048)
    row = (t // 64).astype(f)
    colp = (t % 64).astype(f)
    inv = (np.float32(10000.0) ** (-np.arange(16, dtype=f) / np.float32(16))).astype(f)
    ang = np.concatenate([row[:, None] * inv, row[:, None] * inv, colp[:, None] * inv, colp[:, None] * inv], axis=-1).astype(f)
    cos = np.cos(ang).astype(f)
    sin = np.sin(ang).astype(f)
    sgn = np.where((np.arange(64) % 32) < 16, -1.0, 1.0).astype(f)
    sin_s = sin * sgn[None, :]
    rope = np.zeros((128, 2, 2048), f)
    rope[:, 0, :] = np.concatenate([cos.T, cos.T], 0)
    rope[:, 1, :] = np.concatenate([sin_s.T, sin_s.T], 0)
    sh["rope"] = rope
    return sh

def prep_core(inp, sh, b0, NB):
    f = np.float32
    m = dict(sh)
    m["x"] = np.ascontiguousarray(inp["x"][b0:b0 + NB].reshape(NB * 2048, 1024))
    m["ctx"] = np.ascontiguousarray(inp["ctx"][b0:b0 + NB].reshape(NB * 256, 1024))
    cond = np.zeros((128, 1024), f)
    cond[0:NB] = inp["c"][b0:b0 + NB]
    cond[4] = inp["c_ctx"]
    m["cond"] = cond
    return m


NB_PER_CORE = 4
N_CORES = 8


def kernel(**inputs):
    inp = {k: np.asarray(v) for k, v in inputs.items()}
    sh = prep_shared(inp)
    nc = build(NB_PER_CORE)
    in_maps = [prep_core(inp, sh, c * NB_PER_CORE, NB_PER_CORE) for c in range(N_CORES)]
    res = run_bass_kernel_spmd(nc, in_maps, core_ids=list(range(N_CORES)))
    outs = [np.asarray(r["out"]).reshape(NB_PER_CORE, S, D) for r in res.results]
    return np.concatenate(outs, axis=0).astype(np.float32)
```

```python
import math

import numpy as np
import concourse.bass as bass
import concourse.mybir as mybir
from concourse.bass_utils import run_bass_kernel_spmd

F32 = mybir.dt.float32
BF16 = mybir.dt.bfloat16
AF = mybir.ActivationFunctionType
ALU = mybir.AluOpType
AX = mybir.AxisListType

ENG_ATTR = {"pe": "tensor", "act": "scalar", "dve": "vector", "pool": "gpsimd", "sp": "sync"}
COMPUTE = ("pe", "act", "dve", "pool")
_ESZ = {F32: 4, BF16: 2, mybir.dt.int32: 4, mybir.dt.uint32: 4, mybir.dt.float16: 2}


class Prog:
    SEM_ROT = 30000

    def __init__(self):
        self.nc = bass.Bass("TRN2", target_bir_lowering=False)
        self.ops = []
        self.acc = {}
        self.base = {}
        self.sb_off = 16512
        self.sb_top = 229344
        self.n_psum = 0
        self.dma_rr = {"sp": 0, "pool": 0, "act": 0}
        self.out_dmas = []

    def sbuf(self, name, shape, dtype, at=None):
        esz = _ESZ[dtype]
        nbytes = int(np.prod(shape[1:])) * esz
        if at is None:
            at = (self.sb_off + 63) // 64 * 64
            self.sb_off = at + nbytes
            assert self.sb_off <= self.sb_top, (name, self.sb_off)
        t = self.nc.alloc_sbuf_tensor_at(name, list(shape), dtype, offset=at)
        self.base[t.name] = ("SB", at)
        return t

    def psum(self, name, shape, dtype=F32):
        t = self.nc.alloc_psum_tensor(name, list(shape), dtype)
        self.base[t.name] = ("PS_" + name, 0)
        return t

    def bitcast(self, t, dtype):
        v = t.bitcast(dtype)
        self.base[v.name] = self.base[t.name]
        return v

    def dram_in(self, name, shape, dtype=F32):
        return self.nc.dram_tensor(name, list(shape), dtype, kind="ExternalInput")

    def dram_out(self, name, shape, dtype=F32):
        return self.nc.dram_tensor(name, list(shape), dtype, kind="ExternalOutput")

    def dram_tmp(self, name, shape, dtype=F32):
        return self.nc.dram_tensor(name, list(shape), dtype)

    def region(self, ap):
        t = ap.tensor
        key = self.base.get(t.name)
        if key is None:
            return None
        space, b0 = key
        if space.startswith("PS_"):
            return (space, 0, 128, 0, 2048)
        esz = _ESZ[ap.dtype]
        dims = ap.ap
        pstep, pcnt = dims[0]
        off = ap.offset
        if pstep > 0:
            p0 = off // pstep
            f0 = off % pstep
        else:
            p0, f0 = 0, off
        lo = hi = f0
        for step, cnt in dims[1:]:
            if step >= 0:
                hi += step * (cnt - 1)
            else:
                lo += step * (cnt - 1)
        hi += 1
        return (space, p0, p0 + pcnt, b0 + lo * esz, b0 + hi * esz)

    def add(self, eng, emit, reads=(), writes=(), is_dma=False, is_out=False):
        idx = len(self.ops)
        tag = ("dma", idx) if is_dma else eng
        deps = set()
        rr = [self.region(a) for a in reads if a is not None and not isinstance(a, (int, float))]
        wr = [self.region(a) for a in writes]
        rr = [r for r in rr if r is not None]
        wr = [r for r in wr if r is not None]
        wr = wr + [r for r in rr if r[0].startswith("PS_") and r not in wr]
        for r in rr:
            for e in self.acc.get(r[0], ()):
                if e[6] and e[0] < r[2] and r[1] < e[1] and e[2] < r[4] and r[3] < e[3]:
                    deps.add((e[5], "raw"))
        for w in wr:
            for e in self.acc.get(w[0], ()):
                if e[0] < w[2] and w[1] < e[1] and e[2] < w[4] and w[3] < e[3]:
                    deps.add((e[5], "waw" if e[6] else "war"))
        fdeps = set()
        for d, kind in deps:
            po = self.ops[d]
            if po["is_dma"] or is_dma:
                fdeps.add(d)
            elif po["eng"] == eng:
                if eng != "pe":
                    fdeps.add(d)
            else:
                fdeps.add(d)
        op = dict(eng=eng, emit=emit, deps=fdeps, is_dma=is_dma, is_out=is_out, signal=False)
        self.ops.append(op)
        for d in fdeps:
            self.ops[d]["signal"] = True
        for w in wr:
            lst = self.acc.setdefault(w[0], [])
            lst[:] = [e for e in lst if not (w[1] <= e[0] and e[1] <= w[2] and w[3] <= e[2] and e[3] <= w[4])]
            lst.append([w[1], w[2], w[3], w[4], tag, idx, True])
        for r in rr:
            lst = self.acc.setdefault(r[0], [])
            found = False
            for e in lst:
                if (not e[6]) and e[4] == tag and e[0] == r[1] and e[1] == r[2] and e[2] == r[3] and e[3] == r[4]:
                    e[5] = idx
                    found = True
                    break
            if not found:
                lst.append([r[1], r[2], r[3], r[4], tag, idx, False])
        if is_dma:
            op["signal"] = True
        return idx

    def mm(self, out, lhsT, rhs, start=True, stop=True, **kw):
        return self.add("pe", lambda e: e.matmul(out, lhsT, rhs, start=start, stop=stop, **kw),
                        reads=[lhsT, rhs], writes=[out])

    def tr(self, out, in_, ident):
        return self.add("pe", lambda e: e.transpose(out, in_, ident), reads=[in_, ident], writes=[out])

    def act(self, out, in_, func, bias=0.0, scale=1.0, eng="act"):
        rd = [in_]
        if not isinstance(bias, (int, float)):
            rd.append(bias)
        if not isinstance(scale, (int, float)):
            rd.append(scale)
        return self.add(eng, lambda e: e.activation(out, in_, func, bias=bias, scale=scale),
                        reads=rd, writes=[out])

    def tt(self, out, a, b, op, eng="dve"):
        return self.add(eng, lambda e: e.tensor_tensor(out, a, b, op), reads=[a, b], writes=[out])

    def ts(self, out, a, s1, s2, op0, op1=None, eng="dve"):
        rd = [a] + [s for s in (s1, s2) if s is not None and not isinstance(s, (int, float))]
        if op1 is None:
            return self.add(eng, lambda e: e.tensor_scalar(out, a, s1, None, op0), reads=rd, writes=[out])
        return self.add(eng, lambda e: e.tensor_scalar(out, a, s1, s2, op0, op1), reads=rd, writes=[out])

    def stt(self, out, in0, scalar, in1, op0, op1, eng="dve"):
        rd = [in0, in1] + ([] if isinstance(scalar, (int, float)) else [scalar])
        return self.add(eng, lambda e: e.scalar_tensor_tensor(out, in0, scalar, in1, op0, op1),
                        reads=rd, writes=[out])

    def scan(self, out, d0, d1, initial, op0=ALU.mult, op1=ALU.add):
        rd = [d0, d1] + ([] if isinstance(initial, (int, float)) else [initial])
        return self.add("dve", lambda e: e.tensor_tensor_scan(out, d0, d1, initial, op0, op1),
                        reads=rd, writes=[out])

    def copy(self, out, in_, eng="dve"):
        if eng == "act":
            return self.add("act", lambda e: e.copy(out, in_), reads=[in_], writes=[out])
        return self.add(eng, lambda e: e.tensor_copy(out, in_), reads=[in_], writes=[out])

    def memset(self, ap, val, eng="dve"):
        return self.add(eng, lambda e: e.memset(ap, val), writes=[ap])

    def recip(self, out, in_):
        return self.add("dve", lambda e: e.reciprocal(out, in_), reads=[in_], writes=[out])

    def bn_stats(self, out, in_):
        return self.add("dve", lambda e: e.bn_stats(out, in_), reads=[in_], writes=[out])

    def bn_aggr(self, out, in_):
        return self.add("dve", lambda e: e.bn_aggr(out, in_), reads=[in_], writes=[out])

    def dma(self, out, in_, q="sp", is_out=False, **kw):
        if q == "pool":
            kw.setdefault("max_dma_last_dim", 2048)
        return self.add(q, lambda e: e.dma_start(out, in_, **kw), reads=[in_], writes=[out],
                        is_dma=True, is_out=is_out)

    def finalize(self):
        nc = self.nc
        ops = self.ops
        NDMA = {"sp": 10, "pool": 6, "act": 4}
        cnt = {e: 0 for e in COMPUTE}
        sems = {e: [nc.alloc_semaphore(name=f"s_{e}_0")] for e in COMPUTE}
        dma_sems = {q: [nc.alloc_semaphore(name=f"d_{q}_{i}") for i in range(n)] for q, n in NDMA.items()}
        dma_cnt = {q: [0] * n for q, n in NDMA.items()}
        dma_i = {q: 0 for q in NDMA}
        last_of = {}
        for i, op in enumerate(ops):
            last_of[op["eng"]] = i
        for e, i in last_of.items():
            ops[i]["signal"] = True
        for op in ops:
            if op["is_dma"]:
                q = op["eng"]
                k = dma_i[q] % NDMA[q]
                dma_i[q] += 1
                dma_cnt[q][k] += 1
                op["sem"] = dma_sems[q][k]
                op["semkey"] = ("dma", q, k)
                op["val"] = 16 * dma_cnt[q][k]
            elif op["signal"]:
                e = op["eng"]
                if cnt[e] >= self.SEM_ROT:
                    sems[e].append(nc.alloc_semaphore(name=f"s_{e}_{len(sems[e])}"))
                    cnt[e] = 0
                cnt[e] += 1
                op["sem"] = sems[e][-1]
                op["semkey"] = (e, len(sems[e]) - 1)
                op["val"] = cnt[e]
        by_eng = {e: [] for e in ENG_ATTR}
        for i, op in enumerate(ops):
            by_eng[op["eng"]].append(i)
        fw_ = {}
        for op in ops:
            if op["is_dma"] or op["signal"]:
                fw_[op["semkey"]] = (op["sem"], op["val"], op["semkey"])
        final_waits = list(fw_.values())

        def emit_engine(ename, eobj):
            waited = {}

            def wait(sem, val, key):
                if waited.get(key, 0) >= val:
                    return
                waited[key] = val
                eobj.wait_ge(sem, val)

            for i in by_eng[ename]:
                op = ops[i]
                for d in sorted(op["deps"]):
                    po = ops[d]
                    wait(po["sem"], po["val"], po["semkey"])
                if op["is_dma"] and op["val"] > 16:
                    wait(op["sem"], op["val"] - 16, op["semkey"])
                ins = op["emit"](eobj)
                if op["signal"]:
                    ins.then_inc(op["sem"], 16 if op["is_dma"] else 1)
            if ename == "sp":
                for sem, val, key in final_waits:
                    wait(sem, val, key)

        with nc.Block() as block:
            for ename, attr in ENG_ATTR.items():
                if not by_eng[ename] and ename != "sp":
                    continue

                def mk(ename):
                    def f(eobj):
                        emit_engine(ename, eobj)
                    return f
                getattr(block, attr)(mk(ename))
        return nc


D = 1024
S = 2048
CL = 256
T = S + CL
NT = T // 128
W = 512
ALPHA = 4.0 ** 0.25
EPS = 1e-5
LAM_INIT = 0.8 - 0.6 * math.exp(-0.3)
NTILES = [(0, 512), (512, 512), (1024, 512), (1536, 512), (2048, 256)]


def build(NB, debug=False, stage=99):
    P = Prog()
    x_d = P.dram_in("x", [NB * S, D])
    ctx_d = P.dram_in("ctx", [NB * CL, D])
    cond_d = P.dram_in("cond", [128, D])
    rows_d = P.dram_in("rows", [128, 128])
    wmod_d = [P.dram_in(f"w_mod{l}", [D, 3 * D]) for l in range(2)]
    abin_d = P.dram_in("ab_w_in", [D, 5 * W])
    about_d = P.dram_in("ab_w_out", [D, D])
    cdin_d = P.dram_in("cd_w_in", [D, 8 * W])
    cdout_d = P.dram_in("cd_w_out", [D, D])
    qksw_d = P.dram_in("w_qk_sw", [D, 2 * W])
    wbd_d = P.dram_in("lru_wbd", [128, 16, 128])
    wsT_d = P.dram_in("sgu_wT", [128, 4, 128])
    sgub_d = P.dram_in("sgu_b", [1, 4, 128])
    bcln_d = P.dram_in("bc_ln", [128, 4, D])
    bcsgu_d = P.dram_in("bc_sgu", [128, 2, W])
    bcmisc_d = P.dram_in("bc_misc", [128, 384])
    ident_d = P.dram_in("ident", [128, 128])
    rope_d = P.dram_in("rope", [128, 2, S])
    out_d = P.dram_out("out", [NB * S, D])
    x1s_d = P.dram_tmp("x1s", [NB * S, D])
    P.base[x1s_d.name] = ("DR_x1s", 0)
    dbg = {}
    if debug:
        dbg["x1"] = P.dram_out("dbg_x1", [S, D])
        dbg["c1"] = P.dram_out("dbg_c1", [CL, D])
        dbg["mod"] = P.dram_out("dbg_mod", [128, 2, 192])

    ident = P.sbuf("ident", [128, 128], F32)
    colsT = P.sbuf("colsT", [128, 128], F32)
    zeros = P.sbuf("zeros", [128, 128], F32)
    ones = P.sbuf("ones", [128, 128], F32)
    modT = [P.sbuf(f"modT{l}", [128, 24, 8], F32) for l in range(2)]
    sgub = P.sbuf("sgub", [128, 4, 128], F32)
    bc_sgu = P.sbuf("bc_sgu", [128, 2, W], F32)
    bc_misc = P.sbuf("bc_misc", [128, 384], F32)
    sm = P.sbuf("sm", [128, 96], F32)

    stt_ = P.sbuf("stt_", [128, 16, 6], F32)
    mv = P.sbuf("mv", [128, 40, 2], F32)
    rs = P.sbuf("rs", [128, 40], F32)
    wsT = P.sbuf("wsT", [128, 4, 128], BF16)
    wbd = P.sbuf("wbd", [128, 16, 128], BF16)
    HT = P.sbuf("HT", [128, 8, T], BF16)
    yk0 = P.sb_off = (P.sb_off + 63) // 64 * 64
    YC = P.sbuf("YC", [128, 8, T], BF16, at=yk0)
    KT = P.sbuf("KT", [128, 4, T], BF16, at=yk0)
    VA = P.sbuf("VA", [128, NT, 4, 132], BF16, at=yk0 + 4 * T * 2 + 64)
    P.sb_off = yk0 + 4 * T * 2 + 64 + NT * 4 * 132 * 2
    w0 = P.sb_off = (P.sb_off + 63) // 64 * 64
    WQ = [P.sbuf(f"WQ{i}", [128, 8, 512], BF16, at=w0 + i * 8192) for i in range(4)]
    WS = [P.sbuf(f"WS{i}", [128, 8, 128], BF16, at=w0 + 3 * 8192 + i * 2048) for i in range(4)]
    WO = P.sbuf("WO", [128, 8, 1024], BF16, at=w0 + 16384)
    P.sb_off = w0 + 32768
    s0 = P.sb_off = (P.sb_off + 63) // 64 * 64
    TB = 4 * T
    XR = P.sbuf("XR", [128, T], F32, at=s0)
    XC = P.sbuf("XC", [128, T], F32, at=s0 + TB)
    BA = P.sbuf("BA", [128, T], F32, at=s0 + 2 * TB)
    BS = P.sbuf("BS", [128, T], F32, at=s0 + 3 * TB)
    H0 = P.sbuf("H0", [128, T], F32, at=s0 + 4 * TB)
    H1 = P.sbuf("H1", [128, T], F32, at=s0 + 5 * TB)
    XCB = P.sbuf("XCB", [128, T], BF16, at=s0 + 6 * TB)
    SEND = s0 + 6 * TB + 2 * T
    GVB = P.sbuf("GVB", [128, NT, W], BF16, at=s0)
    UGT = P.sbuf("UGT", [128, 4, T], BF16, at=s0 + NT * W * 2)
    GBC = P.sbuf("GBC", [128, D], F32, at=s0)
    LNB = P.sbuf("LNB", [128, 2, D], F32, at=s0 + 4096)
    GBX = P.sbuf("GBX", [128, D], F32, at=s0 + 12288)
    l1 = s0 + 12288
    YC1 = P.sbuf("YC1", [128, 4, S], BF16, at=l1)
    YD1 = P.sbuf("YD1", [128, 4, 512], BF16, at=l1 + 16384)
    QZ = P.sbuf("QZ", [128, 2, 4, 512], BF16, at=l1 + 20480)
    SG0 = P.sbuf("SG0", [128, 4, 512], BF16, at=l1 + 28672)
    OM = P.sbuf("OM", [128, 2, 4, 132], F32, at=l1 + 32768)
    OA = P.sbuf("OA", [128, 16, 128], F32, at=l1 + 32768 + 4224)
    ET = [P.sbuf(f"ET{i}", [128, 512], BF16, at=l1 + 32768 + 4224 + 8192 + i * 1024) for i in range(3)]
    PRD = P.sbuf("PRD", [128, S], F32, at=l1 + 16384)
    CV = P.sbuf("CV", [128, S], F32, at=l1 + 16384 + 8192)
    l1end = l1 + 32768 + 4224 + 8192 + 3072
    assert l1 + 16384 + 16384 <= l1end + 4096
    P.sb_off = max(SEND, l1end, l1 + 32768)
    SG1 = P.sbuf("SG1", [128, 4, 512], BF16)
    SGS = [SG0, SG1]
    NTMP = 5
    TMP = [P.sbuf(f"TMP{i}", [128, 512], F32) for i in range(NTMP)]
    XIN = [P.sbuf(f"XIN{i}", [128, D], F32) for i in range(3)]
    RPS = P.sbuf("RPS", [128, 2, 512], F32, at=P.base[bc_sgu.name][1])
    print("SBUF end", P.sb_off, "top", P.sb_top)
    assert P.sb_off <= P.sb_top
    ps = [P.psum(f"ps{i}", [128, 512]) for i in range(8)]
    tctr = [0]

    def tmp():
        tctr[0] += 1
        return TMP[tctr[0] % NTMP]

    xctr = [0]

    def xin():
        xctr[0] += 1
        return XIN[xctr[0] % 3]

    def wview(dh, c0, n):
        return dh.ap().rearrange("(kc p) n -> p kc n", p=128)[:, :, c0:c0 + n]

    P.dma(ident[:], ident_d.ap())
    P.dma(colsT[:], rows_d.ap())
    P.dma(sgub[0:1], sgub_d.ap())
    P.dma(bc_misc[:], bcmisc_d.ap())
    P.dma(wsT[:], wsT_d.ap(), q="pool")
    P.dma(wbd[:], wbd_d.ap(), q="pool")
    P.memset(zeros[:], 0.0)
    P.memset(ones[:], 1.0)
    P.tr(ps[0][:, 0:128], colsT[:], ident[:])
    P.copy(colsT[:], ps[0][:, 0:128])
    C_CONVW = lambda k, cc: k * 4 + cc
    C_CONVB = lambda cc: 16 + cc
    C_BA = lambda d, cc: 20 + d * 4 + cc
    C_BX = lambda d, cc: 28 + d * 4 + cc
    C_LAM = lambda d, cc: 36 + d * 4 + cc
    C_SCW = lambda k, cc: 44 + k * 4 + cc
    C_BMOD = lambda l, j: 56 + l * 24 + j
    col = lambda i: colsT[:, i:i + 1]
    for i in range(8):
        P.ts(sm[:, i:i + 1], col(20 + i), 0.5, None, ALU.mult)
        P.ts(sm[:, 8 + i:9 + i], col(28 + i), 0.5, None, ALU.mult)
    zc = sm[:, 32:40]
    P.act(zc, colsT[:, 36:44], AF.Exp, scale=-1.0)
    t1 = sm[:, 40:48]
    P.ts(t1, zc, 1.0 / 3.0, -0.5, ALU.mult, ALU.add)
    P.tt(t1, t1, zc, ALU.mult)
    P.ts(t1, t1, 1.0, None, ALU.add)
    P.tt(t1, t1, zc, ALU.mult)
    P.ts(sm[:, 16:24], t1, -8.0, None, ALU.mult)
    P.ts(sm[:, 24:32], t1, -4.0, None, ALU.mult)
    lt = TMP[0]
    P.tt(lt[:, 0:64], bc_misc[:, 0:64], bc_misc[:, 64:128], ALU.mult)
    P.tt(lt[:, 64:128], bc_misc[:, 128:192], bc_misc[:, 192:256], ALU.mult)
    P.add("dve", lambda e: e.reduce_sum(sm[:, 48:49], lt[:, 0:64], AX.X), reads=[lt[:, 0:64]], writes=[sm[:, 48:49]])
    P.add("dve", lambda e: e.reduce_sum(sm[:, 49:50], lt[:, 64:128], AX.X), reads=[lt[:, 64:128]], writes=[sm[:, 49:50]])
    P.act(sm[:, 50:52], sm[:, 48:50], AF.Exp)
    P.tt(sm[:, 52:53], sm[:, 50:51], sm[:, 51:52], ALU.subtract)
    P.ts(sm[:, 52:53], sm[:, 52:53], LAM_INIT, None, ALU.add)
    lamcol = sm[:, 52:53]
    mh = P.sbuf("mh", [128, 40], F32)
    P.memset(mh[:], -0.5)

    def rsqrt_cols(out, in_, eps):
        n = in_.shape[-1]
        P.ts(out, in_, eps, None, ALU.add)
        P.add("pool", lambda e: e.tensor_tensor(out, out, mh[:, 0:n], ALU.pow),
              reads=[out, mh[:, 0:n]], writes=[out])

    P.dma(XIN[0][:], cond_d.ap())
    sTf = P.sbuf("sTf", [128, 8, 128], BF16, at=P.base[HT.name][1])
    for kc in range(8):
        pb = ps[1 + kc // 4]
        P.tr(pb[:, (kc % 4) * 128:(kc % 4 + 1) * 128], XIN[0][:, kc * 128:(kc + 1) * 128], ident[:])
    for hh in range(2):
        th = TMP[2 + hh]
        P.act(th[:], ps[1 + hh][:], AF.Tanh, scale=0.5)
        P.stt(th[:], th[:], 1.0, ps[1 + hh][:], ALU.add, ALU.mult)
        P.ts(sTf[:, hh * 4:(hh + 1) * 4, :].rearrange("p a b -> p (a b)"), th[:], 0.5, None, ALU.mult)
    for l in range(2):
        for blk in range(6):
            wq = WQ[blk % 2]
            P.dma(wq[:], wview(wmod_d[l], blk * 512, 512), q="pool")
            for jj in range(4):
                j = blk * 4 + jj
                pm = ps[3 + j % 2]
                for kc in range(8):
                    P.mm(pm[:, 0:128], wq[:, kc, jj * 128:(jj + 1) * 128], sTf[:, kc, :],
                         start=(kc == 0), stop=(kc == 7))
                P.ts(modT[l][:, j, :], pm[:, 0:8], col(C_BMOD(l, j)), None, ALU.add)
        P.ts(modT[l][:, 8:16, :], modT[l][:, 8:16, :], 1.0, None, ALU.add)

    def gate_bcast(dst, l, r):
        for kc in range(8):
            rep = tmp()
            P.act(rep[:, 0:128], zeros[:], AF.Identity, bias=modT[l][:, 16 + kc, r:r + 1], scale=0.0)
            pb = ps[3 + kc // 4]
            P.mm(pb[:, (kc % 4) * 128:(kc % 4 + 1) * 128], rep[:, 0:128], ident[:])
        P.copy(dst[:, 0:512], ps[3][:], eng="act")
        P.copy(dst[:, 512:1024], ps[4][:], eng="act")

    def load_xtile(b, t, dst):
        if t < 2:
            P.dma(dst[:], ctx_d[b * CL + t * 128: b * CL + (t + 1) * 128, :])
        else:
            P.dma(dst[:], x_d[b * S + (t - 2) * 128: b * S + (t - 1) * 128, :])

    def mod_transpose(src, l, r, tok0):
        for kc in range(8):
            pb = ps[5 + kc // 4]
            P.tr(pb[:, (kc % 4) * 128:(kc % 4 + 1) * 128], src[:, kc * 128:(kc + 1) * 128], ident[:])
        for kc in range(8):
            pb = ps[5 + kc // 4]
            P.act(HT[:, kc, tok0:tok0 + 128], pb[:, (kc % 4) * 128:(kc % 4 + 1) * 128], AF.Identity,
                  bias=modT[l][:, kc, r:r + 1], scale=modT[l][:, 8 + kc, r:r + 1])

    def ln_stats(z, tag_i):
        P.bn_stats(stt_[:, 0, :], z[:, 0:512])
        P.bn_stats(stt_[:, 1, :], z[:, 512:1024])
        P.bn_aggr(mv[:, tag_i, :], stt_[:, 0:2, :].rearrange("p a b -> p (a b)"))
        rsqrt_cols(rs[:, tag_i:tag_i + 1], mv[:, tag_i, 1:2], EPS)

    def ln_apply(z, tag_i, gb, out_ap):
        P.ts(z[:], z[:], mv[:, tag_i, 0:1], rs[:, tag_i:tag_i + 1], ALU.subtract, ALU.mult)
        P.tt(z[:], z[:], gb[:, 0, :], ALU.mult)
        P.tt(out_ap, z[:], gb[:, 1, :], ALU.add)

    def ln_tile(z, tag_i, gb, out_ap):
        ln_stats(z, tag_i)
        ln_apply(z, tag_i, gb, out_ap)

    for b in range(NB):
        for grp in ([0, 1], [2, 3, 4, 5], [6, 7, 8, 9], [10, 11, 12, 13], [14, 15, 16, 17]):
            for j, t in enumerate(grp):
                xt = xin()
                load_xtile(b, t, xt)
                for kc in range(8):
                    P.tr(ps[kc][:, j * 128:(j + 1) * 128], xt[:, kc * 128:(kc + 1) * 128], ident[:])
            r_ = 4 if grp[0] < 2 else b
            n_ = len(grp) * 128
            for kc in range(8):
                P.act(HT[:, kc, grp[0] * 128:grp[0] * 128 + n_], ps[kc][:, 0:n_], AF.Identity,
                      bias=modT[0][:, kc, r_:r_ + 1], scale=modT[0][:, 8 + kc, r_:r_ + 1])
        if stage <= 1:
            return P.finalize()
        for cc in range(4):
            wx, wg = WS[(2 * cc) % 4], WS[(2 * cc + 1) % 4]
            P.dma(wx[:], wview(abin_d, cc * 128, 128), q="pool")
            P.dma(wg[:], wview(abin_d, W + cc * 128, 128), q="pool")
            for (n0, n) in NTILES:
                pb = ps[0 + (n0 // 512) % 2]
                for kc in range(8):
                    P.mm(pb[:, 0:n], wx[:, kc, :], HT[:, kc, n0:n0 + n], start=(kc == 0), stop=(kc == 7))
                P.copy(XR[:, n0:n0 + n], pb[:, 0:n], eng="act")
            for (a0, a1) in ((0, CL), (CL, T)):
                P.ts(XC[:, a0:a1], XR[:, a0:a1], col(C_CONVW(2, cc)), col(C_CONVB(cc)), ALU.mult, ALU.add)
                P.stt(XC[:, a0 + 2:a1], XR[:, a0:a1 - 2], col(C_CONVW(0, cc)), XC[:, a0 + 2:a1], ALU.mult, ALU.add)
                P.stt(XC[:, a0 + 1:a1], XR[:, a0:a1 - 1], col(C_CONVW(1, cc)), XC[:, a0 + 1:a1], ALU.mult, ALU.add)
                P.stt(XC[:, a0:a1 - 1], XR[:, a0 + 1:a1], col(C_CONVW(3, cc)), XC[:, a0:a1 - 1], ALU.mult, ALU.add)
            P.copy(XCB[:], XC[:], eng="act")
            for d in range(2):
                Hd = H0 if d == 0 else H1
                ia = 0 * 8 + d * 4 + cc
                ix = 1 * 8 + d * 4 + cc
                k8 = d * 4 + cc
                for (n0, n) in NTILES:
                    pr = ps[2]
                    pi = ps[3]
                    P.mm(pr[:, 0:n], wbd[:, ia, :], XCB[:, n0:n0 + n])
                    P.mm(pi[:, 0:n], wbd[:, ix, :], XCB[:, n0:n0 + n])
                    thr = tmp()
                    P.act(thr[:, 0:n], pr[:, 0:n], AF.Tanh, bias=sm[:, k8:k8 + 1], scale=0.5)
                    P.act(BA[:, n0:n0 + n], thr[:, 0:n], AF.Exp, bias=sm[:, 24 + k8:25 + k8], scale=sm[:, 24 + k8:25 + k8])
                    P.act(BS[:, n0:n0 + n], thr[:, 0:n], AF.Exp, bias=sm[:, 16 + k8:17 + k8], scale=sm[:, 16 + k8:17 + k8])
                    P.act(XR[:, n0:n0 + n], pi[:, 0:n], AF.Tanh, bias=sm[:, 8 + k8:9 + k8], scale=0.5)
                P.stt(XR[:], XR[:], 1.0, XC[:], ALU.add, ALU.mult)
                P.act(BS[:], BS[:], AF.Sqrt, bias=1.0, scale=-1.0)
                P.stt(BS[:], XR[:], 0.5, BS[:], ALU.mult, ALU.mult)
                if d == 0:
                    P.scan(Hd[:], BA[:], BS[:], 0.0)
                else:
                    P.scan(Hd[:, 0:CL][:, ::-1], BA[:, 0:CL][:, ::-1], BS[:, 0:CL][:, ::-1], 0.0)
                    P.scan(Hd[:, CL:T][:, ::-1], BA[:, CL:T][:, ::-1], BS[:, CL:T][:, ::-1], Hd[:, 0:1])
            P.tt(H0[:], H0[:], H1[:], ALU.add)
            for (n0, n) in NTILES:
                pb = ps[0 + (n0 // 512) % 2]
                for kc in range(8):
                    P.mm(pb[:, 0:n], wg[:, kc, :], HT[:, kc, n0:n0 + n], start=(kc == 0), stop=(kc == 7))
                thg = tmp()
                P.act(thg[:, 0:n], pb[:, 0:n], AF.Tanh, scale=0.5)
                P.stt(thg[:, 0:n], thg[:, 0:n], 1.0, pb[:, 0:n], ALU.add, ALU.mult)
                P.stt(YC[:, cc, n0:n0 + n], thg[:, 0:n], 0.5, H0[:, n0:n0 + n], ALU.mult, ALU.mult)
        if stage <= 2:
            return P.finalize()
        wU, wV, wG = WQ[0], WQ[1], WQ[2]
        P.dma(bc_sgu[:], bcsgu_d.ap())
        P.dma(wU[:], wview(abin_d, 2 * W, W), q="pool")
        P.dma(wV[:], wview(abin_d, 3 * W, W), q="pool")
        P.dma(wG[:], wview(abin_d, 4 * W, W), q="pool")
        for (n0, n) in NTILES:
            for g in range(4):
                pu, pg = ps[0 + g % 2], ps[2 + g % 2]
                for kc in range(8):
                    P.mm(pu[:, 0:n], wU[:, kc, g * 128:(g + 1) * 128], HT[:, kc, n0:n0 + n], start=(kc == 0), stop=(kc == 7))
                for kc in range(8):
                    P.mm(pg[:, 0:n], wG[:, kc, g * 128:(g + 1) * 128], HT[:, kc, n0:n0 + n], start=(kc == 0), stop=(kc == 7))
                gu = tmp()
                P.act(gu[:, 0:n], pu[:, 0:n], AF.Gelu_apprx_tanh)
                tg = tmp()
                P.act(tg[:, 0:n], pg[:, 0:n], AF.Tanh, scale=0.5)
                P.stt(tg[:, 0:n], tg[:, 0:n], 1.0, pg[:, 0:n], ALU.add, ALU.mult)
                P.stt(UGT[:, g, n0:n0 + n], tg[:, 0:n], 0.5, gu[:, 0:n], ALU.mult, ALU.mult)
        for t in range(NT):
            pv = ps[4 + t % 2]
            for kc in range(8):
                P.mm(pv[:], HT[:, kc, t * 128:(t + 1) * 128], wV[:, kc, :], start=(kc == 0), stop=(kc == 7))
            gv = tmp()
            P.act(gv[:], pv[:], AF.Gelu_apprx_tanh)
            P.bn_stats(stt_[:, 2 + t % 2, :], gv[:])
            P.bn_aggr(mv[:, t, :], stt_[:, 2 + t % 2, :])
            P.copy(GVB[:, t, :], gv[:], eng="act")
        rsqrt_cols(rs[:, 0:NT], mv[:, 0:NT, 1], EPS)
        for t in range(NT):
            vn = tmp()
            P.ts(vn[:], GVB[:, t, :], mv[:, t, 0:1], rs[:, t:t + 1], ALU.subtract, ALU.mult)
            P.tt(vn[:], vn[:], bc_sgu[:, 0, :], ALU.mult)
            P.tt(GVB[:, t, :], vn[:], bc_sgu[:, 1, :], ALU.add)
            pS = ps[6 + t % 2]
            for g in range(4):
                P.mm(pS[:, g * 128:(g + 1) * 128], GVB[:, t, g * 128:(g + 1) * 128], wsT[:, g, :], start=True, stop=False)
                P.mm(pS[:, g * 128:(g + 1) * 128], ones[0:1, :], sgub[0:1, g, :], start=False, stop=True)
            P.tt(YC[:, 4:8, t * 128:(t + 1) * 128], pS[:].rearrange("p (g n) -> p g n", g=4),
                 UGT[:, :, t * 128:(t + 1) * 128], ALU.mult)
        if stage <= 3:
            return P.finalize()
        for hf in range(2):
            P.dma(WO[:, :, hf * 512:(hf + 1) * 512], wview(about_d, hf * 512, 512), q="pool")
        P.dma(LNB[:], bcln_d[:, 0:2, :])
        gate_bcast(GBC, 0, b)
        gate_bcast(GBX, 0, 4)
        xts = {}

        def d_mm(t):
            py0, py1 = ps[0 + 2 * (t % 2)], ps[1 + 2 * (t % 2)]
            for hf, py in ((0, py0), (1, py1)):
                for kc in range(8):
                    P.mm(py[:], YC[:, kc, t * 128:(t + 1) * 128], WO[:, kc, hf * 512:(hf + 1) * 512], start=(kc == 0), stop=(kc == 7))

        def d_load(t):
            xt = xts[t] = xin()
            load_xtile(b, t, xt)

        def d_A(t):
            py0, py1 = ps[0 + 2 * (t % 2)], ps[1 + 2 * (t % 2)]
            xt = xts[t]
            gb = GBX if t < 2 else GBC
            gy = tmp()
            for hf, py in ((0, py0), (1, py1)):
                P.tt(gy[:], py[:], gb[:, hf * 512:(hf + 1) * 512], ALU.mult)
                P.stt(xt[:, hf * 512:(hf + 1) * 512], xt[:, hf * 512:(hf + 1) * 512], ALPHA, gy[:], ALU.mult, ALU.add)
            ln_stats(xt, 20 + t % 8)

        def d_B(t):
            xt = xts[t]
            ln_apply(xt, 20 + t % 8, LNB, xt[:])
            if t >= 2:
                r0 = b * S + (t - 2) * 128
                P.dma(x1s_d[r0:r0 + 128, :], xt[:])
                if debug and b == 0:
                    P.dma(dbg["x1"][(t - 2) * 128:(t - 1) * 128, :], xt[:], is_out=True)
            elif debug and b == 0:
                P.dma(dbg["c1"][t * 128:(t + 1) * 128, :], xt[:], is_out=True)
            mod_transpose(xt, 1, 4 if t < 2 else b, t * 128)

        d_load(0)
        d_load(1)
        d_load(2)
        d_mm(0)
        d_mm(1)
        d_A(0)
        for t in range(NT):
            if t + 2 < NT:
                d_mm(t + 2)
            if t + 1 < NT:
                d_A(t + 1)
            d_B(t)
            if t + 3 < NT:
                d_load(t + 3)
        if stage <= 4:
            return P.finalize()
        P.dma(LNB[:], bcln_d[:, 2:4, :])
        if stage <= 4.1:
            return P.finalize()
        gate_bcast(GBC, 1, b)
        if stage <= 4.2:
            return P.finalize()
        wK, wKs, wVv = WQ[0], WQ[1], WQ[2]
        P.dma(wK[:], wview(cdin_d, 5 * W, W), q="pool")
        P.dma(wKs[:], wview(qksw_d, W, W), q="pool")
        P.dma(wVv[:], wview(cdin_d, 6 * W, W), q="pool")
        if stage <= 4.25:
            return P.finalize()
        P.memset(VA[:].rearrange("p t h e -> p (t h) e")[:, :, 128:132], 0.0)
        P.memset(VA[:].rearrange("p t h e -> p (t h) e")[:, :, 128:129], 1.0)
        if stage <= 4.3:
            return P.finalize()
        for (n0, n) in NTILES:
            la, lb = max(n0, CL), n0 + n
            ln_ = lb - la
            c0 = la - n0
            P.dma(RPS[:, :, 0:ln_], rope_d[:, :, la - CL:lb - CL])
            for h in range(4):
                pk, pks = ps[0 + h % 2], ps[2 + h % 2]
                for kc in range(8):
                    P.mm(pk[:, 0:n], wK[:, kc, h * 128:(h + 1) * 128], HT[:, kc, n0:n0 + n], start=(kc == 0), stop=(kc == 7))
                for kc in range(8):
                    P.mm(pks[:, 0:n], wKs[:, kc, h * 128:(h + 1) * 128], HT[:, kc, n0:n0 + n], start=(kc == 0), stop=(kc == 7))
                if n0 == 0:
                    P.copy(KT[:, h, 0:CL], pk[:, 0:CL], eng="act")
                ta, tb = tmp(), tmp()
                P.tt(ta[:, 0:ln_], pk[:, c0:c0 + ln_], RPS[:, 0, 0:ln_], ALU.mult)
                P.tt(tb[:, 0:ln_], pks[:, c0:c0 + ln_], RPS[:, 1, 0:ln_], ALU.mult)
                P.tt(KT[:, h, la:lb], ta[:, 0:ln_], tb[:, 0:ln_], ALU.add)
        if stage <= 4.6:
            return P.finalize()
        for t in range(NT):
            pv = ps[4 + t % 2]
            for kc in range(8):
                P.mm(pv[:], HT[:, kc, t * 128:(t + 1) * 128], wVv[:, kc, :], start=(kc == 0), stop=(kc == 7))
            P.copy(VA[:, t, :, 0:128], pv[:].rearrange("p (h e) -> p h e", h=4), eng="act")
        if stage <= 5:
            return P.finalize()
        for cc in range(4):
            for i in range(4):
                P.dma(WS[i][:], wview(cdin_d, i * W + cc * 128, 128), q="pool")
            for nb in range(4):
                tk = CL + nb * 512
                ph, pc_ = ps[0], ps[1]
                for kc in range(8):
                    P.mm(ph[:], WS[0][:, kc, :], HT[:, kc, tk:tk + 512], start=(kc == 0), stop=(kc == 7))
                for kc in range(8):
                    P.mm(pc_[:], WS[2][:, kc, :], HT[:, kc, tk:tk + 512], start=(kc == 0), stop=(kc == 7))
                hs = tmp()
                P.copy(hs[:], ph[:], eng="act")
                P.tt(PRD[:, nb * 512:(nb + 1) * 512], pc_[:], hs[:], ALU.mult)
            P.ts(CV[:], PRD[:], col(C_SCW(1, cc)), None, ALU.mult)
            P.stt(CV[:, 1:S], PRD[:, 0:S - 1], col(C_SCW(0, cc)), CV[:, 1:S], ALU.mult, ALU.add)
            P.stt(CV[:, 0:S - 1], PRD[:, 1:S], col(C_SCW(2, cc)), CV[:, 0:S - 1], ALU.mult, ALU.add)
            for nb in range(4):
                tk = CL + nb * 512
                pB, pg = ps[2], ps[3]
                for kc in range(8):
                    P.mm(pB[:], WS[1][:, kc, :], HT[:, kc, tk:tk + 512], start=(kc == 0), stop=(kc == 7))
                for kc in range(8):
                    P.mm(pg[:], WS[3][:, kc, :], HT[:, kc, tk:tk + 512], start=(kc == 0), stop=(kc == 7))
                tg, t2 = tmp(), tmp()
                P.act(tg[:], pg[:], AF.Tanh, scale=0.5)
                P.stt(tg[:], tg[:], 1.0, pg[:], ALU.add, ALU.mult)
                P.tt(t2[:], pB[:], CV[:, nb * 512:(nb + 1) * 512], ALU.mult)
                P.stt(YC1[:, cc, nb * 512:(nb + 1) * 512], tg[:], 0.5, t2[:], ALU.mult, ALU.mult)
        if stage <= 6:
            return P.finalize()
        wQ, wQs, wGa = WQ[0], WQ[1], WQ[2]
        P.dma(wQ[:], wview(cdin_d, 4 * W, W), q="pool")
        P.dma(wQs[:], wview(qksw_d, 0, W), q="pool")
        OSC = 0.5 * (1.0 - LAM_INIT)

        def emit_out_tile(qb, qt, b=b):
            t = qb * 4 + qt
            py0, py1 = ps[2], ps[3]
            for hf, py in ((0, py0), (1, py1)):
                for kc in range(8):
                    lhs = YC1[:, kc, t * 128:(t + 1) * 128] if kc < 4 else YD1[:, kc - 4, qt * 128:(qt + 1) * 128]
                    P.mm(py[:], lhs, WO[:, kc, hf * 512:(hf + 1) * 512], start=(kc == 0), stop=(kc == 7))
            xt = xin()
            r0 = b * S + t * 128
            P.dma(xt[:], x1s_d[r0:r0 + 128, :])
            gy = tmp()
            for hf, py in ((0, py0), (1, py1)):
                P.tt(gy[:], py[:], GBC[:, hf * 512:(hf + 1) * 512], ALU.mult)
                P.stt(xt[:, hf * 512:(hf + 1) * 512], xt[:, hf * 512:(hf + 1) * 512], ALPHA, gy[:], ALU.mult, ALU.add)
            ln_tile(xt, 30 + t % 8, LNB, xt[:])
            P.dma(out_d[r0:r0 + 128, :], xt[:], is_out=True)

        pending = []
        def qphase(qb):
            SG = SGS[qb % 2]
            tk = CL + qb * 512
            if qb == 0:
                P.memset(QZ[:].rearrange("p m h n -> p (m h n)"), 0.0)
            P.dma(wGa[:], wview(cdin_d, 7 * W, W), q="pool")
            P.dma(RPS[:], rope_d[:, :, qb * 512:(qb + 1) * 512])
            for h in range(4):
                pq, pqs, pg = ps[(3 * h) % 4], ps[(3 * h + 1) % 4], ps[(3 * h + 2) % 4]
                for kc in range(8):
                    P.mm(pq[:], wQ[:, kc, h * 128:(h + 1) * 128], HT[:, kc, tk:tk + 512], start=(kc == 0), stop=(kc == 7))
                for kc in range(8):
                    P.mm(pqs[:], wQs[:, kc, h * 128:(h + 1) * 128], HT[:, kc, tk:tk + 512], start=(kc == 0), stop=(kc == 7))
                for kc in range(8):
                    P.mm(pg[:], wGa[:, kc, h * 128:(h + 1) * 128], HT[:, kc, tk:tk + 512], start=(kc == 0), stop=(kc == 7))
                ta, tb, tg = tmp(), tmp(), tmp()
                P.tt(ta[:], pq[:], RPS[:, 0, :], ALU.mult)
                P.tt(tb[:], pqs[:], RPS[:, 1, :], ALU.mult)
                P.tt(QZ[0:64, 0, h, :], ta[0:64, :], tb[0:64, :], ALU.add)
                P.tt(QZ[64:128, 1, h, :], ta[64:128, :], tb[64:128, :], ALU.add)
                P.act(tg[:], pg[:], AF.Tanh, scale=0.5)
                P.stt(SG[:, h, :], tg[:], 1.0, pg[:], ALU.add, ALU.mult)

        qphase(0)
        for qb in range(4):
            SG = SGS[qb % 2]
            for hf in range(2):
                P.dma(WO[:, :, hf * 512:(hf + 1) * 512], wview(cdout_d, hf * 512, 512), q="pool")
            seq = [(h, m, kt) for h in range(4) for m in range(2) for kt in range(NT)]

            def score(i):
                h, m, kt = seq[i]
                pS_ = ps[i % 2]
                P.mm(pS_[:], KT[:, h, kt * 128:(kt + 1) * 128], QZ[:, m, h, :])
                P.act(ET[i % 3][:], pS_[:], AF.Exp, scale=0.125)

            def pv(i):
                h, m, kt = seq[i]
                e_ = ET[i % 3]
                for qt in range(4):
                    P.mm(ps[4 + qt][:, 0:130], e_[:, qt * 128:(qt + 1) * 128], VA[:, kt, h, 0:130],
                         start=(kt == 0), stop=(kt == NT - 1))
                if kt == NT - 1:
                    for qt in range(4):
                        P.copy(OM[:, m, qt, 0:129], ps[4 + qt][:, 0:129], eng="dve")
                    if pending and (h * 2 + m) >= 1:
                        emit_out_tile(*pending.pop(0))
                    if m == 1:
                        for qt in range(4):
                            sl = qt * 4 + h
                            c_ = 56 + 2 * (sl % 8)
                            r0_, r1_ = sm[:, c_:c_ + 1], sm[:, c_ + 1:c_ + 2]
                            P.recip(r0_, OM[:, 0, qt, 128:129])
                            P.recip(r1_, OM[:, 1, qt, 128:129])
                            P.ts(r1_, r1_, lamcol, None, ALU.mult)
                            t_ = tmp()
                            P.ts(t_[:, 0:128], OM[:, 1, qt, 0:128], r1_, None, ALU.mult)
                            P.stt(OA[:, sl, :], OM[:, 0, qt, 0:128], r0_, t_[:, 0:128], ALU.mult, ALU.subtract)
                            P.bn_stats(stt_[:, 4 + sl % 8, :], OA[:, sl, :])
                            P.bn_aggr(mv[:, sl, :], stt_[:, 4 + sl % 8, :])
                            P.stt(rs[:, sl:sl + 1], mv[:, sl, 0:1], mv[:, sl, 0:1], mv[:, sl, 1:2], ALU.mult, ALU.add)

            score(0)
            score(1)
            for i in range(len(seq)):
                if i + 2 < len(seq):
                    score(i + 2)
                    if i + 3 == len(seq) and qb + 1 < 4:
                        qphase(qb + 1)
                pv(i)
            rsqrt_cols(rs[:, 0:16], rs[:, 0:16], EPS)
            for h in range(4):
                pt = ps[2 + h % 2]
                for qt in range(4):
                    sl = qt * 4 + h
                    on = tmp()
                    P.ts(on[:, 0:128], OA[:, sl, :], rs[:, sl:sl + 1], OSC, ALU.mult, ALU.mult)
                    P.tt(on[:, 0:128], on[:, 0:128], bc_misc[:, 256:384], ALU.mult)
                    P.tr(pt[:, qt * 128:(qt + 1) * 128], on[:, 0:128], ident[:])
                P.tt(YD1[:, h, :], pt[:], SG[:, h, :], ALU.mult)
            pending.extend((qb, qt) for qt in range(4))
        while pending:
            emit_out_tile(*pending.pop(0))
    return P.finalize()


def prep_shared(inp):
    f = np.float32
    sh = {}
    rows = np.zeros((128, 128), f)
    rows[0:16] = inp["lru_conv_w"][0].reshape(4, 4, 128).reshape(16, 128)
    rows[16:20] = inp["lru_conv_b"][0].reshape(4, 128)
    rows[20:28] = inp["lru_b_a"][0].reshape(8, 128)
    rows[28:36] = inp["lru_b_x"][0].reshape(8, 128)
    rows[36:44] = inp["lru_lambda"][0].reshape(8, 128)
    rows[44:56] = inp["sconv_w"][0].reshape(12, 128)
    rows[56:104] = inp["b_mod"].reshape(48, 128)
    sh["rows"] = rows
    sh["w_mod0"] = np.ascontiguousarray(inp["w_mod"][0])
    sh["w_mod1"] = np.ascontiguousarray(inp["w_mod"][1])
    sh["ab_w_in"] = np.ascontiguousarray(inp["ab_w_in"][0])
    sh["ab_w_out"] = np.ascontiguousarray(inp["ab_w_out"][0])
    cd = inp["cd_w_in"][0]
    sh["cd_w_in"] = np.ascontiguousarray(cd)
    sh["cd_w_out"] = np.ascontiguousarray(inp["cd_w_out"][0])
    perm = np.arange(512) ^ 16
    sh["w_qk_sw"] = np.ascontiguousarray(np.concatenate([cd[:, 2048:2560][:, perm], cd[:, 2560:3072][:, perm]], axis=1))
    wbd = np.zeros((128, 16, 128), f)
    for gi, key in enumerate(("lru_w_a", "lru_w_x")):
        w = inp[key][0]
        for d in range(2):
            for cc in range(4):
                idx = gi * 8 + d * 4 + cc
                wbd[0:64, idx, 0:64] = w[d, 2 * cc]
                wbd[64:128, idx, 64:128] = w[d, 2 * cc + 1]
    sh["lru_wbd"] = wbd
    sh["sgu_wT"] = np.ascontiguousarray(np.transpose(inp["sgu_w"][0], (2, 0, 1)))
    sh["sgu_b"] = np.ascontiguousarray(inp["sgu_b"][0][None])
    bc_ln = np.stack([inp["ln_g"][0], inp["ln_b"][0], inp["ln_g"][1], inp["ln_b"][1]], 0)
    sh["bc_ln"] = np.ascontiguousarray(np.broadcast_to(bc_ln[None], (128, 4, 1024)))
    bc_sgu = np.stack([inp["sgu_ln_g"][0], inp["sgu_ln_b"][0]], 0)
    sh["bc_sgu"] = np.ascontiguousarray(np.broadcast_to(bc_sgu[None], (128, 2, 512)))
    misc = np.concatenate([inp["diff_lambda"][0].reshape(256), inp["diff_subln_g"][0].reshape(128)])
    sh["bc_misc"] = np.ascontiguousarray(np.broadcast_to(misc[None], (128, 384)))
    sh["ident"] = np.eye(128, dtype=f)
    t = np.arange(2048)
    row = (t // 64).astype(f)
    colp = (t % 64).astype(f)
    inv = (np.float32(10000.0) ** (-np.arange(16, dtype=f) / np.float32(16))).astype(f)
    ang = np.concatenate([row[:, None] * inv, row[:, None] * inv, colp[:, None] * inv, colp[:, None] * inv], axis=-1).astype(f)
    cos = np.cos(ang).astype(f)
    sin = np.sin(ang).astype(f)
    sgn = np.where((np.arange(64) % 32) < 16, -1.0, 1.0).astype(f)
    sin_s = sin * sgn[None, :]
    rope = np.zeros((128, 2, 2048), f)
    rope[:, 0, :] = np.concatenate([cos.T, cos.T], 0)
    rope[:, 1, :] = np.concatenate([sin_s.T, sin_s.T], 0)
    sh["rope"] = rope
    return sh

def prep_core(inp, sh, b0, NB):
    f = np.float32
    m = dict(sh)
    m["x"] = np.ascontiguousarray(inp["x"][b0:b0 + NB].reshape(NB * 2048, 1024))
    m["ctx"] = np.ascontiguousarray(inp["ctx"][b0:b0 + NB].reshape(NB * 256, 1024))
    cond = np.zeros((128, 1024), f)
    cond[0:NB] = inp["c"][b0:b0 + NB]
    cond[4] = inp["c_ctx"]
    m["cond"] = cond
    return m


NB_PER_CORE = 4
N_CORES = 8


def kernel(**inputs):
    inp = {k: np.asarray(v) for k, v in inputs.items()}
    sh = prep_shared(inp)
    nc = build(NB_PER_CORE)
    in_maps = [prep_core(inp, sh, c * NB_PER_CORE, NB_PER_CORE) for c in range(N_CORES)]
    res = run_bass_kernel_spmd(nc, in_maps, core_ids=list(range(N_CORES)))
    outs = [np.asarray(r["out"]).reshape(NB_PER_CORE, S, D) for r in res.results]
    return np.concatenate(outs, axis=0).astype(np.float32)
```

```python
import math

import numpy as np
import concourse.bass as bass
import concourse.mybir as mybir
from concourse.bass_utils import run_bass_kernel_spmd

F32 = mybir.dt.float32
BF16 = mybir.dt.bfloat16
AF = mybir.ActivationFunctionType
ALU = mybir.AluOpType
AX = mybir.AxisListType

ENG_ATTR = {"pe": "tensor", "act": "scalar", "dve": "vector", "pool": "gpsimd", "sp": "sync"}
COMPUTE = ("pe", "act", "dve", "pool")
_ESZ = {F32: 4, BF16: 2, mybir.dt.int32: 4, mybir.dt.uint32: 4, mybir.dt.float16: 2}


class Prog:
    SEM_ROT = 30000

    def __init__(self):
        self.nc = bass.Bass("TRN2", target_bir_lowering=False)
        self.ops = []
        self.acc = {}
        self.base = {}
        self.sb_off = 16512
        self.sb_top = 229344
        self.n_psum = 0
        self.dma_rr = {"sp": 0, "pool": 0, "act": 0}
        self.out_dmas = []

    def sbuf(self, name, shape, dtype, at=None):
        esz = _ESZ[dtype]
        nbytes = int(np.prod(shape[1:])) * esz
        if at is None:
            at = (self.sb_off + 63) // 64 * 64
            self.sb_off = at + nbytes
            assert self.sb_off <= self.sb_top, (name, self.sb_off)
        t = self.nc.alloc_sbuf_tensor_at(name, list(shape), dtype, offset=at)
        self.base[t.name] = ("SB", at)
        return t

    def psum(self, name, shape, dtype=F32):
        t = self.nc.alloc_psum_tensor(name, list(shape), dtype)
        self.base[t.name] = ("PS_" + name, 0)
        return t

    def bitcast(self, t, dtype):
        v = t.bitcast(dtype)
        self.base[v.name] = self.base[t.name]
        return v

    def dram_in(self, name, shape, dtype=F32):
        return self.nc.dram_tensor(name, list(shape), dtype, kind="ExternalInput")

    def dram_out(self, name, shape, dtype=F32):
        return self.nc.dram_tensor(name, list(shape), dtype, kind="ExternalOutput")

    def dram_tmp(self, name, shape, dtype=F32):
        return self.nc.dram_tensor(name, list(shape), dtype)

    def region(self, ap):
        t = ap.tensor
        key = self.base.get(t.name)
        if key is None:
            return None
        space, b0 = key
        if space.startswith("PS_"):
            return (space, 0, 128, 0, 2048)
        esz = _ESZ[ap.dtype]
        dims = ap.ap
        pstep, pcnt = dims[0]
        off = ap.offset
        if pstep > 0:
            p0 = off // pstep
            f0 = off % pstep
        else:
            p0, f0 = 0, off
        lo = hi = f0
        for step, cnt in dims[1:]:
            if step >= 0:
                hi += step * (cnt - 1)
            else:
                lo += step * (cnt - 1)
        hi += 1
        return (space, p0, p0 + pcnt, b0 + lo * esz, b0 + hi * esz)

    def add(self, eng, emit, reads=(), writes=(), is_dma=False, is_out=False):
        idx = len(self.ops)
        tag = ("dma", idx) if is_dma else eng
        deps = set()
        rr = [self.region(a) for a in reads if a is not None and not isinstance(a, (int, float))]
        wr = [self.region(a) for a in writes]
        rr = [r for r in rr if r is not None]
        wr = [r for r in wr if r is not None]
        wr = wr + [r for r in rr if r[0].startswith("PS_") and r not in wr]
        for r in rr:
            for e in self.acc.get(r[0], ()):
                if e[6] and e[0] < r[2] and r[1] < e[1] and e[2] < r[4] and r[3] < e[3]:
                    deps.add((e[5], "raw"))
        for w in wr:
            for e in self.acc.get(w[0], ()):
                if e[0] < w[2] and w[1] < e[1] and e[2] < w[4] and w[3] < e[3]:
                    deps.add((e[5], "waw" if e[6] else "war"))
        fdeps = set()
        for d, kind in deps:
            po = self.ops[d]
            if po["is_dma"] or is_dma:
                fdeps.add(d)
            elif po["eng"] == eng:
                if eng != "pe":
                    fdeps.add(d)
            else:
                fdeps.add(d)
        op = dict(eng=eng, emit=emit, deps=fdeps, is_dma=is_dma, is_out=is_out, signal=False)
        self.ops.append(op)
        for d in fdeps:
            self.ops[d]["signal"] = True
        for w in wr:
            lst = self.acc.setdefault(w[0], [])
            lst[:] = [e for e in lst if not (w[1] <= e[0] and e[1] <= w[2] and w[3] <= e[2] and e[3] <= w[4])]
            lst.append([w[1], w[2], w[3], w[4], tag, idx, True])
        for r in rr:
            lst = self.acc.setdefault(r[0], [])
            found = False
            for e in lst:
                if (not e[6]) and e[4] == tag and e[0] == r[1] and e[1] == r[2] and e[2] == r[3] and e[3] == r[4]:
                    e[5] = idx
                    found = True
                    break
            if not found:
                lst.append([r[1], r[2], r[3], r[4], tag, idx, False])
        if is_dma:
            op["signal"] = True
        return idx

    def mm(self, out, lhsT, rhs, start=True, stop=True, **kw):
        return self.add("pe", lambda e: e.matmul(out, lhsT, rhs, start=start, stop=stop, **kw),
                        reads=[lhsT, rhs], writes=[out])

    def tr(self, out, in_, ident):
        return self.add("pe", lambda e: e.transpose(out, in_, ident), reads=[in_, ident], writes=[out])

    def act(self, out, in_, func, bias=0.0, scale=1.0, eng="act"):
        rd = [in_]
        if not isinstance(bias, (int, float)):
            rd.append(bias)
        if not isinstance(scale, (int, float)):
            rd.append(scale)
        return self.add(eng, lambda e: e.activation(out, in_, func, bias=bias, scale=scale),
                        reads=rd, writes=[out])

    def tt(self, out, a, b, op, eng="dve"):
        return self.add(eng, lambda e: e.tensor_tensor(out, a, b, op), reads=[a, b], writes=[out])

    def ts(self, out, a, s1, s2, op0, op1=None, eng="dve"):
        rd = [a] + [s for s in (s1, s2) if s is not None and not isinstance(s, (int, float))]
        if op1 is None:
            return self.add(eng, lambda e: e.tensor_scalar(out, a, s1, None, op0), reads=rd, writes=[out])
        return self.add(eng, lambda e: e.tensor_scalar(out, a, s1, s2, op0, op1), reads=rd, writes=[out])

    def stt(self, out, in0, scalar, in1, op0, op1, eng="dve"):
        rd = [in0, in1] + ([] if isinstance(scalar, (int, float)) else [scalar])
        return self.add(eng, lambda e: e.scalar_tensor_tensor(out, in0, scalar, in1, op0, op1),
                        reads=rd, writes=[out])

    def scan(self, out, d0, d1, initial, op0=ALU.mult, op1=ALU.add):
        rd = [d0, d1] + ([] if isinstance(initial, (int, float)) else [initial])
        return self.add("dve", lambda e: e.tensor_tensor_scan(out, d0, d1, initial, op0, op1),
                        reads=rd, writes=[out])

    def copy(self, out, in_, eng="dve"):
        if eng == "act":
            return self.add("act", lambda e: e.copy(out, in_), reads=[in_], writes=[out])
        return self.add(eng, lambda e: e.tensor_copy(out, in_), reads=[in_], writes=[out])

    def memset(self, ap, val, eng="dve"):
        return self.add(eng, lambda e: e.memset(ap, val), writes=[ap])

    def recip(self, out, in_):
        return self.add("dve", lambda e: e.reciprocal(out, in_), reads=[in_], writes=[out])

    def bn_stats(self, out, in_):
        return self.add("dve", lambda e: e.bn_stats(out, in_), reads=[in_], writes=[out])

    def bn_aggr(self, out, in_):
        return self.add("dve", lambda e: e.bn_aggr(out, in_), reads=[in_], writes=[out])

    def dma(self, out, in_, q="sp", is_out=False, **kw):
        if q == "pool":
            kw.setdefault("max_dma_last_dim", 2048)
        return self.add(q, lambda e: e.dma_start(out, in_, **kw), reads=[in_], writes=[out],
                        is_dma=True, is_out=is_out)

    def finalize(self):
        nc = self.nc
        ops = self.ops
        NDMA = {"sp": 10, "pool": 6, "act": 4}
        cnt = {e: 0 for e in COMPUTE}
        sems = {e: [nc.alloc_semaphore(name=f"s_{e}_0")] for e in COMPUTE}
        dma_sems = {q: [nc.alloc_semaphore(name=f"d_{q}_{i}") for i in range(n)] for q, n in NDMA.items()}
        dma_cnt = {q: [0] * n for q, n in NDMA.items()}
        dma_i = {q: 0 for q in NDMA}
        last_of = {}
        for i, op in enumerate(ops):
            last_of[op["eng"]] = i
        for e, i in last_of.items():
            ops[i]["signal"] = True
        for op in ops:
            if op["is_dma"]:
                q = op["eng"]
                k = dma_i[q] % NDMA[q]
                dma_i[q] += 1
                dma_cnt[q][k] += 1
                op["sem"] = dma_sems[q][k]
                op["semkey"] = ("dma", q, k)
                op["val"] = 16 * dma_cnt[q][k]
            elif op["signal"]:
                e = op["eng"]
                if cnt[e] >= self.SEM_ROT:
                    sems[e].append(nc.alloc_semaphore(name=f"s_{e}_{len(sems[e])}"))
                    cnt[e] = 0
                cnt[e] += 1
                op["sem"] = sems[e][-1]
                op["semkey"] = (e, len(sems[e]) - 1)
                op["val"] = cnt[e]
        by_eng = {e: [] for e in ENG_ATTR}
        for i, op in enumerate(ops):
            by_eng[op["eng"]].append(i)
        fw_ = {}
        for op in ops:
            if op["is_dma"] or op["signal"]:
                fw_[op["semkey"]] = (op["sem"], op["val"], op["semkey"])
        final_waits = list(fw_.values())

        def emit_engine(ename, eobj):
            waited = {}

            def wait(sem, val, key):
                if waited.get(key, 0) >= val:
                    return
                waited[key] = val
                eobj.wait_ge(sem, val)

            for i in by_eng[ename]:
                op = ops[i]
                for d in sorted(op["deps"]):
                    po = ops[d]
                    wait(po["sem"], po["val"], po["semkey"])
                if op["is_dma"] and op["val"] > 16:
                    wait(op["sem"], op["val"] - 16, op["semkey"])
                ins = op["emit"](eobj)
                if op["signal"]:
                    ins.then_inc(op["sem"], 16 if op["is_dma"] else 1)
            if ename == "sp":
                for sem, val, key in final_waits:
                    wait(sem, val, key)

        with nc.Block() as block:
            for ename, attr in ENG_ATTR.items():
                if not by_eng[ename] and ename != "sp":
                    continue

                def mk(ename):
                    def f(eobj):
                        emit_engine(ename, eobj)
                    return f
                getattr(block, attr)(mk(ename))
        return nc


D = 1024
S = 2048
CL = 256
T = S + CL
NT = T // 128
W = 512
ALPHA = 4.0 ** 0.25
EPS = 1e-5
LAM_INIT = 0.8 - 0.6 * math.exp(-0.3)
NTILES = [(0, 512), (512, 512), (1024, 512), (1536, 512), (2048, 256)]


def build(NB, debug=False, stage=99):
    P = Prog()
    x_d = P.dram_in("x", [NB * S, D])
    ctx_d = P.dram_in("ctx", [NB * CL, D])
    cond_d = P.dram_in("cond", [128, D])
    rows_d = P.dram_in("rows", [128, 128])
    wmod_d = [P.dram_in(f"w_mod{l}", [D, 3 * D]) for l in range(2)]
    abin_d = P.dram_in("ab_w_in", [D, 5 * W])
    about_d = P.dram_in("ab_w_out", [D, D])
    cdin_d = P.dram_in("cd_w_in", [D, 8 * W])
    cdout_d = P.dram_in("cd_w_out", [D, D])
    qksw_d = P.dram_in("w_qk_sw", [D, 2 * W])
    wbd_d = P.dram_in("lru_wbd", [128, 16, 128])
    wsT_d = P.dram_in("sgu_wT", [128, 4, 128])
    sgub_d = P.dram_in("sgu_b", [1, 4, 128])
    bcln_d = P.dram_in("bc_ln", [128, 4, D])
    bcsgu_d = P.dram_in("bc_sgu", [128, 2, W])
    bcmisc_d = P.dram_in("bc_misc", [128, 384])
    ident_d = P.dram_in("ident", [128, 128])
    rope_d = P.dram_in("rope", [128, 2, S])
    out_d = P.dram_out("out", [NB * S, D])
    x1s_d = P.dram_tmp("x1s", [NB * S, D])
    P.base[x1s_d.name] = ("DR_x1s", 0)
    dbg = {}
    if debug:
        dbg["x1"] = P.dram_out("dbg_x1", [S, D])
        dbg["c1"] = P.dram_out("dbg_c1", [CL, D])
        dbg["mod"] = P.dram_out("dbg_mod", [128, 2, 192])

    ident = P.sbuf("ident", [128, 128], F32)
    colsT = P.sbuf("colsT", [128, 128], F32)
    zeros = P.sbuf("zeros", [128, 128], F32)
    ones = P.sbuf("ones", [128, 128], F32)
    modT = [P.sbuf(f"modT{l}", [128, 24, 8], F32) for l in range(2)]
    sgub = P.sbuf("sgub", [128, 4, 128], F32)
    bc_sgu = P.sbuf("bc_sgu", [128, 2, W], F32)
    bc_misc = P.sbuf("bc_misc", [128, 384], F32)
    sm = P.sbuf("sm", [128, 96], F32)

    stt_ = P.sbuf("stt_", [128, 16, 6], F32)
    mv = P.sbuf("mv", [128, 40, 2], F32)
    rs = P.sbuf("rs", [128, 40], F32)
    wsT = P.sbuf("wsT", [128, 4, 128], BF16)
    wbd = P.sbuf("wbd", [128, 16, 128], BF16)
    HT = P.sbuf("HT", [128, 8, T], BF16)
    yk0 = P.sb_off = (P.sb_off + 63) // 64 * 64
    YC = P.sbuf("YC", [128, 8, T], BF16, at=yk0)
    KT = P.sbuf("KT", [128, 4, T], BF16, at=yk0)
    VA = P.sbuf("VA", [128, NT, 4, 132], BF16, at=yk0 + 4 * T * 2 + 64)
    P.sb_off = yk0 + 4 * T * 2 + 64 + NT * 4 * 132 * 2
    w0 = P.sb_off = (P.sb_off + 63) // 64 * 64
    WQ = [P.sbuf(f"WQ{i}", [128, 8, 512], BF16, at=w0 + i * 8192) for i in range(4)]
    WS = [P.sbuf(f"WS{i}", [128, 8, 128], BF16, at=w0 + 3 * 8192 + i * 2048) for i in range(4)]
    WO = P.sbuf("WO", [128, 8, 1024], BF16, at=w0 + 16384)
    P.sb_off = w0 + 32768
    s0 = P.sb_off = (P.sb_off + 63) // 64 * 64
    TB = 4 * T
    XR = P.sbuf("XR", [128, T], F32, at=s0)
    XC = P.sbuf("XC", [128, T], F32, at=s0 + TB)
    BA = P.sbuf("BA", [128, T], F32, at=s0 + 2 * TB)
    BS = P.sbuf("BS", [128, T], F32, at=s0 + 3 * TB)
    H0 = P.sbuf("H0", [128, T], F32, at=s0 + 4 * TB)
    H1 = P.sbuf("H1", [128, T], F32, at=s0 + 5 * TB)
    XCB = P.sbuf("XCB", [128, T], BF16, at=s0 + 6 * TB)
    SEND = s0 + 6 * TB + 2 * T
    GVB = P.sbuf("GVB", [128, NT, W], BF16, at=s0)
    UGT = P.sbuf("UGT", [128, 4, T], BF16, at=s0 + NT * W * 2)
    GBC = P.sbuf("GBC", [128, D], F32, at=s0)
    LNB = P.sbuf("LNB", [128, 2, D], F32, at=s0 + 4096)
    GBX = P.sbuf("GBX", [128, D], F32, at=s0 + 12288)
    l1 = s0 + 12288
    YC1 = P.sbuf("YC1", [128, 4, S], BF16, at=l1)
    YD1 = P.sbuf("YD1", [128, 4, 512], BF16, at=l1 + 16384)
    QZ = P.sbuf("QZ", [128, 2, 4, 512], BF16, at=l1 + 20480)
    SG0 = P.sbuf("SG0", [128, 4, 512], BF16, at=l1 + 28672)
    OM = P.sbuf("OM", [128, 2, 4, 132], F32, at=l1 + 32768)
    OA = P.sbuf("OA", [128, 16, 128], F32, at=l1 + 32768 + 4224)
    ET = [P.sbuf(f"ET{i}", [128, 512], BF16, at=l1 + 32768 + 4224 + 8192 + i * 1024) for i in range(3)]
    PRD = P.sbuf("PRD", [128, S], F32, at=l1 + 16384)
    CV = P.sbuf("CV", [128, S], F32, at=l1 + 16384 + 8192)
    l1end = l1 + 32768 + 4224 + 8192 + 3072
    assert l1 + 16384 + 16384 <= l1end + 4096
    P.sb_off = max(SEND, l1end, l1 + 32768)
    SG1 = P.sbuf("SG1", [128, 4, 512], BF16)
    SGS = [SG0, SG1]
    NTMP = 5
    TMP = [P.sbuf(f"TMP{i}", [128, 512], F32) for i in range(NTMP)]
    XIN = [P.sbuf(f"XIN{i}", [128, D], F32) for i in range(3)]
    RPS = P.sbuf("RPS", [128, 2, 512], F32, at=P.base[bc_sgu.name][1])
    print("SBUF end", P.sb_off, "top", P.sb_top)
    assert P.sb_off <= P.sb_top
    ps = [P.psum(f"ps{i}", [128, 512]) for i in range(8)]
    tctr = [0]

    def tmp():
        tctr[0] += 1
        return TMP[tctr[0] % NTMP]

    xctr = [0]

    def xin():
        xctr[0] += 1
        return XIN[xctr[0] % 3]

    def wview(dh, c0, n):
        return dh.ap().rearrange("(kc p) n -> p kc n", p=128)[:, :, c0:c0 + n]

    P.dma(ident[:], ident_d.ap())
    P.dma(colsT[:], rows_d.ap())
    P.dma(sgub[0:1], sgub_d.ap())
    P.dma(bc_misc[:], bcmisc_d.ap())
    P.dma(wsT[:], wsT_d.ap(), q="pool")
    P.dma(wbd[:], wbd_d.ap(), q="pool")
    P.memset(zeros[:], 0.0)
    P.memset(ones[:], 1.0)
    P.tr(ps[0][:, 0:128], colsT[:], ident[:])
    P.copy(colsT[:], ps[0][:, 0:128])
    C_CONVW = lambda k, cc: k * 4 + cc
    C_CONVB = lambda cc: 16 + cc
    C_BA = lambda d, cc: 20 + d * 4 + cc
    C_BX = lambda d, cc: 28 + d * 4 + cc
    C_LAM = lambda d, cc: 36 + d * 4 + cc
    C_SCW = lambda k, cc: 44 + k * 4 + cc
    C_BMOD = lambda l, j: 56 + l * 24 + j
    col = lambda i: colsT[:, i:i + 1]
    for i in range(8):
        P.ts(sm[:, i:i + 1], col(20 + i), 0.5, None, ALU.mult)
        P.ts(sm[:, 8 + i:9 + i], col(28 + i), 0.5, None, ALU.mult)
    zc = sm[:, 32:40]
    P.act(zc, colsT[:, 36:44], AF.Exp, scale=-1.0)
    t1 = sm[:, 40:48]
    P.ts(t1, zc, 1.0 / 3.0, -0.5, ALU.mult, ALU.add)
    P.tt(t1, t1, zc, ALU.mult)
    P.ts(t1, t1, 1.0, None, ALU.add)
    P.tt(t1, t1, zc, ALU.mult)
    P.ts(sm[:, 16:24], t1, -8.0, None, ALU.mult)
    P.ts(sm[:, 24:32], t1, -4.0, None, ALU.mult)
    lt = TMP[0]
    P.tt(lt[:, 0:64], bc_misc[:, 0:64], bc_misc[:, 64:128], ALU.mult)
    P.tt(lt[:, 64:128], bc_misc[:, 128:192], bc_misc[:, 192:256], ALU.mult)
    P.add("dve", lambda e: e.reduce_sum(sm[:, 48:49], lt[:, 0:64], AX.X), reads=[lt[:, 0:64]], writes=[sm[:, 48:49]])
    P.add("dve", lambda e: e.reduce_sum(sm[:, 49:50], lt[:, 64:128], AX.X), reads=[lt[:, 64:128]], writes=[sm[:, 49:50]])
    P.act(sm[:, 50:52], sm[:, 48:50], AF.Exp)
    P.tt(sm[:, 52:53], sm[:, 50:51], sm[:, 51:52], ALU.subtract)
    P.ts(sm[:, 52:53], sm[:, 52:53], LAM_INIT, None, ALU.add)
    lamcol = sm[:, 52:53]
    mh = P.sbuf("mh", [128, 40], F32)
    P.memset(mh[:], -0.5)

    def rsqrt_cols(out, in_, eps):
        n = in_.shape[-1]
        P.ts(out, in_, eps, None, ALU.add)
        P.add("pool", lambda e: e.tensor_tensor(out, out, mh[:, 0:n], ALU.pow),
              reads=[out, mh[:, 0:n]], writes=[out])

    P.dma(XIN[0][:], cond_d.ap())
    sTf = P.sbuf("sTf", [128, 8, 128], BF16, at=P.base[HT.name][1])
    for kc in range(8):
        pb = ps[1 + kc // 4]
        P.tr(pb[:, (kc % 4) * 128:(kc % 4 + 1) * 128], XIN[0][:, kc * 128:(kc + 1) * 128], ident[:])
    for hh in range(2):
        th = TMP[2 + hh]
        P.act(th[:], ps[1 + hh][:], AF.Tanh, scale=0.5)
        P.stt(th[:], th[:], 1.0, ps[1 + hh][:], ALU.add, ALU.mult)
        P.ts(sTf[:, hh * 4:(hh + 1) * 4, :].rearrange("p a b -> p (a b)"), th[:], 0.5, None, ALU.mult)
    for l in range(2):
        for blk in range(6):
            wq = WQ[blk % 2]
            P.dma(wq[:], wview(wmod_d[l], blk * 512, 512), q="pool")
            for jj in range(4):
                j = blk * 4 + jj
                pm = ps[3 + j % 2]
                for kc in range(8):
                    P.mm(pm[:, 0:128], wq[:, kc, jj * 128:(jj + 1) * 128], sTf[:, kc, :],
                         start=(kc == 0), stop=(kc == 7))
                P.ts(modT[l][:, j, :], pm[:, 0:8], col(C_BMOD(l, j)), None, ALU.add)
        P.ts(modT[l][:, 8:16, :], modT[l][:, 8:16, :], 1.0, None, ALU.add)

    def gate_bcast(dst, l, r):
        for kc in range(8):
            rep = tmp()
            P.act(rep[:, 0:128], zeros[:], AF.Identity, bias=modT[l][:, 16 + kc, r:r + 1], scale=0.0)
            pb = ps[3 + kc // 4]
            P.mm(pb[:, (kc % 4) * 128:(kc % 4 + 1) * 128], rep[:, 0:128], ident[:])
        P.copy(dst[:, 0:512], ps[3][:], eng="act")
        P.copy(dst[:, 512:1024], ps[4][:], eng="act")

    def load_xtile(b, t, dst):
        if t < 2:
            P.dma(dst[:], ctx_d[b * CL + t * 128: b * CL + (t + 1) * 128, :])
        else:
            P.dma(dst[:], x_d[b * S + (t - 2) * 128: b * S + (t - 1) * 128, :])

    def mod_transpose(src, l, r, tok0):
        for kc in range(8):
            pb = ps[5 + kc // 4]
            P.tr(pb[:, (kc % 4) * 128:(kc % 4 + 1) * 128], src[:, kc * 128:(kc + 1) * 128], ident[:])
        for kc in range(8):
            pb = ps[5 + kc // 4]
            P.act(HT[:, kc, tok0:tok0 + 128], pb[:, (kc % 4) * 128:(kc % 4 + 1) * 128], AF.Identity,
                  bias=modT[l][:, kc, r:r + 1], scale=modT[l][:, 8 + kc, r:r + 1])

    def ln_stats(z, tag_i):
        P.bn_stats(stt_[:, 0, :], z[:, 0:512])
        P.bn_stats(stt_[:, 1, :], z[:, 512:1024])
        P.bn_aggr(mv[:, tag_i, :], stt_[:, 0:2, :].rearrange("p a b -> p (a b)"))
        rsqrt_cols(rs[:, tag_i:tag_i + 1], mv[:, tag_i, 1:2], EPS)

    def ln_apply(z, tag_i, gb, out_ap):
        P.ts(z[:], z[:], mv[:, tag_i, 0:1], rs[:, tag_i:tag_i + 1], ALU.subtract, ALU.mult)
        P.tt(z[:], z[:], gb[:, 0, :], ALU.mult)
        P.tt(out_ap, z[:], gb[:, 1, :], ALU.add)

    def ln_tile(z, tag_i, gb, out_ap):
        ln_stats(z, tag_i)
        ln_apply(z, tag_i, gb, out_ap)

    for b in range(NB):
        for grp in ([0, 1], [2, 3, 4, 5], [6, 7, 8, 9], [10, 11, 12, 13], [14, 15, 16, 17]):
            for j, t in enumerate(grp):
                xt = xin()
                load_xtile(b, t, xt)
                for kc in range(8):
                    P.tr(ps[kc][:, j * 128:(j + 1) * 128], xt[:, kc * 128:(kc + 1) * 128], ident[:])
            r_ = 4 if grp[0] < 2 else b
            n_ = len(grp) * 128
            for kc in range(8):
                P.act(HT[:, kc, grp[0] * 128:grp[0] * 128 + n_], ps[kc][:, 0:n_], AF.Identity,
                      bias=modT[0][:, kc, r_:r_ + 1], scale=modT[0][:, 8 + kc, r_:r_ + 1])
        if stage <= 1:
            return P.finalize()
        for cc in range(4):
            wx, wg = WS[(2 * cc) % 4], WS[(2 * cc + 1) % 4]
            P.dma(wx[:], wview(abin_d, cc * 128, 128), q="pool")
            P.dma(wg[:], wview(abin_d, W + cc * 128, 128), q="pool")
            for (n0, n) in NTILES:
                pb = ps[0 + (n0 // 512) % 2]
                for kc in range(8):
                    P.mm(pb[:, 0:n], wx[:, kc, :], HT[:, kc, n0:n0 + n], start=(kc == 0), stop=(kc == 7))
                P.copy(XR[:, n0:n0 + n], pb[:, 0:n], eng="act")
            for (a0, a1) in ((0, CL), (CL, T)):
                P.ts(XC[:, a0:a1], XR[:, a0:a1], col(C_CONVW(2, cc)), col(C_CONVB(cc)), ALU.mult, ALU.add)
                P.stt(XC[:, a0 + 2:a1], XR[:, a0:a1 - 2], col(C_CONVW(0, cc)), XC[:, a0 + 2:a1], ALU.mult, ALU.add)
                P.stt(XC[:, a0 + 1:a1], XR[:, a0:a1 - 1], col(C_CONVW(1, cc)), XC[:, a0 + 1:a1], ALU.mult, ALU.add)
                P.stt(XC[:, a0:a1 - 1], XR[:, a0 + 1:a1], col(C_CONVW(3, cc)), XC[:, a0:a1 - 1], ALU.mult, ALU.add)
            P.copy(XCB[:], XC[:], eng="act")
            for d in range(2):
                Hd = H0 if d == 0 else H1
                ia = 0 * 8 + d * 4 + cc
                ix = 1 * 8 + d * 4 + cc
                k8 = d * 4 + cc
                for (n0, n) in NTILES:
                    pr = ps[2]
                    pi = ps[3]
                    P.mm(pr[:, 0:n], wbd[:, ia, :], XCB[:, n0:n0 + n])
                    P.mm(pi[:, 0:n], wbd[:, ix, :], XCB[:, n0:n0 + n])
                    thr = tmp()
                    P.act(thr[:, 0:n], pr[:, 0:n], AF.Tanh, bias=sm[:, k8:k8 + 1], scale=0.5)
                    P.act(BA[:, n0:n0 + n], thr[:, 0:n], AF.Exp, bias=sm[:, 24 + k8:25 + k8], scale=sm[:, 24 + k8:25 + k8])
                    P.act(BS[:, n0:n0 + n], thr[:, 0:n], AF.Exp, bias=sm[:, 16 + k8:17 + k8], scale=sm[:, 16 + k8:17 + k8])
                    P.act(XR[:, n0:n0 + n], pi[:, 0:n], AF.Tanh, bias=sm[:, 8 + k8:9 + k8], scale=0.5)
                P.stt(XR[:], XR[:], 1.0, XC[:], ALU.add, ALU.mult)
                P.act(BS[:], BS[:], AF.Sqrt, bias=1.0, scale=-1.0)
                P.stt(BS[:], XR[:], 0.5, BS[:], ALU.mult, ALU.mult)
                if d == 0:
                    P.scan(Hd[:], BA[:], BS[:], 0.0)
                else:
                    P.scan(Hd[:, 0:CL][:, ::-1], BA[:, 0:CL][:, ::-1], BS[:, 0:CL][:, ::-1], 0.0)
                    P.scan(Hd[:, CL:T][:, ::-1], BA[:, CL:T][:, ::-1], BS[:, CL:T][:, ::-1], Hd[:, 0:1])
            P.tt(H0[:], H0[:], H1[:], ALU.add)
            for (n0, n) in NTILES:
                pb = ps[0 + (n0 // 512) % 2]
                for kc in range(8):
                    P.mm(pb[:, 0:n], wg[:, kc, :], HT[:, kc, n0:n0 + n], start=(kc == 0), stop=(kc == 7))
                thg = tmp()
                P.act(thg[:, 0:n], pb[:, 0:n], AF.Tanh, scale=0.5)
                P.stt(thg[:, 0:n], thg[:, 0:n], 1.0, pb[:, 0:n], ALU.add, ALU.mult)
                P.stt(YC[:, cc, n0:n0 + n], thg[:, 0:n], 0.5, H0[:, n0:n0 + n], ALU.mult, ALU.mult)
        if stage <= 2:
            return P.finalize()
        wU, wV, wG = WQ[0], WQ[1], WQ[2]
        P.dma(bc_sgu[:], bcsgu_d.ap())
        P.dma(wU[:], wview(abin_d, 2 * W, W), q="pool")
        P.dma(wV[:], wview(abin_d, 3 * W, W), q="pool")
        P.dma(wG[:], wview(abin_d, 4 * W, W), q="pool")
        for (n0, n) in NTILES:
            for g in range(4):
                pu, pg = ps[0 + g % 2], ps[2 + g % 2]
                for kc in range(8):
                    P.mm(pu[:, 0:n], wU[:, kc, g * 128:(g + 1) * 128], HT[:, kc, n0:n0 + n], start=(kc == 0), stop=(kc == 7))
                for kc in range(8):
                    P.mm(pg[:, 0:n], wG[:, kc, g * 128:(g + 1) * 128], HT[:, kc, n0:n0 + n], start=(kc == 0), stop=(kc == 7))
                gu = tmp()
                P.act(gu[:, 0:n], pu[:, 0:n], AF.Gelu_apprx_tanh)
                tg = tmp()
                P.act(tg[:, 0:n], pg[:, 0:n], AF.Tanh, scale=0.5)
                P.stt(tg[:, 0:n], tg[:, 0:n], 1.0, pg[:, 0:n], ALU.add, ALU.mult)
                P.stt(UGT[:, g, n0:n0 + n], tg[:, 0:n], 0.5, gu[:, 0:n], ALU.mult, ALU.mult)
        for t in range(NT):
            pv = ps[4 + t % 2]
            for kc in range(8):
                P.mm(pv[:], HT[:, kc, t * 128:(t + 1) * 128], wV[:, kc, :], start=(kc == 0), stop=(kc == 7))
            gv = tmp()
            P.act(gv[:], pv[:], AF.Gelu_apprx_tanh)
            P.bn_stats(stt_[:, 2 + t % 2, :], gv[:])
            P.bn_aggr(mv[:, t, :], stt_[:, 2 + t % 2, :])
            P.copy(GVB[:, t, :], gv[:], eng="act")
        rsqrt_cols(rs[:, 0:NT], mv[:, 0:NT, 1], EPS)
        for t in range(NT):
            vn = tmp()
            P.ts(vn[:], GVB[:, t, :], mv[:, t, 0:1], rs[:, t:t + 1], ALU.subtract, ALU.mult)
            P.tt(vn[:], vn[:], bc_sgu[:, 0, :], ALU.mult)
            P.tt(GVB[:, t, :], vn[:], bc_sgu[:, 1, :], ALU.add)
            pS = ps[6 + t % 2]
            for g in range(4):
                P.mm(pS[:, g * 128:(g + 1) * 128], GVB[:, t, g * 128:(g + 1) * 128], wsT[:, g, :], start=True, stop=False)
                P.mm(pS[:, g * 128:(g + 1) * 128], ones[0:1, :], sgub[0:1, g, :], start=False, stop=True)
            P.tt(YC[:, 4:8, t * 128:(t + 1) * 128], pS[:].rearrange("p (g n) -> p g n", g=4),
                 UGT[:, :, t * 128:(t + 1) * 128], ALU.mult)
        if stage <= 3:
            return P.finalize()
        for hf in range(2):
            P.dma(WO[:, :, hf * 512:(hf + 1) * 512], wview(about_d, hf * 512, 512), q="pool")
        P.dma(LNB[:], bcln_d[:, 0:2, :])
        gate_bcast(GBC, 0, b)
        gate_bcast(GBX, 0, 4)
        xts = {}

        def d_mm(t):
            py0, py1 = ps[0 + 2 * (t % 2)], ps[1 + 2 * (t % 2)]
            for hf, py in ((0, py0), (1, py1)):
                for kc in range(8):
                    P.mm(py[:], YC[:, kc, t * 128:(t + 1) * 128], WO[:, kc, hf * 512:(hf + 1) * 512], start=(kc == 0), stop=(kc == 7))

        def d_load(t):
            xt = xts[t] = xin()
            load_xtile(b, t, xt)

        def d_A(t):
            py0, py1 = ps[0 + 2 * (t % 2)], ps[1 + 2 * (t % 2)]
            xt = xts[t]
            gb = GBX if t < 2 else GBC
            gy = tmp()
            for hf, py in ((0, py0), (1, py1)):
                P.tt(gy[:], py[:], gb[:, hf * 512:(hf + 1) * 512], ALU.mult)
                P.stt(xt[:, hf * 512:(hf + 1) * 512], xt[:, hf * 512:(hf + 1) * 512], ALPHA, gy[:], ALU.mult, ALU.add)
            ln_stats(xt, 20 + t % 8)

        def d_B(t):
            xt = xts[t]
            ln_apply(xt, 20 + t % 8, LNB, xt[:])
            if t >= 2:
                r0 = b * S + (t - 2) * 128
                P.dma(x1s_d[r0:r0 + 128, :], xt[:])
                if debug and b == 0:
                    P.dma(dbg["x1"][(t - 2) * 128:(t - 1) * 128, :], xt[:], is_out=True)
            elif debug and b == 0:
                P.dma(dbg["c1"][t * 128:(t + 1) * 128, :], xt[:], is_out=True)
            mod_transpose(xt, 1, 4 if t < 2 else b, t * 128)

        d_load(0)
        d_load(1)
        d_load(2)
        d_mm(0)
        d_mm(1)
        d_A(0)
        for t in range(NT):
            if t + 2 < NT:
                d_mm(t + 2)
            if t + 1 < NT:
                d_A(t + 1)
            d_B(t)
            if t + 3 < NT:
                d_load(t + 3)
        if stage <= 4:
            return P.finalize()
        P.dma(LNB[:], bcln_d[:, 2:4, :])
        if stage <= 4.1:
            return P.finalize()
        gate_bcast(GBC, 1, b)
        if stage <= 4.2:
            return P.finalize()
        wK, wKs, wVv = WQ[0], WQ[1], WQ[2]
        P.dma(wK[:], wview(cdin_d, 5 * W, W), q="pool")
        P.dma(wKs[:], wview(qksw_d, W, W), q="pool")
        P.dma(wVv[:], wview(cdin_d, 6 * W, W), q="pool")
        if stage <= 4.25:
            return P.finalize()
        P.memset(VA[:].rearrange("p t h e -> p (t h) e")[:, :, 128:132], 0.0)
        P.memset(VA[:].rearrange("p t h e -> p (t h) e")[:, :, 128:129], 1.0)
        if stage <= 4.3:
            return P.finalize()
        for (n0, n) in NTILES:
            la, lb = max(n0, CL), n0 + n
            ln_ = lb - la
            c0 = la - n0
            P.dma(RPS[:, :, 0:ln_], rope_d[:, :, la - CL:lb - CL])
            for h in range(4):
                pk, pks = ps[0 + h % 2], ps[2 + h % 2]
                for kc in range(8):
                    P.mm(pk[:, 0:n], wK[:, kc, h * 128:(h + 1) * 128], HT[:, kc, n0:n0 + n], start=(kc == 0), stop=(kc == 7))
                for kc in range(8):
                    P.mm(pks[:, 0:n], wKs[:, kc, h * 128:(h + 1) * 128], HT[:, kc, n0:n0 + n], start=(kc == 0), stop=(kc == 7))
                if n0 == 0:
                    P.copy(KT[:, h, 0:CL], pk[:, 0:CL], eng="act")
                ta, tb = tmp(), tmp()
                P.tt(ta[:, 0:ln_], pk[:, c0:c0 + ln_], RPS[:, 0, 0:ln_], ALU.mult)
                P.tt(tb[:, 0:ln_], pks[:, c0:c0 + ln_], RPS[:, 1, 0:ln_], ALU.mult)
                P.tt(KT[:, h, la:lb], ta[:, 0:ln_], tb[:, 0:ln_], ALU.add)
        if stage <= 4.6:
            return P.finalize()
        for t in range(NT):
            pv = ps[4 + t % 2]
            for kc in range(8):
                P.mm(pv[:], HT[:, kc, t * 128:(t + 1) * 128], wVv[:, kc, :], start=(kc == 0), stop=(kc == 7))
            P.copy(VA[:, t, :, 0:128], pv[:].rearrange("p (h e) -> p h e", h=4), eng="act")
        if stage <= 5:
            return P.finalize()
        for cc in range(4):
            for i in range(4):
                P.dma(WS[i][:], wview(cdin_d, i * W + cc * 128, 128), q="pool")
            for nb in range(4):
                tk = CL + nb * 512
                ph, pc_ = ps[0], ps[1]
                for kc in range(8):
                    P.mm(ph[:], WS[0][:, kc, :], HT[:, kc, tk:tk + 512], start=(kc == 0), stop=(kc == 7))
                for kc in range(8):
                    P.mm(pc_[:], WS[2][:, kc, :], HT[:, kc, tk:tk + 512], start=(kc == 0), stop=(kc == 7))
                hs = tmp()
                P.copy(hs[:], ph[:], eng="act")
                P.tt(PRD[:, nb * 512:(nb + 1) * 512], pc_[:], hs[:], ALU.mult)
            P.ts(CV[:], PRD[:], col(C_SCW(1, cc)), None, ALU.mult)
            P.stt(CV[:, 1:S], PRD[:, 0:S - 1], col(C_SCW(0, cc)), CV[:, 1:S], ALU.mult, ALU.add)
            P.stt(CV[:, 0:S - 1], PRD[:, 1:S], col(C_SCW(2, cc)), CV[:, 0:S - 1], ALU.mult, ALU.add)
            for nb in range(4):
                tk = CL + nb * 512
                pB, pg = ps[2], ps[3]
                for kc in range(8):
                    P.mm(pB[:], WS[1][:, kc, :], HT[:, kc, tk:tk + 512], start=(kc == 0), stop=(kc == 7))
                for kc in range(8):
                    P.mm(pg[:], WS[3][:, kc, :], HT[:, kc, tk:tk + 512], start=(kc == 0), stop=(kc == 7))
                tg, t2 = tmp(), tmp()
                P.act(tg[:], pg[:], AF.Tanh, scale=0.5)
                P.stt(tg[:], tg[:], 1.0, pg[:], ALU.add, ALU.mult)
                P.tt(t2[:], pB[:], CV[:, nb * 512:(nb + 1) * 512], ALU.mult)
                P.stt(YC1[:, cc, nb * 512:(nb + 1) * 512], tg[:], 0.5, t2[:], ALU.mult, ALU.mult)
        if stage <= 6:
            return P.finalize()
        wQ, wQs, wGa = WQ[0], WQ[1], WQ[2]
        P.dma(wQ[:], wview(cdin_d, 4 * W, W), q="pool")
        P.dma(wQs[:], wview(qksw_d, 0, W), q="pool")
        OSC = 0.5 * (1.0 - LAM_INIT)

        oxt = {}

        def o_mm(qb, qt, banks):
            t = qb * 4 + qt
            for hf, py in ((0, banks[0]), (1, banks[1])):
                for kc in range(8):
                    lhs = YC1[:, kc, t * 128:(t + 1) * 128] if kc < 4 else YD1[:, kc - 4, qt * 128:(qt + 1) * 128]
                    P.mm(py[:], lhs, WO[:, kc, hf * 512:(hf + 1) * 512], start=(kc == 0), stop=(kc == 7))

        def o_load(qb, qt, b=b):
            t = qb * 4 + qt
            xt = oxt[t] = xin()
            r0 = b * S + t * 128
            P.dma(xt[:], x1s_d[r0:r0 + 128, :])

        def o_epi(qb, qt, banks, b=b):
            t = qb * 4 + qt
            xt = oxt[t]
            r0 = b * S + t * 128
            gy = tmp()
            for hf, py in ((0, banks[0]), (1, banks[1])):
                P.tt(gy[:], py[:], GBC[:, hf * 512:(hf + 1) * 512], ALU.mult)
                P.stt(xt[:, hf * 512:(hf + 1) * 512], xt[:, hf * 512:(hf + 1) * 512], ALPHA, gy[:], ALU.mult, ALU.add)
            ln_tile(xt, 30 + t % 8, LNB, xt[:])
            P.dma(out_d[r0:r0 + 128, :], xt[:], is_out=True)

        def emit_out_tile(qb, qt):
            o_mm(qb, qt, (ps[2], ps[3]))
            o_load(qb, qt)
            o_epi(qb, qt, (ps[2], ps[3]))

        pending = []
        def qphase(qb):
            SG = SGS[qb % 2]
            tk = CL + qb * 512
            if qb == 0:
                P.memset(QZ[:].rearrange("p m h n -> p (m h n)"), 0.0)
            P.dma(wGa[:], wview(cdin_d, 7 * W, W), q="pool")
            P.dma(RPS[:], rope_d[:, :, qb * 512:(qb + 1) * 512])
            for h in range(4):
                pq, pqs, pg = ps[(3 * h) % 4], ps[(3 * h + 1) % 4], ps[(3 * h + 2) % 4]
                for kc in range(8):
                    P.mm(pq[:], wQ[:, kc, h * 128:(h + 1) * 128], HT[:, kc, tk:tk + 512], start=(kc == 0), stop=(kc == 7))
                for kc in range(8):
                    P.mm(pqs[:], wQs[:, kc, h * 128:(h + 1) * 128], HT[:, kc, tk:tk + 512], start=(kc == 0), stop=(kc == 7))
                for kc in range(8):
                    P.mm(pg[:], wGa[:, kc, h * 128:(h + 1) * 128], HT[:, kc, tk:tk + 512], start=(kc == 0), stop=(kc == 7))
                ta, tb, tg = tmp(), tmp(), tmp()
                P.tt(ta[:], pq[:], RPS[:, 0, :], ALU.mult)
                P.tt(tb[:], pqs[:], RPS[:, 1, :], ALU.mult)
                P.tt(QZ[0:64, 0, h, :], ta[0:64, :], tb[0:64, :], ALU.add)
                P.tt(QZ[64:128, 1, h, :], ta[64:128, :], tb[64:128, :], ALU.add)
                P.act(tg[:], pg[:], AF.Tanh, scale=0.5)
                P.stt(SG[:, h, :], tg[:], 1.0, pg[:], ALU.add, ALU.mult)

        qphase(0)
        for qb in range(4):
            SG = SGS[qb % 2]
            for hf in range(2):
                P.dma(WO[:, :, hf * 512:(hf + 1) * 512], wview(cdout_d, hf * 512, 512), q="pool")
            seq = [(h, m, kt) for h in range(4) for m in range(2) for kt in range(NT)]

            def score(i):
                h, m, kt = seq[i]
                pS_ = ps[i % 2]
                P.mm(pS_[:], KT[:, h, kt * 128:(kt + 1) * 128], QZ[:, m, h, :])
                P.act(ET[i % 3][:], pS_[:], AF.Exp, scale=0.125)

            def pv(i):
                h, m, kt = seq[i]
                e_ = ET[i % 3]
                for qt in range(4):
                    P.mm(ps[4 + qt][:, 0:130], e_[:, qt * 128:(qt + 1) * 128], VA[:, kt, h, 0:130],
                         start=(kt == 0), stop=(kt == NT - 1))
                if kt == NT - 1:
                    for qt in range(4):
                        P.copy(OM[:, m, qt, 0:129], ps[4 + qt][:, 0:129], eng="dve")
                    if pending and (h * 2 + m) >= 1:
                        emit_out_tile(*pending.pop(0))
                    if m == 1:
                        for qt in range(4):
                            sl = qt * 4 + h
                            c_ = 56 + 2 * (sl % 8)
                            r0_, r1_ = sm[:, c_:c_ + 1], sm[:, c_ + 1:c_ + 2]
                            P.recip(r0_, OM[:, 0, qt, 128:129])
                            P.recip(r1_, OM[:, 1, qt, 128:129])
                            P.ts(r1_, r1_, lamcol, None, ALU.mult)
                            t_ = tmp()
                            P.ts(t_[:, 0:128], OM[:, 1, qt, 0:128], r1_, None, ALU.mult)
                            P.stt(OA[:, sl, :], OM[:, 0, qt, 0:128], r0_, t_[:, 0:128], ALU.mult, ALU.subtract)
                            P.bn_stats(stt_[:, 4 + sl % 8, :], OA[:, sl, :])
                            P.bn_aggr(mv[:, sl, :], stt_[:, 4 + sl % 8, :])
                            P.stt(rs[:, sl:sl + 1], mv[:, sl, 0:1], mv[:, sl, 0:1], mv[:, sl, 1:2], ALU.mult, ALU.add)

            score(0)
            score(1)
            for i in range(len(seq)):
                if i + 2 < len(seq):
                    score(i + 2)
                    if i + 3 == len(seq) and qb + 1 < 4:
                        qphase(qb + 1)
                pv(i)
            rsqrt_cols(rs[:, 0:16], rs[:, 0:16], EPS)
            for h in range(4):
                pt = ps[2 + h % 2]
                for qt in range(4):
                    sl = qt * 4 + h
                    on = tmp()
                    P.ts(on[:, 0:128], OA[:, sl, :], rs[:, sl:sl + 1], OSC, ALU.mult, ALU.mult)
                    P.tt(on[:, 0:128], on[:, 0:128], bc_misc[:, 256:384], ALU.mult)
                    P.tr(pt[:, qt * 128:(qt + 1) * 128], on[:, 0:128], ident[:])
                P.tt(YD1[:, h, :], pt[:], SG[:, h, :], ALU.mult)
            pending.extend((qb, qt) for qt in range(4))
        fl = list(pending)
        del pending[:]
        for i, (qb_, qt_) in enumerate(fl):
            o_mm(qb_, qt_, (ps[2 * i], ps[2 * i + 1]))
        for (qb_, qt_) in fl[:3]:
            o_load(qb_, qt_)
        for i, (qb_, qt_) in enumerate(fl):
            o_epi(qb_, qt_, (ps[2 * i], ps[2 * i + 1]))
            if i + 3 < len(fl):
                o_load(*fl[i + 3])
    return P.finalize()


def prep_shared(inp):
    f = np.float32
    sh = {}
    rows = np.zeros((128, 128), f)
    rows[0:16] = inp["lru_conv_w"][0].reshape(4, 4, 128).reshape(16, 128)
    rows[16:20] = inp["lru_conv_b"][0].reshape(4, 128)
    rows[20:28] = inp["lru_b_a"][0].reshape(8, 128)
    rows[28:36] = inp["lru_b_x"][0].reshape(8, 128)
    rows[36:44] = inp["lru_lambda"][0].reshape(8, 128)
    rows[44:56] = inp["sconv_w"][0].reshape(12, 128)
    rows[56:104] = inp["b_mod"].reshape(48, 128)
    sh["rows"] = rows
    sh["w_mod0"] = np.ascontiguousarray(inp["w_mod"][0])
    sh["w_mod1"] = np.ascontiguousarray(inp["w_mod"][1])
    sh["ab_w_in"] = np.ascontiguousarray(inp["ab_w_in"][0])
    sh["ab_w_out"] = np.ascontiguousarray(inp["ab_w_out"][0])
    cd = inp["cd_w_in"][0]
    sh["cd_w_in"] = np.ascontiguousarray(cd)
    sh["cd_w_out"] = np.ascontiguousarray(inp["cd_w_out"][0])
    perm = np.arange(512) ^ 16
    sh["w_qk_sw"] = np.ascontiguousarray(np.concatenate([cd[:, 2048:2560][:, perm], cd[:, 2560:3072][:, perm]], axis=1))
    wbd = np.zeros((128, 16, 128), f)
    for gi, key in enumerate(("lru_w_a", "lru_w_x")):
        w = inp[key][0]
        for d in range(2):
            for cc in range(4):
                idx = gi * 8 + d * 4 + cc
                wbd[0:64, idx, 0:64] = w[d, 2 * cc]
                wbd[64:128, idx, 64:128] = w[d, 2 * cc + 1]
    sh["lru_wbd"] = wbd
    sh["sgu_wT"] = np.ascontiguousarray(np.transpose(inp["sgu_w"][0], (2, 0, 1)))
    sh["sgu_b"] = np.ascontiguousarray(inp["sgu_b"][0][None])
    bc_ln = np.stack([inp["ln_g"][0], inp["ln_b"][0], inp["ln_g"][1], inp["ln_b"][1]], 0)
    sh["bc_ln"] = np.ascontiguousarray(np.broadcast_to(bc_ln[None], (128, 4, 1024)))
    bc_sgu = np.stack([inp["sgu_ln_g"][0], inp["sgu_ln_b"][0]], 0)
    sh["bc_sgu"] = np.ascontiguousarray(np.broadcast_to(bc_sgu[None], (128, 2, 512)))
    misc = np.concatenate([inp["diff_lambda"][0].reshape(256), inp["diff_subln_g"][0].reshape(128)])
    sh["bc_misc"] = np.ascontiguousarray(np.broadcast_to(misc[None], (128, 384)))
    sh["ident"] = np.eye(128, dtype=f)
    t = np.arange(2048)
    row = (t // 64).astype(f)
    colp = (t % 64).astype(f)
    inv = (np.float32(10000.0) ** (-np.arange(16, dtype=f) / np.float32(16))).astype(f)
    ang = np.concatenate([row[:, None] * inv, row[:, None] * inv, colp[:, None] * inv, colp[:, None] * inv], axis=-1).astype(f)
    cos = np.cos(ang).astype(f)
    sin = np.sin(ang).astype(f)
    sgn = np.where((np.arange(64) % 32) < 16, -1.0, 1.0).astype(f)
    sin_s = sin * sgn[None, :]
    rope = np.zeros((128, 2, 2048), f)
    rope[:, 0, :] = np.concatenate([cos.T, cos.T], 0)
    rope[:, 1, :] = np.concatenate([sin_s.T, sin_s.T], 0)
    sh["rope"] = rope
    return sh

def prep_core(inp, sh, b0, NB):
    f = np.float32
    m = dict(sh)
    m["x"] = np.ascontiguousarray(inp["x"][b0:b0 + NB].reshape(NB * 2048, 1024))
    m["ctx"] = np.ascontiguousarray(inp["ctx"][b0:b0 + NB].reshape(NB * 256, 1024))
    cond = np.zeros((128, 1024), f)
    cond[0:NB] = inp["c"][b0:b0 + NB]
    cond[4] = inp["c_ctx"]
    m["cond"] = cond
    return m


NB_PER_CORE = 4
N_CORES = 8


def kernel(**inputs):
    inp = {k: np.asarray(v) for k, v in inputs.items()}
    sh = prep_shared(inp)
    nc = build(NB_PER_CORE)
    in_maps = [prep_core(inp, sh, c * NB_PER_CORE, NB_PER_CORE) for c in range(N_CORES)]
    res = run_bass_kernel_spmd(nc, in_maps, core_ids=list(range(N_CORES)))
    outs = [np.asarray(r["out"]).reshape(NB_PER_CORE, S, D) for r in res.results]
    return np.concatenate(outs, axis=0).astype(np.float32)
```
